# Optimizing a Trainium2 kernel written in Bass

```python
import jax, jax.numpy as jnp
from jax import lax
import numpy as np

D_MODEL = 1024
BATCH = 8
SEQ = 4096
DEPTH = 2
DEC_BATCH = 32
DEC_SEQ = 16
PAST_LEN = 2048

CHUNK = 64
A_HEADS = 8
A_HEAD_DIM = 64
A_WIDTH = A_HEADS * A_HEAD_DIM
A_BAND_CHUNKS = 8
A_WINDOW = A_BAND_CHUNKS * CHUNK
A_BAND = A_WINDOW + CHUNK
MAX_REL = 256
B_HEADS = 4
B_HEAD_DIM = 64
B_WIDTH = B_HEADS * B_HEAD_DIM
CONV_W = 4
C_WIDTH = D_MODEL - A_WIDTH - B_WIDTH
C_GROUP = 16
C_GROUPS = C_WIDTH // C_GROUP
C_STATE = 64
D_FF = 4 * D_MODEL
IN_SIZES = (A_WIDTH, A_WIDTH, A_WIDTH, 3 * B_WIDTH, B_HEADS, B_HEADS, B_WIDTH, C_WIDTH)
IN_WIDTH = 3 * A_WIDTH + 4 * B_WIDTH + 2 * B_HEADS + C_WIDTH
DN_ALPHA = (2.0 * DEPTH) ** 0.25
DN_BETA = (8.0 * DEPTH) ** -0.25
LN_EPS = 1e-5
RMS_EPS = 1e-6
NEG_INF = -1e30

kernel_name = 'hymba_streaming_encoder_step'


def split_points(sizes):
    pts, acc = [], 0
    for s in sizes[:-1]:
        acc += s
        pts.append(acc)
    return pts


def layer_norm(x, g, b):
    xf = x.astype(jnp.float32)
    mu = jnp.mean(xf, -1, keepdims=True)
    xc = xf - mu
    var = jnp.mean(xc * xc, -1, keepdims=True)
    y = xc * lax.rsqrt(var + LN_EPS) * g.astype(jnp.float32) + b.astype(jnp.float32)
    return y.astype(x.dtype)


def l2_normalize(x):
    return x * lax.rsqrt(jnp.sum(x * x, -1, keepdims=True) + RMS_EPS)


def rel_bias(table, rel):
    return jnp.take(table, jnp.clip(rel, -MAX_REL, MAX_REL) + MAX_REL, axis=1).astype(jnp.float32)


def band_attention_prompt(q, k, v, table):
    bsz, t_len, n_h, d_h = q.shape
    n_chunks = t_len // CHUNK
    pad = ((0, 0), (A_WINDOW, 0), (0, 0), (0, 0))
    kp, vp = jnp.pad(k, pad), jnp.pad(v, pad)
    rel = A_WINDOW + jnp.arange(CHUNK)[:, None] - jnp.arange(A_BAND)[None, :]
    bias = rel_bias(table, rel)
    scale = d_h ** -0.5

    def one_chunk(c):
        start = c * CHUNK
        qc = lax.dynamic_slice_in_dim(q, start, CHUNK, axis=1)
        kc = lax.dynamic_slice_in_dim(kp, start, A_BAND, axis=1)
        vc = lax.dynamic_slice_in_dim(vp, start, A_BAND, axis=1)
        s = jnp.einsum('bqhd,bkhd->bhqk', qc, kc).astype(jnp.float32) * scale + bias
        valid = start - A_WINDOW + jnp.arange(A_BAND) >= 0
        s = jnp.where(valid, s, NEG_INF)
        p = jax.nn.softmax(s, axis=-1).astype(v.dtype)
        return jnp.einsum('bhqk,bkhd->bqhd', p, vc)

    out = lax.map(one_chunk, jnp.arange(n_chunks))
    return jnp.moveaxis(out, 0, 1).reshape(bsz, t_len, n_h * d_h)


def band_attention_sample(q, k, v, k_cache, v_cache, table):
    bsz, s_len, n_h, d_h = q.shape
    n_cache = k_cache.shape[1]
    kk = jnp.concatenate([k_cache.astype(k.dtype), k], axis=1)
    vv = jnp.concatenate([v_cache.astype(v.dtype), v], axis=1)
    rel = n_cache + jnp.arange(s_len)[:, None] - jnp.arange(n_cache + s_len)[None, :]
    s = jnp.einsum('bqhd,bkhd->bhqk', q, kk).astype(jnp.float32) * d_h ** -0.5 + rel_bias(table, rel)
    p = jax.nn.softmax(s, axis=-1).astype(v.dtype)
    return jnp.einsum('bhqk,bkhd->bqhd', p, vv).reshape(bsz, s_len, n_h * d_h)


def gated_delta_rule(q, k, v, g, beta, s0):
    bsz, t_len, n_h, d_k = q.shape
    d_v = v.shape[-1]
    pad = (-t_len) % CHUNK
    n_chunks = (t_len + pad) // CHUNK

    def to_chunks(t):
        t = jnp.pad(t.astype(jnp.float32), ((0, 0), (0, pad)) + ((0, 0),) * (t.ndim - 2))
        t = t.reshape((bsz, n_chunks, CHUNK) + t.shape[2:])
        return jnp.moveaxis(t, 3, 1)

    q, k, v, g, beta = to_chunks(q), to_chunks(k), to_chunks(v), to_chunks(g), to_chunks(beta)
    q = q * d_k ** -0.5
    gc = jnp.cumsum(g, axis=-1)
    tri = jnp.tril(jnp.ones((CHUNK, CHUNK), dtype=bool))
    strict = jnp.tril(jnp.ones((CHUNK, CHUNK), dtype=bool), -1)
    diff = gc[..., :, None] - gc[..., None, :]
    decay = jnp.where(tri, jnp.exp(jnp.where(tri, diff, 0.0)), 0.0)
    k_beta = k * beta[..., None]
    a_low = jnp.where(strict, jnp.einsum('bhnid,bhnjd->bhnij', k_beta, k) * decay, 0.0)
    rhs = jnp.concatenate([v * beta[..., None], k_beta * jnp.exp(gc)[..., None]], axis=-1)
    sol = lax.linalg.triangular_solve(a_low + jnp.eye(CHUNK, dtype=jnp.float32), rhs,
                                      left_side=True, lower=True, unit_diagonal=True)
    u, w = sol[..., :d_v], sol[..., d_v:]
    qk = jnp.einsum('bhnid,bhnjd->bhnij', q, k) * decay
    q_dec = q * jnp.exp(gc)[..., None]
    k_dec = k * jnp.exp(gc[..., -1:] - gc)[..., None]
    g_tot = jnp.exp(gc[..., -1])

    def step(state, xs):
        q_c, k_c, u_c, w_c, qk_c, gt_c = xs
        v_new = u_c - jnp.einsum('bhcd,bhde->bhce', w_c, state)
        o_c = jnp.einsum('bhcd,bhde->bhce', q_c, state) + jnp.einsum('bhij,bhje->bhie', qk_c, v_new)
        state = state * gt_c[..., None, None] + jnp.einsum('bhcd,bhce->bhde', k_c, v_new)
        return state, o_c

    xs = tuple(jnp.moveaxis(t, 2, 0) for t in (q_dec, k_dec, u, w, qk, g_tot))
    s_final, o = lax.scan(step, s0.astype(jnp.float32), xs)
    o = jnp.moveaxis(jnp.moveaxis(o, 0, 2), 1, 3)
    return o.reshape(bsz, t_len + pad, n_h, d_v)[:, :t_len], s_final


def complex_linear_combine(e1, e2):
    a1r, a1i, b1r, b1i = e1
    a2r, a2i, b2r, b2i = e2
    return (a2r * a1r - a2i * a1i, a2r * a1i + a2i * a1r,
            a2r * b1r - a2i * b1i + b2r, a2r * b1i + a2i * b1r + b2i)


def s5_ssm(u, h0_re, h0_im, a_re, a_im, log_dt, b_re, b_im, c_re, c_im, d_skip):
    f32 = jnp.float32
    a_re, a_im, b_re, b_im = a_re.astype(f32), a_im.astype(f32), b_re.astype(f32), b_im.astype(f32)
    c_re, c_im, d_skip = c_re.astype(f32), c_im.astype(f32), d_skip.astype(f32)
    dt = jnp.exp(log_dt.astype(f32))[:, None]
    mag = jnp.exp(dt * a_re)
    lam_re, lam_im = mag * jnp.cos(dt * a_im), mag * jnp.sin(dt * a_im)
    den = a_re * a_re + a_im * a_im
    coef_re = ((lam_re - 1.0) * a_re + lam_im * a_im) / den
    coef_im = (lam_im * a_re - (lam_re - 1.0) * a_im) / den
    bb_re = coef_re[..., None] * b_re - coef_im[..., None] * b_im
    bb_im = coef_re[..., None] * b_im + coef_im[..., None] * b_re
    x_re = jnp.einsum('btgh,gph->btgp', u, bb_re)
    x_im = jnp.einsum('btgh,gph->btgp', u, bb_im)
    x_re = x_re.at[:, 0].add(lam_re * h0_re - lam_im * h0_im)
    x_im = x_im.at[:, 0].add(lam_re * h0_im + lam_im * h0_re)
    lr = jnp.broadcast_to(lam_re, x_re.shape)
    li = jnp.broadcast_to(lam_im, x_im.shape)
    _, _, h_re, h_im = lax.associative_scan(complex_linear_combine, (lr, li, x_re, x_im), axis=1)
    y = (jnp.einsum('btgp,ghp->btgh', h_re, c_re) - jnp.einsum('btgp,ghp->btgh', h_im, c_im)
         + d_skip * u)
    return y, h_re[:, -1], h_im[:, -1]


def hybrid_mixer(x, lp, kv_cache, conv_buf, s0, h0_re, h0_im):
    bsz, t_len, _ = x.shape
    f32 = jnp.float32
    proj = jnp.einsum('btd,de->bte', x, lp['w_in'])
    qa, ka, va, qkv_b, beta_in, a_in, gate_b, u_c = jnp.split(proj, split_points(IN_SIZES), axis=-1)

    qa = qa.reshape(bsz, t_len, A_HEADS, A_HEAD_DIM)
    ka = ka.reshape(bsz, t_len, A_HEADS, A_HEAD_DIM)
    va = va.reshape(bsz, t_len, A_HEADS, A_HEAD_DIM)
    if kv_cache is None:
        out_a = band_attention_prompt(qa, ka, va, lp['a_rel_bias'])
        new_k, new_v = ka[:, -A_WINDOW:], va[:, -A_WINDOW:]
    else:
        out_a = band_attention_sample(qa, ka, va, kv_cache[0], kv_cache[1], lp['a_rel_bias'])
        new_k, new_v = ka, va

    buf = jnp.concatenate([conv_buf.astype(qkv_b.dtype), qkv_b], axis=1)
    new_conv = buf[:, -(CONV_W - 1):]
    conv = lax.conv_general_dilated(buf, lp['b_conv_w'][:, None, :].astype(buf.dtype),
                                    window_strides=(1,), padding='VALID',
                                    dimension_numbers=('NWC', 'WIO', 'NWC'),
                                    feature_group_count=3 * B_WIDTH)
    conv = jax.nn.silu(conv.astype(f32) + lp['b_conv_b'].astype(f32))
    qb, kb, vb = jnp.split(conv, 3, axis=-1)
    qb = l2_normalize(qb.reshape(bsz, t_len, B_HEADS, B_HEAD_DIM))
    kb = l2_normalize(kb.reshape(bsz, t_len, B_HEADS, B_HEAD_DIM))
    vb = vb.reshape(bsz, t_len, B_HEADS, B_HEAD_DIM)
    beta = jax.nn.sigmoid(beta_in.astype(f32))
    g = -jnp.exp(lp['b_a_log'].astype(f32)) * jax.nn.softplus(a_in.astype(f32) + lp['b_dt_bias'].astype(f32))
    ob, s_new = gated_delta_rule(qb, kb, vb, g, beta, s0)
    gate = jax.nn.silu(gate_b.astype(f32).reshape(bsz, t_len, B_HEADS, B_HEAD_DIM))
    ob = ob * lax.rsqrt(jnp.mean(ob * ob, -1, keepdims=True) + RMS_EPS) * lp['b_norm_w'].astype(f32) * gate
    out_b = ob.reshape(bsz, t_len, B_WIDTH).astype(x.dtype)

    u = u_c.astype(f32).reshape(bsz, t_len, C_GROUPS, C_GROUP)
    y_c, h_re, h_im = s5_ssm(u, h0_re.astype(f32), h0_im.astype(f32), lp['c_a_re'], lp['c_a_im'],
                             lp['c_log_dt'], lp['c_b_re'], lp['c_b_im'], lp['c_c_re'], lp['c_c_im'], lp['c_d'])
    z = jax.nn.gelu(y_c.reshape(bsz, t_len, C_WIDTH))
    out_c = (z * jax.nn.sigmoid(z @ lp['c_glu_w'].astype(f32) + lp['c_glu_b'].astype(f32))).astype(x.dtype)

    mix = jnp.concatenate([out_a.astype(x.dtype), out_b, out_c], axis=-1) @ lp['w_out']
    return mix, new_k, new_v, new_conv, s_new, h_re, h_im


def trunk_layer(x, lp, kv_cache, conv_buf, s0, h0_re, h0_im):
    mix, new_k, new_v, new_conv, s_new, h_re, h_im = hybrid_mixer(x, lp, kv_cache, conv_buf, s0, h0_re, h0_im)
    x = layer_norm(DN_ALPHA * x + mix.astype(x.dtype), lp['ln1_g'], lp['ln1_b'])
    hid = jnp.square(jax.nn.relu(x @ lp['w_up'] + lp['b_up']))
    x = layer_norm(DN_ALPHA * x + (hid @ lp['w_down']).astype(x.dtype), lp['ln2_g'], lp['ln2_b'])
    return x, new_k, new_v, new_conv, s_new, h_re, h_im


def setup_inputs(seed: int = 0) -> dict:
    key = jax.random.key(seed)
    ks = jax.random.split(key, 40)
    f32 = jnp.float32

    def nrm(i, shape, scale):
        return scale * jax.random.normal(ks[i], shape, f32)

    n_cache = min(A_WINDOW, PAST_LEN)
    dt_b = jnp.exp(jax.random.uniform(ks[13], (DEPTH, B_HEADS), f32, np.log(1e-3), np.log(1e-1)))
    n_idx = jnp.broadcast_to(jnp.arange(C_STATE, dtype=f32), (DEPTH, C_GROUPS, C_STATE))
    return {
        'x_prompt': nrm(0, (BATCH, SEQ, D_MODEL), 1.0),
        'x_sample': nrm(1, (DEC_BATCH, DEC_SEQ, D_MODEL), 1.0),
        'cache_a_k': nrm(2, (DEPTH, DEC_BATCH, n_cache, A_HEADS, A_HEAD_DIM), 1.0),
        'cache_a_v': nrm(3, (DEPTH, DEC_BATCH, n_cache, A_HEADS, A_HEAD_DIM), 1.0),
        'state_b_conv': nrm(4, (DEPTH, DEC_BATCH, CONV_W - 1, 3 * B_WIDTH), 1.0),
        'state_b_ssm': nrm(5, (DEPTH, DEC_BATCH, B_HEADS, B_HEAD_DIM, B_HEAD_DIM), 0.1),
        'state_c_re': nrm(6, (DEPTH, DEC_BATCH, C_GROUPS, C_STATE), 0.5),
        'state_c_im': nrm(7, (DEPTH, DEC_BATCH, C_GROUPS, C_STATE), 0.5),
        'w_in': nrm(8, (DEPTH, D_MODEL, IN_WIDTH), D_MODEL ** -0.5),
        'a_rel_bias': nrm(9, (DEPTH, A_HEADS, 2 * MAX_REL + 1), 0.1),
        'b_conv_w': nrm(10, (DEPTH, CONV_W, 3 * B_WIDTH), CONV_W ** -0.5),
        'b_conv_b': nrm(11, (DEPTH, 3 * B_WIDTH), 0.01),
        'b_a_log': jnp.log(jax.random.uniform(ks[12], (DEPTH, B_HEADS), f32, 1.0, 16.0)),
        'b_dt_bias': dt_b + jnp.log(-jnp.expm1(-dt_b)),
        'b_norm_w': 1.0 + nrm(14, (DEPTH, B_HEAD_DIM), 0.01),
        'c_a_re': -0.5 + nrm(15, (DEPTH, C_GROUPS, C_STATE), 0.01),
        'c_a_im': np.pi * n_idx + nrm(16, (DEPTH, C_GROUPS, C_STATE), 0.01),
        'c_log_dt': jax.random.uniform(ks[17], (DEPTH, C_GROUPS), f32, np.log(1e-3), np.log(1e-1)),
        'c_b_re': nrm(18, (DEPTH, C_GROUPS, C_STATE, C_GROUP), (2.0 * C_GROUP) ** -0.5),
        'c_b_im': nrm(19, (DEPTH, C_GROUPS, C_STATE, C_GROUP), (2.0 * C_GROUP) ** -0.5),
        'c_c_re': nrm(20, (DEPTH, C_GROUPS, C_GROUP, C_STATE), C_STATE ** -0.5),
        'c_c_im': nrm(21, (DEPTH, C_GROUPS, C_GROUP, C_STATE), C_STATE ** -0.5),
        'c_d': nrm(22, (DEPTH, C_GROUPS, C_GROUP), 1.0),
        'c_glu_w': nrm(23, (DEPTH, C_WIDTH, C_WIDTH), C_WIDTH ** -0.5),
        'c_glu_b': nrm(24, (DEPTH, C_WIDTH), 0.01),
        'w_out': nrm(25, (DEPTH, D_MODEL, D_MODEL), D_MODEL ** -0.5 * DN_BETA),
        'ln1_g': 1.0 + nrm(26, (DEPTH, D_MODEL), 0.01),
        'ln1_b': nrm(27, (DEPTH, D_MODEL), 0.01),
        'w_up': nrm(28, (DEPTH, D_MODEL, D_FF), D_MODEL ** -0.5),
        'b_up': nrm(29, (DEPTH, D_FF), 0.01),
        'w_down': nrm(30, (DEPTH, D_FF, D_MODEL), D_FF ** -0.5 * DN_BETA),
        'ln2_g': 1.0 + nrm(31, (DEPTH, D_MODEL), 0.01),
        'ln2_b': nrm(32, (DEPTH, D_MODEL), 0.01),
    }


def reference(x_prompt, x_sample, cache_a_k, cache_a_v, state_b_conv, state_b_ssm, state_c_re, state_c_im,
              w_in, a_rel_bias, b_conv_w, b_conv_b, b_a_log, b_dt_bias, b_norm_w,
              c_a_re, c_a_im, c_log_dt, c_b_re, c_b_im, c_c_re, c_c_im, c_d, c_glu_w, c_glu_b,
              w_out, ln1_g, ln1_b, w_up, b_up, w_down, ln2_g, ln2_b):
    yp, ys = x_prompt, x_sample
    bsz = x_prompt.shape[0]
    p_out = [[], [], [], [], [], []]
    s_out = [[], [], [], [], [], []]
    for l in range(DEPTH):
        lp = {
            'w_in': w_in[l], 'a_rel_bias': a_rel_bias[l],
            'b_conv_w': b_conv_w[l], 'b_conv_b': b_conv_b[l], 'b_a_log': b_a_log[l],
            'b_dt_bias': b_dt_bias[l], 'b_norm_w': b_norm_w[l],
            'c_a_re': c_a_re[l], 'c_a_im': c_a_im[l], 'c_log_dt': c_log_dt[l],
            'c_b_re': c_b_re[l], 'c_b_im': c_b_im[l], 'c_c_re': c_c_re[l], 'c_c_im': c_c_im[l],
            'c_d': c_d[l], 'c_glu_w': c_glu_w[l], 'c_glu_b': c_glu_b[l], 'w_out': w_out[l],
            'ln1_g': ln1_g[l], 'ln1_b': ln1_b[l], 'w_up': w_up[l], 'b_up': b_up[l],
            'w_down': w_down[l], 'ln2_g': ln2_g[l], 'ln2_b': ln2_b[l],
        }
        zero_conv = jnp.zeros((bsz, CONV_W - 1, 3 * B_WIDTH), x_prompt.dtype)
        zero_s = jnp.zeros((bsz, B_HEADS, B_HEAD_DIM, B_HEAD_DIM), jnp.float32)
        zero_h = jnp.zeros((bsz, C_GROUPS, C_STATE), jnp.float32)
        yp, *st_p = trunk_layer(yp, lp, None, zero_conv, zero_s, zero_h, zero_h)
        ys, *st_s = trunk_layer(ys, lp, (cache_a_k[l], cache_a_v[l]), state_b_conv[l], state_b_ssm[l],
                                state_c_re[l], state_c_im[l])
        for acc, s in zip(p_out, st_p):
            acc.append(s)
        for acc, s in zip(s_out, st_s):
            acc.append(s)
    new_a_k_prompt = jnp.stack(p_out[0])
    new_a_v_prompt = jnp.stack(p_out[1])
    new_b_conv_prompt = jnp.stack(p_out[2])
    new_b_ssm_prompt = jnp.stack(p_out[3])
    new_c_re_prompt = jnp.stack(p_out[4])
    new_c_im_prompt = jnp.stack(p_out[5])
    new_a_k_sample = jnp.stack(s_out[0])
    new_a_v_sample = jnp.stack(s_out[1])
    new_b_conv_sample = jnp.stack(s_out[2])
    new_b_ssm_sample = jnp.stack(s_out[3])
    new_c_re_sample = jnp.stack(s_out[4])
    new_c_im_sample = jnp.stack(s_out[5])
    return (yp, ys,
            new_a_k_prompt, new_a_v_prompt, new_b_conv_prompt, new_b_ssm_prompt, new_c_re_prompt, new_c_im_prompt,
            new_a_k_sample, new_a_v_sample, new_b_conv_sample, new_b_ssm_sample, new_c_re_sample, new_c_im_sample)
```

```python
import math
import numpy as np
import concourse.bass as bass
import concourse.mybir as mybir
from concourse.bass_utils import run_bass_kernel_spmd

F32 = mybir.dt.float32
F32R = mybir.dt.float32r
BF16 = mybir.dt.bfloat16
ALU = mybir.AluOpType
AF = mybir.ActivationFunctionType
AX = mybir.AxisListType

D = 1024
INW = 2824
DFF = 4096
NEG = -30000.0
ALPHA = 4.0 ** 0.25
LN_EPS = 1e-5
RMS_EPS = 1e-6
PI = math.pi


class Buf:
    def __init__(self, name=""):
        self.w = {}
        self.r = {}
        self.dsem = None
        self.dcnt = 0
        self.name = name
        self.psum = False


class T(Buf):
    def __init__(self, h, name):
        super().__init__(name)
        self.h = h

    def __getitem__(self, k):
        return self.h[k]


class V(Buf):
    def __init__(self, ap, name):
        super().__init__(name)
        self.h = ap

    def __getitem__(self, k):
        return self.h[k]


class Eng:
    def __init__(self, name, e, sem):
        self.name = name
        self.e = e
        self.sem = sem
        self.cnt = 0
        self.known = {}


class KB:
    def __init__(self, nc):
        self.nc = nc
        self.E = {}
        for nm, e in [("pe", nc.tensor), ("act", nc.scalar), ("dve", nc.vector),
                      ("pool", nc.gpsimd), ("sp", nc.sync)]:
            self.E[nm] = Eng(nm, e, nc.alloc_semaphore("s_" + nm))
        self.sems = {}
        self.dma_bufs = []
        self.nps = 0
        self.ps = [self.psum("psb%d" % i) for i in range(8)]
        self.uid = 0

    def sb(self, name, shape, dt):
        return T(self.nc.alloc_sbuf_tensor(name, list(shape), dt), name)

    def psum(self, name):
        t = T(self.nc.alloc_psum_tensor(name, [128, 512], F32), name)
        t.psum = True
        return t

    def psn(self):
        p = self.ps[(0, 1, 2, 3, 7)[self.nps % 5]]
        self.nps += 1
        return p

    def _deps(self, E, r, w):
        deps = {}

        def add(d):
            for key, (sem, val) in d.items():
                if key not in deps or deps[key][1] < val:
                    deps[key] = (sem, val)
        for b in r:
            add(b.w)
            if b.psum:
                add({kk: vv for kk, vv in b.r.items() if kk != E.name})
        for b in w:
            add(b.w)
            add(b.r)
        for key, (sem, val) in deps.items():
            if E.name == "pe" and key == "pe" and val > E.cnt:
                continue
            if E.known.get(key, 0) < val:
                E.e.wait_ge(sem, val)
                E.known[key] = val

    def op(self, en, fn, r=(), w=(), sig=True):
        E = self.E[en]
        self._deps(E, r, w)
        ins = fn(E.e)
        key = E.name
        if sig:
            E.cnt += 1
            ins.then_inc(E.sem, 1)
            E.unsig = False
            st = (E.sem, E.cnt)
        else:
            assert en == "pe"
            E.unsig = True
            st = (E.sem, E.cnt + 1)
        for b in w:
            b.w = {key: st}
            b.r = {}
        for b in r:
            if b not in w:
                b.r[key] = st

    def dma(self, q, out, in_, r=(), w=(), own=None, part=False, **kw):
        E = self.E[q]
        self._deps(E, r, w)
        if own is None:
            own = w[0] if w else r[0]
        qt = "sw" if q == "pool" else "hw"
        if own.dsem is None:
            own.dsem = {}
        if qt not in own.dsem:
            self.uid += 1
            own.dsem[qt] = [self.nc.alloc_semaphore("d%d" % self.uid), 0]
            self.dma_bufs.append(own.dsem[qt])
        ent = own.dsem[qt]
        ins = E.e.dma_start(out=out, in_=in_, **kw)
        ent[1] += 16
        ins.then_inc(ent[0], 16)
        key = "d%d%s" % (id(own), qt)
        st = (ent[0], ent[1])
        for b in w:
            if part:
                b.w[key] = st
            else:
                b.w = {key: st}
                b.r = {}
        for b in r:
            b.r[key] = st

    def fence(self, srcs, dsts):
        for d in dsts:
            for s in srcs:
                for dd in (s.w, s.r):
                    for key, (sem, val) in dd.items():
                        if key not in d.r or d.r[key][1] < val:
                            d.r[key] = (sem, val)

    def finish(self):
        sp = self.E["sp"]
        assert not getattr(self.E["pe"], "unsig", False)
        for ent in self.dma_bufs:
            sp.e.wait_ge(ent[0], ent[1])
        for nm in ("pe", "act", "dve", "pool"):
            e = self.E[nm]
            if e.cnt:
                sp.e.wait_ge(e.sem, e.cnt)


def r32(ap):
    return ap.bitcast(F32R)


STOP = [99]
SERIAL = [0]
DBG = [0]


class StopBuild(Exception):
    pass


def chk(level):
    if STOP[0] <= level:
        raise StopBuild()


def build(SEQ, DEPTH, NS=4, SL=16):
    nc = bass.Bass("TRN2", target_bir_lowering=False)
    k = KB(nc)
    NB = SEQ // 512
    NSK = NS * SL

    def din(name, shape, dt=F32):
        return nc.dram_tensor(name, list(shape), dt, kind="ExternalInput")

    def dout(name, shape):
        return nc.dram_tensor(name, list(shape), F32, kind="ExternalOutput")

    def dscr(name, shape, dt):
        return T(nc.dram_tensor(name, list(shape), dt, kind="Internal"), name)

    xp = din("xp", [SEQ, D]); xs = din("xs", [NSK, D])
    ck = din("ck", [DEPTH, NS, 512, 512]); cv = din("cv", [DEPTH, NS, 512, 512])
    sconv = din("sconv", [DEPTH, NS, 3, 768]); sssm = din("sssm", [DEPTH, NS, 4, 64, 64])
    scre = din("scre", [DEPTH, NS, 1024]); scim = din("scim", [DEPTH, NS, 1024])
    w_in = din("w_in", [DEPTH, D, INW]); fext = din("fext", [DEPTH, 8, 768])
    arb = din("arb", [DEPTH, 8, 513])
    conv_w = din("conv_w", [DEPTH, 4, 768]); conv_b = din("conv_b", [DEPTH, 768])
    a_log = din("a_log", [DEPTH, 4]); dt_bias = din("dt_bias", [DEPTH, 4]); norm_w = din("norm_w", [DEPTH, 64])
    ca_re = din("ca_re", [DEPTH, 1024]); ca_im = din("ca_im", [DEPTH, 1024]); clogdt = din("clogdt", [DEPTH, 16])
    cb_re = din("cb_re", [DEPTH, 1024, 16]); cb_im = din("cb_im", [DEPTH, 1024, 16])
    cc_re = din("cc_re", [DEPTH, 256, 64]); cc_im = din("cc_im", [DEPTH, 256, 64])
    c_d = din("c_d", [DEPTH, 256]); glu_w = din("glu_w", [DEPTH, 256, 256]); glu_b = din("glu_b", [DEPTH, 256])
    w_out = din("w_out", [DEPTH, D, D]); ln1g = din("ln1g", [DEPTH, D]); ln1b = din("ln1b", [DEPTH, D])
    w_up = din("w_up", [DEPTH, D, DFF]); b_up = din("b_up", [DEPTH, DFF]); w_dn = din("w_dn", [DEPTH, DFF, D])
    ln2g = din("ln2g", [DEPTH, D]); ln2b = din("ln2b", [DEPTH, D])

    yp = dout("yp", [SEQ, D]); ys = dout("ys", [NSK, D])
    o_kp = dout("o_kp", [DEPTH, 512, 512]); o_vp = dout("o_vp", [DEPTH, 512, 512])
    o_convp = dout("o_convp", [DEPTH, 3, 768]); o_ssmp = dout("o_ssmp", [DEPTH, 4, 64, 64])
    o_crep = dout("o_crep", [DEPTH, 1024]); o_cimp = dout("o_cimp", [DEPTH, 1024])
    o_ks = dout("o_ks", [DEPTH, NSK, 512]); o_vs = dout("o_vs", [DEPTH, NSK, 512])
    o_convs = dout("o_convs", [DEPTH, NS, 3, 768]); o_ssms = dout("o_ssms", [DEPTH, NS, 4, 64, 64])
    o_cres = dout("o_cres", [DEPTH, NS, 1024]); o_cims = dout("o_cims", [DEPTH, NS, 1024])

    win_s = dscr("win_s", [DEPTH, 128, 8, INW], BF16)
    wout_s = dscr("wout_s", [DEPTH, 128, 8, D], BF16)
    wup_s = dscr("wup_s", [DEPTH, 128, 8, DFF], BF16)
    wdn_s = dscr("wdn_s", [DEPTH, 128, 32, D], BF16)

    def dap(t, off, pat):
        mx = off + sum(st * (c - 1) for st, c in pat)
        tot = 1
        for d_ in t.shape:
            tot *= d_
        assert 0 <= off and mx < tot, (t.name, off, pat, mx, tot)
        return bass.AP(t, off, [list(p) for p in pat])

    ident = k.sb("ident", [128, 128], F32); identb = k.sb("identb", [128, 128], BF16)
    ones = k.sb("ones", [128, 128], F32); zeros = k.sb("zeros", [128, 128], F32)
    U = k.sb("U", [128, 128], F32); NEG_SL = k.sb("NEG_SL", [128, 128], F32); NEG_UI = k.sb("NEG_UI", [128, 128], F32)
    mask0 = k.sb("mask0", [128, 128], F32); bones = k.sb("bones", [128, 128], F32)
    k.op("pool", lambda e: e.memset(ones[:], 1.0), w=[ones])
    k.op("pool", lambda e: e.memset(zeros[:], 0.0), w=[zeros])
    k.op("pool", lambda e: e.affine_select(out=ident[:], in_=ones[:], pattern=[[1, 128]], compare_op=ALU.is_equal,
                                           fill=0.0, base=0, channel_multiplier=-1), r=[ones], w=[ident])
    k.op("pool", lambda e: e.affine_select(out=U[:], in_=ones[:], pattern=[[1, 128]], compare_op=ALU.is_ge,
                                           fill=0.0, base=0, channel_multiplier=-1), r=[ones], w=[U])
    k.op("pool", lambda e: e.affine_select(out=NEG_UI[:], in_=zeros[:], pattern=[[1, 128]], compare_op=ALU.is_ge,
                                           fill=NEG, base=0, channel_multiplier=-1), r=[zeros], w=[NEG_UI])
    k.op("pool", lambda e: e.affine_select(out=NEG_SL[:], in_=zeros[:], pattern=[[-1, 128]], compare_op=ALU.is_gt,
                                           fill=NEG, base=0, channel_multiplier=1), r=[zeros], w=[NEG_SL])
    k.op("dve", lambda e: e.tensor_copy(out=identb[:], in_=ident[:]), r=[ident], w=[identb])
    k.op("pool", lambda e: e.memset(mask0[:], 0.0), w=[mask0])
    k.op("pool", lambda e: e.memset(mask0[0:64, 64:128], NEG), w=[mask0])
    k.op("pool", lambda e: e.tensor_copy(out=r32(bones[:]), in_=zeros[:]), r=[zeros], w=[bones])
    k.op("pool", lambda e: e.tensor_copy(out=r32(bones[0:64, 0:64]), in_=ones[0:64, 0:64]), r=[ones], w=[bones])
    k.op("pool", lambda e: e.tensor_copy(out=r32(bones[64:128, 64:128]), in_=ones[64:128, 64:128]), r=[ones], w=[bones])

    oa = k.sb("oa", [128, 8, 64], BF16)
    Et = V(oa.h[:, :, :].rearrange("p a b -> p (a b)").bitcast(F32)[:, 0:128], "Et")
    BDm = {}
    for G in (16, 32, 64):
        ng = 128 // G
        k.op("pool", lambda e: e.memset(Et[0:ng, :], 1.0), w=[Et])
        k.op("pool", lambda e, G=G, ng=ng: e.affine_select(out=Et[0:ng, :], in_=Et[0:ng, :], pattern=[[1, 128]], compare_op=ALU.is_ge,
                                                         fill=0.0, base=0, channel_multiplier=-G), w=[Et])
        k.op("pool", lambda e, G=G, ng=ng: e.affine_select(out=Et[0:ng, :], in_=Et[0:ng, :], pattern=[[-1, 128]], compare_op=ALU.is_ge,
                                                         fill=0.0, base=G - 1, channel_multiplier=G), w=[Et])
        pm_ = k.psn()
        k.op("pe", lambda e, ng=ng: e.matmul(pm_[:, 0:128], lhsT=Et[0:ng, :], rhs=Et[0:ng, :], start=True, stop=True), r=[Et], w=[pm_])
        BDm[G] = k.sb("BD%d" % G, [128, 128], F32)
        k.op("dve", lambda e, G=G: e.tensor_copy(out=BDm[G][:], in_=pm_[:, 0:128]), r=[pm_], w=[BDm[G]])
    BD16 = BDm[16]; OFF32 = BDm[32]; OFF64 = BDm[64]
    OFF128 = k.sb("OFF128", [128, 128], F32)
    k.op("dve", lambda e: e.tensor_tensor(out=OFF128[:], in0=ones[:], in1=BDm[64][:], op=ALU.subtract), r=[ones, BDm[64]], w=[OFF128])
    k.op("dve", lambda e: e.tensor_tensor(out=OFF64[:], in0=BDm[64][:], in1=BDm[32][:], op=ALU.subtract), r=[BDm[32]], w=[OFF64])
    k.op("dve", lambda e: e.tensor_tensor(out=OFF32[:], in0=BDm[32][:], in1=BDm[16][:], op=ALU.subtract), r=[BDm[16]], w=[OFF32])
    k.fence([Et], [oa])
    if STOP[0] <= -3:
        k.finish()
        return nc
    NSLOT = 1 + NS
    L = []
    tmpH = None
    for l in range(DEPTH):
        o = {}
        o["kT"] = k.sb("kT%d" % l, [128, 4, 1024], BF16)
        o["vr"] = k.sb("vr%d" % l, [128, 8, 528], BF16)
        o["biasM"] = k.sb("biasM%d" % l, [128, 8, 3, 128], BF16)
        o["cbias"] = k.sb("cbias%d" % l, [128, 8], F32)
        o["convw"] = k.sb("convw%d" % l, [128, 6, 4], F32)
        o["convb"] = k.sb("convb%d" % l, [128, 6], F32)
        o["negA"] = k.sb("negA%d" % l, [128, 4], F32)
        o["dtb"] = k.sb("dtb%d" % l, [128, 4], F32)
        o["normw"] = k.sb("normw%d" % l, [128, 64], F32)
        o["convh"] = [k.sb("convh%d_%d" % (l, s), [128, 6, 3], F32) for s in range(NSLOT)]
        o["S"] = [k.sb("S%d_%d" % (l, s), [128, 4, 64], F32) for s in range(NSLOT)]
        o["Sb"] = [[Buf("Sb%d_%d_%d" % (l, s, h_)) for h_ in range(4)] for s in range(NSLOT)]
        o["hr"] = [k.sb("hr%d_%d" % (l, s), [128, 8], F32) for s in range(NSLOT)]
        o["hi"] = [k.sb("hi%d_%d" % (l, s), [128, 8], F32) for s in range(NSLOT)]
        o["rotc"] = k.sb("rotc%d" % l, [128, 8, 64], F32)
        o["rots"] = k.sb("rots%d" % l, [128, 8, 64], F32)
        o["rmag"] = k.sb("rmag%d" % l, [128, 8], F32)
        o["c1"] = k.sb("c1_%d" % l, [128, 8], F32)
        o["s1"] = k.sb("s1_%d" % l, [128, 8], F32)
        o["BBr"] = k.sb("BBr%d" % l, [128, 4, 128], F32)
        o["BBi"] = k.sb("BBi%d" % l, [128, 4, 128], F32)
        o["CTr"] = k.sb("CTr%d" % l, [128, 8, 32], BF16)
        o["CTi"] = k.sb("CTi%d" % l, [128, 8, 32], BF16)
        o["CTpr"] = k.sb("CTpr%d" % l, [128, 2, 2, 64], BF16)
        o["CTpi"] = k.sb("CTpi%d" % l, [128, 2, 2, 64], BF16)
        o["dcol"] = k.sb("dcol%d" % l, [128, 2], F32)
        o["glub"] = k.sb("glub%d" % l, [128, 2], F32)
        o["gluw"] = k.sb("gluw%d" % l, [128, 2, 256], BF16)
        o["bup"] = k.sb("bup%d" % l, [128, 32], F32)
        L.append(o)

    xtok = [k.sb("xtok%d" % i, [128, 1024], F32) for i in range(4)]
    xT = k.sb("xT", [128, 8, 512], BF16)
    xTb = [Buf("xTb%d" % i) for i in range(4)]
    wsl = [k.sb("wsl%d" % i, [128, 8, 528], BF16) for i in range(2)]
    qT = k.sb("qT", [128, 4, 512], BF16)
    tmpH = k.sb("tmpH", [128, 3, 128], F32)
    rec = k.sb("rec", [128, 8], F32)
    arena = k.sb("arena", [128, 16384], BF16)
    AR = arena.h

    def av(b0, nbytes, dt):
        v = AR[:, b0 // 2:(b0 + nbytes) // 2]
        return v if dt == BF16 else v.bitcast(dt)
    hT = AR[:, :].rearrange("p (f t) -> p f t", t=512)
    mixT = av(0, 8192, BF16).rearrange("p (m t) -> p m t", t=512)
    ktok = av(8192, 4096, F32).rearrange("p (a c) -> p a c", c=256)
    vtok = av(12288, 4096, F32).rearrange("p (a c) -> p a c", c=256)
    gtok = V(av(16384, 4224, F32).rearrange("p (a c) -> p a c", c=264), "gtok")
    cbuf = [V(av(20608, 2112, F32), "cbuf")] * 2
    sc = [V(av(22720, 2048, F32), "sc")] * 2
    pT = [V(av(24768, 1280, BF16).rearrange("p (a c) -> p a c", c=128), "pT")] * 2
    sg = V(av(26048, 1024, F32), "sg")
    ob = V(av(27072, 1024, F32).rearrange("p (a c) -> p a c", c=64), "ob")
    obb = V(av(28096, 512, BF16), "obb")
    zT = V(av(28608, 2048, BF16).rearrange("p (a c) -> p a c", c=512), "zT")
    qkvT_t = k.sb("qkvT", [128, 6, 512], F32)
    qkvT = qkvT_t.h
    uT_t = k.sb("uT", [128, 2, 512], F32)
    uT = uT_t.h
    sqr = k.sb("sqr", [128, 512], F32)
    hTb = Buf("hT"); qkvb = [Buf("qkv%d" % i) for i in range(6)]; mixb = [Buf("mix%d" % i) for i in range(8)]
    ktb = Buf("ktok"); vtb = Buf("vtok"); uTb = Buf("uT")
    gtb = [Buf("gt%d" % i) for i in range(4)]
    mixer_bufs = mixb + [ktb, vtb, gtok, cbuf[0], sc[0], pT[0], sg, ob, obb, zT] + gtb
    ctmp = [k.sb("ctmp%d" % i, [128, 512], F32) for i in range(1)] * 2
    ctm2 = [k.sb("ctm2%d" % i, [128, 512], F32) for i in range(1)] * 2
    kvst = [k.sb("kvst%d" % i, [128, 512], F32) for i in range(2)]
    lnpA = [wsl[i].h[:, :, :].rearrange("p a b -> p (a b)").bitcast(F32)[:, 0:1024] for i in range(2)]
    lnp = wsl
    stats = k.sb("stats", [128, 2, 6], F32); mv = k.sb("mv", [128, 2], F32)
    rstd = k.sb("rstd", [128, 1], F32); nmr = k.sb("nmr", [128, 1], F32)
    small = {n: k.sb("sm_" + n, [128, 4], F32) for n in
             ["beta", "nbeta", "xa", "ea", "sp", "g", "gcs", "gln", "gl64", "egc", "ekl", "egl", "qs", "wsc", "ssq", "rs4"]}
    DM = [{n: k.sb("dm%d_%s" % (i, n), [128, 128], F32) for n in
           ["Gb", "NGb", "E2", "E1", "P", "PT", "X", "XT", "Pn", "PTn", "QKm", "Qa", "Qb"]} for i in range(1)]
    DV = [{n: k.sb("dv%d_%s" % (i, n), [128, 64], F32) for n in ["rU", "rW", "u", "vn", "t1"]} for i in range(1)]
    wTt = [k.sb("wTt%d" % i, [64, 128], F32) for i in range(2)]
    kd2s = [k.sb("kd2_%d" % i, [128, 128], F32) for i in range(2)]
    dm2 = {n: k.sb("dm1_%s" % n, [128, 128], F32) for n in ["P", "PT", "X", "XT", "Pn", "PTn", "QKm", "Qa", "Qb"]}
    for j_, n_ in enumerate(["Gb", "NGb", "E2", "E1"]):
        dm2[n_] = V(kvst[0].h[:, j_ * 128:(j_ + 1) * 128], "dm1_" + n_)
    DM.append(dm2)
    dv2 = {n: k.sb("dv1_%s" % n, [128, 64], F32) for n in ["rU", "rW", "vn"]}
    dv2["u"] = V(kvst[1].h[:, 0:64], "dv1_u"); dv2["t1"] = V(kvst[1].h[:, 64:128], "dv1_t1")
    DV.append(dv2)
    dn2_views = [dm2[n_] for n_ in ["Gb", "NGb", "E2", "E1"]] + [dv2["u"], dv2["t1"]]
    _n12 = ["xr", "xi", "t1", "t2", "t3", "t4", "zr", "zi", "wr", "wi", "hr", "hi"]
    S5c = {}
    for j_, n_ in enumerate(_n12):
        host = ctmp[0].h if j_ < 8 else ctm2[0].h
        jj_ = j_ if j_ < 8 else j_ - 8
        S5c[n_] = V(host[:, jj_ * 64:(jj_ + 1) * 64], "s5c_" + n_)
    S5cb = {n_: V(ctm2[0].h[:, 256 + j_ * 32:256 + (j_ + 1) * 32].bitcast(BF16), "s5cb_" + n_) for j_, n_ in enumerate(["hr", "hi"])}
    s5c_views = list(S5c.values()) + list(S5cb.values())
    S5 = [S5c, S5c]
    S5b = [S5cb, S5cb]
    xTf = xT.h[:, :, :].rearrange("p a b -> p (a b)").bitcast(F32)
    S5sets = [S5[0]]; S5bsets = [S5b[0]]; s5_views = []
    for si_ in range(2):
        base_ = si_ * 832
        d1 = {n_: V(xTf[:, base_ + j_ * 64:base_ + (j_ + 1) * 64], "s5v%d_%s" % (si_, n_)) for j_, n_ in
              enumerate(["xr", "xi", "t1", "t2", "t3", "t4", "zr", "zi", "wr", "wi", "hr", "hi"])}
        d2 = {n_: V(xTf[:, base_ + 768 + j_ * 32:base_ + 768 + (j_ + 1) * 32].bitcast(BF16), "s5vb%d_%s" % (si_, n_)) for j_, n_ in enumerate(["hr", "hi"])}
        S5sets.append(d1); S5bsets.append(d2)
        s5_views += list(d1.values()) + list(d2.values())
    w0r = k.sb("w0r", [128, 8], F32); w0i = k.sb("w0i", [128, 8], F32); w0t = k.sb("w0t", [128, 8], F32)
    ysb = [k.sb("ysb%d" % i, [128, 64], F32) for i in range(2)]
    yt = [k.sb("yt%d" % i, [128, 64], F32) for i in range(2)]
    fup = [ctmp[0], ctm2[0]]
    wst = [AR[:, i * 4096:(i + 1) * 4096].bitcast(F32) for i in range(2)]
    wstb = [Buf("wst%d" % i) for i in range(2)]
    wsb = [xtok[i].h[:, :].bitcast(BF16) for i in range(2)]
    print("SBUF bytes remaining:", nc.sbuf_bytes_remaining)

    SH = {}
    for i_, n_ in enumerate(["bre", "bim", "bbr", "bbi"]):
        SH[n_] = V(xT.h[:, 0:4, :].rearrange("p a b -> p (a b)").bitcast(F32)[:, i_ * 128:(i_ + 1) * 128].rearrange("p (a b) -> p a b", b=16), "sh_" + n_)
    SH["bt_"] = V(ctm2[0].h[:, 384:512].rearrange("p (a b) -> p a b", b=16), "sh_bt")
    SH["ra"] = V(ctmp[0].h[:, 0:256].rearrange("p (a b) -> p a b", b=32), "sh_ra")
    SH["rb"] = V(ctmp[0].h[:, 256:512].rearrange("p (a b) -> p a b", b=32), "sh_rb")
    SH["PTm"] = V(ctm2[0].h[:, 0:128], "sh_PTm")
    SH["Cd"] = V(ctm2[0].h[:, 128:384].rearrange("p (a b) -> p a b", b=128), "sh_Cd")
    oaf = oa.h[:, :, :].rearrange("p a b -> p (a b)").bitcast(F32)
    sh_small = []
    for j_, n_ in enumerate(["are", "aim", "dtr", "th", "lre", "lim", "cfr", "cfi", "t80", "t81", "t82", "t83"]):
        SH[n_] = V(oaf[:, 128 + 8 * j_:128 + 8 * (j_ + 1)], "sh_" + n_)
        sh_small.append(SH[n_])
    SH["ti32"] = V(oaf[:, 128 + 96:128 + 104].bitcast(mybir.dt.int32), "sh_ti32")
    sh_small.append(SH["ti32"])
    cnt = {"ev": 0}

    def evac_eng():
        cnt["ev"] += 1
        return "act" if cnt["ev"] % 2 else "dve"

    def copy(en, out, in_, r, w):
        if en == "act":
            k.op("act", lambda e: e.activation(out=out, in_=in_, func=AF.Copy), r=r, w=w)
        else:
            k.op(en, lambda e: e.tensor_copy(out=out, in_=in_), r=r, w=w)

    for l in range(DEPTH):
        o = L[l]
        for h in range(8):
            k.dma("sp", tmpH[:], dap(fext, (l * 8 + h) * 768 + 256, [[1, 128], [128, 3], [1, 128]]), w=[tmpH])
            for jj in range(3):
                rev = bass.AP(tmpH.h, jj * 128 + 127, [list(tmpH[:].ap[0]), [-1, 128]])
                k.op("pool", lambda e, rev=rev, jj=jj, h=h: e.tensor_copy(out=o["biasM"][:, h, jj, :], in_=rev),
                     r=[tmpH], w=[o["biasM"]])
        if STOP[0] <= -2:
            k.finish()
            return nc
        k.op("pool", lambda e: e.memset(o["biasM"][64:128, :, 2, 0:64], NEG), w=[o["biasM"]])
        k.dma("sp", o["cbias"][:], dap(arb, l * 8 * 513 + 512, [[0, 128], [513, 8]]), w=[o["cbias"]],
              allow_slow_non_contiguous=True)
        k.op("pool", lambda e: e.memset(o["vr"][:], 1.0), w=[o["vr"]])
        for ct in range(6):
            k.dma("sp", o["convw"][:, ct, :], dap(conv_w, l * 4 * 768 + ct * 128, [[1, 128], [768, 4]]), w=[o["convw"]], part=(ct > 0),
                  allow_slow_non_contiguous=True)
        k.dma("sp", o["convb"][:], dap(conv_b, l * 768, [[1, 128], [128, 6]]), w=[o["convb"]],
              allow_slow_non_contiguous=True)
        k.dma("sp", o["negA"][:], dap(a_log, l * 4, [[0, 128], [1, 4]]), w=[o["negA"]])
        k.dma("sp", o["dtb"][:], dap(dt_bias, l * 4, [[0, 128], [1, 4]]), w=[o["dtb"]])
        k.dma("sp", o["normw"][:], dap(norm_w, l * 64, [[0, 128], [1, 64]]), w=[o["normw"]])
        k.op("act", lambda e: e.activation(out=o["negA"][:], in_=o["negA"][:], func=AF.Exp), w=[o["negA"]])
        k.op("act", lambda e: e.activation(out=o["negA"][:], in_=o["negA"][:], func=AF.Copy, scale=-1.0), w=[o["negA"]])
        for s in range(NSLOT):
            if s == 0:
                k.op("pool", lambda e: e.memset(o["convh"][0][:], 0.0), w=[o["convh"][0]])
                for h_ in range(4):
                    k.op("pool", lambda e, h_=h_: e.tensor_copy(out=r32(o["S"][0][:, h_, :]), in_=zeros[:, 0:64]), r=[zeros], w=[o["S"][0]])
                k.op("pool", lambda e: e.memset(o["hr"][0][:], 0.0), w=[o["hr"][0]])
                k.op("pool", lambda e: e.memset(o["hi"][0][:], 0.0), w=[o["hi"][0]])
            else:
                sq = s - 1
                for ct in range(6):
                    k.dma("sp", o["convh"][s][:, ct, :], dap(sconv, (l * NS + sq) * 2304 + ct * 128, [[1, 128], [768, 3]]),
                          w=[o["convh"][s]], part=(ct > 0), allow_slow_non_contiguous=True)
                st_ = kvst[sq % 2]
                for hh_ in range(2):
                    k.dma("sp", st_[64 * hh_:64 * hh_ + 64, 0:256].rearrange("p (h e) -> p h e", e=64),
                          dap(sssm, (l * NS + sq) * 16384, [[64, 64], [4096, 4], [1, 64]]), w=[st_], part=(hh_ == 1))
                k.op("dve", lambda e, st_=st_, s=s: e.tensor_copy(
                    out=r32(o["S"][s][:]), in_=st_[:, 0:256].rearrange("p (h e) -> p h e", e=64)),
                    r=[st_], w=[o["S"][s]])
                k.dma("sp", o["hr"][s][:], dap(scre, (l * NS + sq) * 1024, [[1, 128], [128, 8]]), w=[o["hr"][s]],
                      allow_slow_non_contiguous=True)
                k.dma("sp", o["hi"][s][:], dap(scim, (l * NS + sq) * 1024, [[1, 128], [128, 8]]), w=[o["hi"][s]],
                      allow_slow_non_contiguous=True)
        if STOP[0] <= -1:
            k.finish()
            return nc
        are, aim, dtr, th, lre, lim, cfr, cfi = [SH[n_] for n_ in ["are", "aim", "dtr", "th", "lre", "lim", "cfr", "cfi"]]
        t8 = [SH["t80"], SH["t81"], SH["t82"], SH["t83"]]
        k.dma("sp", are[:], dap(ca_re, l * 1024, [[1, 128], [128, 8]]), w=[are], allow_slow_non_contiguous=True)
        k.dma("sp", aim[:], dap(ca_im, l * 1024, [[1, 128], [128, 8]]), w=[aim], allow_slow_non_contiguous=True)
        for gg in range(2):
            k.dma("sp", dtr[64 * gg:64 * gg + 64, :], dap(clogdt, l * 16 + gg, [[0, 64], [2, 8]]), w=[dtr], part=(gg == 1),
                  allow_slow_non_contiguous=True)
        k.op("act", lambda e: e.activation(out=dtr[:], in_=dtr[:], func=AF.Exp), w=[dtr])
        k.op("dve", lambda e: e.tensor_tensor(out=th[:], in0=dtr[:], in1=aim[:], op=ALU.mult), r=[dtr, aim], w=[th])
        k.op("dve", lambda e: e.tensor_tensor(out=t8[0][:], in0=dtr[:], in1=are[:], op=ALU.mult), r=[dtr, are], w=[t8[0]])
        k.op("act", lambda e: e.activation(out=o["rmag"][:], in_=t8[0][:], func=AF.Exp), r=[t8[0]], w=[o["rmag"]])
        ti32 = SH["ti32"]
        for (dst_, shift) in ((t8[1], 0.0), (t8[2], 0.5 * PI)):
            k.op("dve", lambda e, dst_=dst_, shift=shift: e.tensor_scalar_add(out=dst_[:], in0=th[:], scalar1=shift), r=[th], w=[dst_])
            k.op("dve", lambda e, dst_=dst_: e.tensor_scalar_mul(out=ti32[:], in0=dst_[:], scalar1=1.0 / (2 * PI)), r=[dst_], w=[ti32])
            k.op("dve", lambda e: e.tensor_copy(out=t8[3][:], in_=ti32[:]), r=[ti32], w=[t8[3]])
            k.op("dve", lambda e, dst_=dst_: e.scalar_tensor_tensor(out=dst_[:], in0=t8[3][:], scalar=-2 * PI, in1=dst_[:], op0=ALU.mult, op1=ALU.add), r=[t8[3]], w=[dst_])
            k.op("dve", lambda e, dst_=dst_: e.tensor_scalar(out=t8[3][:], in0=dst_[:], scalar1=PI, scalar2=-2 * PI, op0=ALU.is_gt, op1=ALU.mult), r=[dst_], w=[t8[3]])
            k.op("dve", lambda e, dst_=dst_: e.tensor_tensor(out=dst_[:], in0=dst_[:], in1=t8[3][:], op=ALU.add), r=[t8[3]], w=[dst_])
            k.op("dve", lambda e, dst_=dst_: e.tensor_scalar(out=t8[3][:], in0=dst_[:], scalar1=-PI, scalar2=2 * PI, op0=ALU.is_lt, op1=ALU.mult), r=[dst_], w=[t8[3]])
            k.op("dve", lambda e, dst_=dst_: e.tensor_tensor(out=dst_[:], in0=dst_[:], in1=t8[3][:], op=ALU.add), r=[t8[3]], w=[dst_])
        k.op("act", lambda e: e.activation(out=o["s1"][:], in_=t8[1][:], func=AF.Sin), r=[t8[1]], w=[o["s1"]])
        k.op("act", lambda e: e.activation(out=o["c1"][:], in_=t8[2][:], func=AF.Sin), r=[t8[2]], w=[o["c1"]])
        k.op("dve", lambda e: e.tensor_tensor(out=lre[:], in0=o["rmag"][:], in1=o["c1"][:], op=ALU.mult), r=[o["rmag"], o["c1"]], w=[lre])
        k.op("dve", lambda e: e.tensor_tensor(out=lim[:], in0=o["rmag"][:], in1=o["s1"][:], op=ALU.mult), r=[o["rmag"], o["s1"]], w=[lim])
        rc, rs = o["rotc"], o["rots"]
        k.op("pool", lambda e: e.memset(rc[:, :, 0:1], 1.0), w=[rc])
        k.op("pool", lambda e: e.memset(rs[:, :, 0:1], 0.0), w=[rs])
        k.op("dve", lambda e: e.tensor_copy(out=rc[:, :, 1:2], in_=o["c1"][:].unsqueeze(2)), r=[o["c1"]], w=[rc])
        k.op("dve", lambda e: e.tensor_copy(out=rs[:, :, 1:2], in_=o["s1"][:].unsqueeze(2)), r=[o["s1"]], w=[rs])
        ra, rb = SH["ra"], SH["rb"]
        K_ = 1
        while K_ < 63:
            n2 = min(K_, 63 - K_)
            cK = rc[:, :, K_:K_ + 1].to_broadcast([128, 8, n2]); sK = rs[:, :, K_:K_ + 1].to_broadcast([128, 8, n2])
            cs_ = rc[:, :, 1:1 + n2]; ss_ = rs[:, :, 1:1 + n2]
            d0 = K_ + 1
            k.op("dve", lambda e: e.tensor_tensor(out=ra[:, :, 0:n2], in0=cs_, in1=cK, op=ALU.mult), r=[rc], w=[ra])
            k.op("dve", lambda e: e.tensor_tensor(out=rb[:, :, 0:n2], in0=ss_, in1=sK, op=ALU.mult), r=[rs], w=[rb])
            k.op("dve", lambda e: e.tensor_tensor(out=rc[:, :, d0:d0 + n2], in0=ra[:, :, 0:n2], in1=rb[:, :, 0:n2], op=ALU.subtract), r=[ra, rb], w=[rc])
            k.op("dve", lambda e: e.tensor_tensor(out=ra[:, :, 0:n2], in0=cs_, in1=sK, op=ALU.mult), r=[rc, rs], w=[ra])
            k.op("dve", lambda e: e.tensor_tensor(out=rb[:, :, 0:n2], in0=ss_, in1=cK, op=ALU.mult), r=[rs, rc], w=[rb])
            k.op("dve", lambda e: e.tensor_tensor(out=rs[:, :, d0:d0 + n2], in0=ra[:, :, 0:n2], in1=rb[:, :, 0:n2], op=ALU.add), r=[ra, rb], w=[rs])
            K_ += n2
        k.op("dve", lambda e: e.tensor_tensor(out=t8[0][:], in0=are[:], in1=are[:], op=ALU.mult), r=[are], w=[t8[0]])
        k.op("dve", lambda e: e.tensor_tensor(out=t8[1][:], in0=aim[:], in1=aim[:], op=ALU.mult), r=[aim], w=[t8[1]])
        k.op("dve", lambda e: e.tensor_tensor(out=t8[0][:], in0=t8[0][:], in1=t8[1][:], op=ALU.add), r=[t8[1]], w=[t8[0]])
        k.op("dve", lambda e: e.reciprocal(out=t8[0][:], in_=t8[0][:]), w=[t8[0]])
        k.op("dve", lambda e: e.tensor_scalar_add(out=t8[1][:], in0=lre[:], scalar1=-1.0), r=[lre], w=[t8[1]])
        k.op("dve", lambda e: e.tensor_tensor(out=t8[2][:], in0=t8[1][:], in1=are[:], op=ALU.mult), r=[t8[1], are], w=[t8[2]])
        k.op("dve", lambda e: e.tensor_tensor(out=t8[3][:], in0=lim[:], in1=aim[:], op=ALU.mult), r=[lim, aim], w=[t8[3]])
        k.op("dve", lambda e: e.tensor_tensor(out=t8[2][:], in0=t8[2][:], in1=t8[3][:], op=ALU.add), r=[t8[3]], w=[t8[2]])
        k.op("dve", lambda e: e.tensor_tensor(out=cfr[:], in0=t8[2][:], in1=t8[0][:], op=ALU.mult), r=[t8[2], t8[0]], w=[cfr])
        k.op("dve", lambda e: e.tensor_tensor(out=t8[2][:], in0=lim[:], in1=are[:], op=ALU.mult), r=[lim, are], w=[t8[2]])
        k.op("dve", lambda e: e.tensor_tensor(out=t8[3][:], in0=t8[1][:], in1=aim[:], op=ALU.mult), r=[t8[1], aim], w=[t8[3]])
        k.op("dve", lambda e: e.tensor_tensor(out=t8[2][:], in0=t8[2][:], in1=t8[3][:], op=ALU.subtract), r=[t8[3]], w=[t8[2]])
        k.op("dve", lambda e: e.tensor_tensor(out=cfi[:], in0=t8[2][:], in1=t8[0][:], op=ALU.mult), r=[t8[2], t8[0]], w=[cfi])
        bre, bim, bbr, bbi, bt_ = [SH[n_] for n_ in ["bre", "bim", "bbr", "bbi", "bt_"]]
        k.dma("sp", bre[:], dap(cb_re, l * 16384, [[16, 128], [2048, 8], [1, 16]]), w=[bre])
        k.dma("sp", bim[:], dap(cb_im, l * 16384, [[16, 128], [2048, 8], [1, 16]]), w=[bim])
        cfrb = cfr[:].unsqueeze(2).to_broadcast([128, 8, 16]); cfib = cfi[:].unsqueeze(2).to_broadcast([128, 8, 16])
        k.op("dve", lambda e: e.tensor_tensor(out=bbr[:], in0=bre[:], in1=cfrb, op=ALU.mult), r=[bre, cfr], w=[bbr])
        k.op("dve", lambda e: e.tensor_tensor(out=bt_[:], in0=bim[:], in1=cfib, op=ALU.mult), r=[bim, cfi], w=[bt_])
        k.op("dve", lambda e: e.tensor_tensor(out=bbr[:], in0=bbr[:], in1=bt_[:], op=ALU.subtract), r=[bt_], w=[bbr])
        k.op("dve", lambda e: e.tensor_tensor(out=bbi[:], in0=bim[:], in1=cfrb, op=ALU.mult), r=[bim, cfr], w=[bbi])
        k.op("dve", lambda e: e.tensor_tensor(out=bt_[:], in0=bre[:], in1=cfib, op=ALU.mult), r=[bre, cfi], w=[bt_])
        k.op("dve", lambda e: e.tensor_tensor(out=bbi[:], in0=bbi[:], in1=bt_[:], op=ALU.add), r=[bt_], w=[bbi])
        PTm = SH["PTm"]
        for (src, dst) in ((bbr, o["BBr"]), (bbi, o["BBi"])):
            for i in range(8):
                hf, j = i // 4, i % 4
                k.op("pool", lambda e: e.memset(PTm[:], 0.0), w=[PTm])
                k.op("pool", lambda e, src=src, i=i, j=j: e.tensor_copy(out=PTm[0:64, 32 * j:32 * j + 16], in_=src[0:64, i, :]), r=[src], w=[PTm])
                k.op("pool", lambda e, src=src, i=i, j=j: e.tensor_copy(out=PTm[64:128, 32 * j + 16:32 * j + 32], in_=src[64:128, i, :]), r=[src], w=[PTm])
                p = k.psn()
                k.op("pe", lambda e, p=p: e.transpose(p[:, 0:128], PTm[:], ident[:]), r=[PTm, ident], w=[p])
                if j < 3:
                    k.op("dve", lambda e, p=p, dst=dst, hf=hf, j=j: e.tensor_copy(
                        out=r32(dst[32 * j:32 * j + 32, hf, :]), in_=p[32 * j:32 * j + 32, 0:128]), r=[p], w=[dst])
                else:
                    k.op("dve", lambda e, p=p, dst=dst, hf=hf, j=j: e.tensor_copy(
                        out=r32(dst[64:128, 2 + hf, :]), in_=p[64:128, 0:128]), r=[p], w=[dst])
        Cd = SH["Cd"]
        for (srcd, dst, neg) in ((cc_re, o["CTr"], False), (cc_im, o["CTi"], True)):
            for dup in range(2):
                k.dma("sp", Cd[:, :, 64 * dup:64 * dup + 64], dap(srcd, l * 16384, [[64, 128], [8192, 2], [1, 64]]),
                      w=[Cd], part=(dup == 1))
            k.op("pool", lambda e, dst=dst: e.memset(dst[:], 0.0), w=[dst])
            for hf in range(2):
                p = k.psn()
                k.op("pe", lambda e, p=p, hf=hf: e.transpose(p[:, 0:128], Cd[:, hf, :], ident[:]), r=[Cd, ident], w=[p])
                pv = p[:, 0:128].rearrange("p (j c) -> p j c", c=32)
                if neg:
                    k.op("act", lambda e, pv=pv, dst=dst, hf=hf: e.activation(out=dst[0:64, 4 * hf:4 * hf + 4, 0:16], in_=pv[0:64, :, 0:16], func=AF.Copy, scale=-1.0), r=[p], w=[dst])
                    k.op("act", lambda e, pv=pv, dst=dst, hf=hf: e.activation(out=dst[64:128, 4 * hf:4 * hf + 4, 16:32], in_=pv[64:128, :, 16:32], func=AF.Copy, scale=-1.0), r=[p], w=[dst])
                else:
                    k.op("dve", lambda e, pv=pv, dst=dst, hf=hf: e.tensor_copy(out=dst[0:64, 4 * hf:4 * hf + 4, 0:16], in_=pv[0:64, :, 0:16]), r=[p], w=[dst])
                    k.op("dve", lambda e, pv=pv, dst=dst, hf=hf: e.tensor_copy(out=dst[64:128, 4 * hf:4 * hf + 4, 16:32], in_=pv[64:128, :, 16:32]), r=[p], w=[dst])
        for (cs_, cp_) in ((o["CTr"], o["CTpr"]), (o["CTi"], o["CTpi"])):
            k.op("pool", lambda e, cp_=cp_: e.memset(cp_[:], 0.0), w=[cp_])
            for hf in range(2):
                k.op("pool", lambda e, cs_=cs_, cp_=cp_, hf=hf: e.tensor_copy(out=cp_[:, hf, 0, 0:32], in_=cs_[:, 4 * hf + 2, :]), r=[cs_], w=[cp_])
                k.op("pool", lambda e, cs_=cs_, cp_=cp_, hf=hf: e.tensor_copy(out=cp_[:, hf, 1, 32:64], in_=cs_[:, 4 * hf + 3, :]), r=[cs_], w=[cp_])
        k.dma("sp", o["dcol"][:], dap(c_d, l * 256, [[1, 128], [128, 2]]), w=[o["dcol"]], allow_slow_non_contiguous=True)
        k.dma("sp", o["glub"][:], dap(glu_b, l * 256, [[1, 128], [128, 2]]), w=[o["glub"]], allow_slow_non_contiguous=True)
        k.dma("sp", o["bup"][:], dap(b_up, l * 4096, [[1, 128], [128, 32]]), w=[o["bup"]], allow_slow_non_contiguous=True)
        gst = kvst[0]
        k.dma("sp", gst[:, 0:512].rearrange("p (a c) -> p a c", c=256), dap(glu_w, l * 65536, [[256, 128], [32768, 2], [1, 256]]), w=[gst])
        k.op("dve", lambda e: e.tensor_copy(out=o["gluw"][:], in_=gst[:, 0:512].rearrange("p (a c) -> p a c", c=256)), r=[gst], w=[o["gluw"]])

    if STOP[0] <= 0:
        k.finish()
        return nc
    k.fence([SH[n_] for n_ in ["bre", "bim", "bbr", "bbi", "bt_", "ra", "rb", "PTm", "Cd"]], [ctmp[0], ctm2[0]] + xTb)
    k.fence(sh_small, [oa])
    ci = 0
    ceng = ["dve", "act", "pool"]
    for l in range(DEPTH):
        jobs = []
        for kt in range(8):
            for (c0, W) in ((0, 1412), (1412, 1412)):
                jobs.append((w_in, (l * D + kt * 128) * INW + c0, INW, W, win_s, win_s.h[l, :, kt, c0:c0 + W]))
        for kt in range(8):
            jobs.append((w_out, (l * D + kt * 128) * D, D, D, wout_s, wout_s.h[l, :, kt, :]))
        for kt in range(8):
            for c0 in (0, 2048):
                jobs.append((w_up, (l * D + kt * 128) * DFF + c0, DFF, 2048, wup_s, wup_s.h[l, :, kt, c0:c0 + 2048]))
        for ft in range(32):
            jobs.append((w_dn, (l * DFF + ft * 128) * D, D, D, wdn_s, wdn_s.h[l, :, ft, :]))
        for (src, off, RS, W, scr, dstap) in jobs:
            a, ab = wst[ci % 2], wstb[ci % 2]
            b, bb = wsb[ci % 2], xtok[ci % 2]
            k.dma("sp", a[:, 0:W], dap(src, off, [[RS, 128], [1, W]]), w=[ab])
            copy(ceng[ci % 3], b[:, 0:W], a[:, 0:W], [ab], [bb])
            k.dma("pool", dstap, b[:, 0:W], r=[bb], w=[scr], own=bb, part=True)
            ci += 1

    if STOP[0] <= 1:
        k.finish()
        return nc
    def load_slab(scr_ap, W, scr, si):
        s = wsl[si[0] % 2]
        si[0] += 1
        k.dma("sp", s[:, :, 0:W], scr_ap, r=[scr], w=[s])
        return s

    si = [0]

    def transposes_to_xT(tiles):
        for ti, (c0, n) in enumerate(tiles):
            for half in range(2):
                p = k.psn()
                for q in range(4):
                    kt = half * 4 + q
                    k.op("pe", lambda e, p=p, q=q, kt=kt, ti=ti, n=n: e.transpose(
                        p[:, q * 128:q * 128 + n], xtok[ti][0:n, kt * 128:(kt + 1) * 128], ident[0:n, 0:n]),
                        r=[xtok[ti], ident], w=[p], sig=(q == 3))
                en = evac_eng()
                copy(en, xT[:, half * 4:half * 4 + 4, c0:c0 + n],
                     p[:, :].rearrange("p (q c) -> p q c", c=128)[:, :, 0:n], [p], [xTb[ti]])

    def layer_norm(ti, n, gsrc, bsrc, l):
        x = xtok[ti]
        g_t, b_t = lnp[0], lnp[1]
        gA, bA = lnpA[0], lnpA[1]
        if ti == 0:
            k.dma("sp", gA, dap(gsrc, l * D, [[0, 128], [1, D]]), w=[g_t])
            k.dma("sp", bA, dap(bsrc, l * D, [[0, 128], [1, D]]), w=[b_t])
        for hh in range(2):
            k.op("dve", lambda e, hh=hh: e.bn_stats(out=stats[0:n, hh, :], in_=x[0:n, hh * 512:(hh + 1) * 512]), r=[x], w=[stats])
        k.op("dve", lambda e: e.bn_aggr(out=mv[0:n, :], in_=stats[0:n, :, :].rearrange("p a b -> p (a b)")), r=[stats], w=[mv])
        k.op("act", lambda e: e.activation(out=rstd[0:n, :], in_=mv[0:n, 1:2], func=AF.Sqrt, bias=LN_EPS, scale=1.0), r=[mv], w=[rstd])
        k.op("dve", lambda e: e.reciprocal(out=rstd[0:n, :], in_=rstd[0:n, :]), w=[rstd])
        k.op("dve", lambda e: e.scalar_tensor_tensor(out=nmr[0:n, :], in0=mv[0:n, 0:1], scalar=-1.0, in1=rstd[0:n, :],
                                                     op0=ALU.mult, op1=ALU.mult), r=[mv, rstd], w=[nmr])
        k.op("act", lambda e: e.activation(out=x[0:n, :], in_=x[0:n, :], func=AF.Identity, bias=nmr[0:n, :], scale=rstd[0:n, :]),
             r=[nmr, rstd], w=[x])
        k.op("dve", lambda e: e.tensor_tensor(out=x[0:n, :], in0=x[0:n, :], in1=gA[0:n, :], op=ALU.mult), r=[g_t], w=[x])
        k.op("pool", lambda e: e.tensor_tensor(out=x[0:n, :], in0=x[0:n, :], in1=bA[0:n, :], op=ALU.add), r=[b_t], w=[x])

    def attention(l, qcols, nq, ktiles, ti):
        o = L[l]
        q0 = qcols
        poA, poB = k.ps[4], k.ps[5]
        for h in range(8):
            hp, pb = h // 2, 64 * (h % 2)
            po = poA if h < 4 else poB
            pc = (h % 4) * 66
            psA, psB = k.psn(), k.psn()
            s_, p_ = sc[h % 2], pT[h % 2]
            for qi_, (j, kc, nk, vs_) in enumerate(ktiles):
                dst = psB[0:nk, 0:nq] if j == 0 else psA[0:nk, (j - 1) * 128:(j - 1) * 128 + nq]
                k.op("pe", lambda e, dst=dst, kc=kc, nk=nk: e.matmul(
                    dst, lhsT=o["kT"][pb:pb + 64, hp, kc:kc + nk], rhs=qT[pb:pb + 64, hp, q0:q0 + nq], start=True, stop=True),
                    r=[o["kT"], qT], w=[psB if j == 0 else psA], sig=(qi_ == len(ktiles) - 1))
            for (j, kc, nk, vs_) in ktiles:
                if j == 0:
                    k.op("dve", lambda e, nk=nk: e.scalar_tensor_tensor(out=s_[0:nk, 0:nq], in0=psB[0:nk, 0:nq], scalar=0.125,
                                                                       in1=mask0[0:nk, 0:nq], op0=ALU.mult, op1=ALU.add),
                         r=[psB, mask0], w=[s_])
                    k.op("act", lambda e, nk=nk: e.activation(out=p_[0:nk, 0, 0:nq], in_=s_[0:nk, 0:nq], func=AF.Exp,
                                                              bias=o["cbias"][0:nk, h:h + 1], scale=1.0), r=[s_, o["cbias"]], w=[p_])
                elif j == 1:
                    k.op("act", lambda e, nk=nk: e.activation(out=p_[0:nk, 1, 0:nq], in_=psA[0:nk, 0:nq], func=AF.Exp,
                                                              bias=o["cbias"][0:nk, h:h + 1], scale=0.125), r=[psA, o["cbias"]], w=[p_])
                else:
                    cs = (j - 1) * 128
                    k.op("dve", lambda e, nk=nk, cs=cs, j=j: e.scalar_tensor_tensor(
                        out=s_[0:nk, cs:cs + nq], in0=psA[0:nk, cs:cs + nq], scalar=0.125, in1=o["biasM"][0:nk, h, j - 2, 0:nq],
                        op0=ALU.mult, op1=ALU.add), r=[psA, o["biasM"]], w=[s_])
                    k.op("act", lambda e, nk=nk, cs=cs, j=j: e.activation(out=p_[0:nk, j, 0:nq], in_=s_[0:nk, cs:cs + nq], func=AF.Exp),
                         r=[s_], w=[p_])
            for idx, (j, kc, nk, vs_) in enumerate(ktiles):
                k.op("pe", lambda e, j=j, nk=nk, vs_=vs_, idx=idx: e.matmul(
                    po[0:nq, pc:pc + 65], lhsT=p_[0:nk, j, 0:nq], rhs=o["vr"][0:nk, vs_, h * 66:h * 66 + 65],
                    start=(idx == 0), stop=(idx == len(ktiles) - 1)), r=[p_, o["vr"]], w=[po], sig=(idx == len(ktiles) - 1))
            yield
        for half, po in enumerate((poA, poB)):
            pv = po[0:nq, 0:264].rearrange("p (h c) -> p h c", c=66)
            k.op("dve", lambda e, pv=pv, half=half: e.reciprocal(out=rec[0:nq, half * 4:half * 4 + 4], in_=pv[:, :, 64]), r=[po], w=[rec])
            k.op("dve", lambda e, pv=pv, half=half: e.tensor_tensor(
                out=oa[0:nq, half * 4:half * 4 + 4, :], in0=pv[:, :, 0:64],
                in1=rec[0:nq, half * 4:half * 4 + 4].unsqueeze(2).to_broadcast([nq, 4, 64]), op=ALU.mult), r=[po, rec], w=[oa])
        p = k.psn()
        pb_ = p[:, :].bitcast(BF16)
        for m in range(4):
            k.op("pe", lambda e, m=m: e.transpose(pb_[:, m * 128:m * 128 + nq],
                                                   oa[0:nq, 2 * m:2 * m + 2, :].rearrange("p a b -> p (a b)"), identb[0:nq, 0:nq]),
                 r=[oa, identb], w=[p], sig=(m == 3))
        copy(evac_eng(), mixT[:, 0:4, q0:q0 + nq], pb_[:, 0:512].rearrange("p (m c) -> p m c", c=128)[:, :, 0:nq], [p], [mixb[ti]])

    def deltanet_tile(l, ti, c0, n, slot):
        o = L[l]
        sm = small
        gt = gtok[0:n, ti, :]
        S = o["S"][slot]
        k.op("act", lambda e: e.activation(out=sm["beta"][0:n, :], in_=gtok[0:n, ti, 0:4], func=AF.Sigmoid), r=[gtb[ti]], w=[sm["beta"]])
        k.op("dve", lambda e: e.tensor_scalar_mul(out=sm["nbeta"][0:n, :], in0=sm["beta"][0:n, :], scalar1=-1.0), r=[sm["beta"]], w=[sm["nbeta"]])
        k.op("dve", lambda e: e.tensor_tensor(out=sm["xa"][0:n, :], in0=gtok[0:n, ti, 4:8], in1=o["dtb"][0:n, :], op=ALU.add), r=[gtb[ti], o["dtb"]], w=[sm["xa"]])
        k.op("act", lambda e: e.activation(out=sm["ea"][0:n, :], in_=sm["xa"][0:n, :], func=AF.Exp), r=[sm["xa"]], w=[sm["ea"]])
        k.op("act", lambda e: e.activation(out=sm["sp"][0:n, :], in_=sm["ea"][0:n, :], func=AF.Ln, bias=1.0, scale=1.0), r=[sm["ea"]], w=[sm["sp"]])
        k.op("dve", lambda e: e.tensor_tensor(out=sm["g"][0:n, :], in0=sm["sp"][0:n, :], in1=o["negA"][0:n, :], op=ALU.mult), r=[sm["sp"], o["negA"]], w=[sm["g"]])
        p = k.psn()
        k.op("pe", lambda e: e.matmul(p[0:n, 0:4], lhsT=U[0:n, 0:n], rhs=sm["g"][0:n, :], start=True, stop=True), r=[U, sm["g"]], w=[p])
        k.op("pe", lambda e: e.matmul(p[0:n, 4:8], lhsT=ones[0:n, 0:n], rhs=sm["g"][0:n, :], start=True, stop=True), r=[ones, sm["g"]], w=[p])
        k.op("pe", lambda e: e.matmul(p[0:128, 8:12], lhsT=ones[0:n, 0:128], rhs=sm["g"][0:n, :], start=True, stop=True), r=[ones, sm["g"]], w=[p])
        k.op("dve", lambda e: e.tensor_copy(out=sm["gcs"][0:n, :], in_=p[0:n, 0:4]), r=[p], w=[sm["gcs"]])
        k.op("act", lambda e: e.activation(out=sm["egc"][0:n, :], in_=p[0:n, 0:4], func=AF.Exp), r=[p], w=[sm["egc"]])
        k.op("act", lambda e: e.activation(out=sm["egl"][:, :], in_=p[:, 8:12], func=AF.Exp), r=[p], w=[sm["egl"]])
        k.op("dve", lambda e: e.tensor_tensor(out=sm["gln"][0:n, :], in0=p[0:n, 4:8], in1=sm["gcs"][0:n, :], op=ALU.subtract), r=[p, sm["gcs"]], w=[sm["gln"]])
        k.op("act", lambda e: e.activation(out=sm["ekl"][0:n, :], in_=sm["gln"][0:n, :], func=AF.Exp), r=[sm["gln"]], w=[sm["ekl"]])
        k.op("dve", lambda e: e.tensor_scalar_mul(out=sm["qs"][0:n, :], in0=sm["egc"][0:n, :], scalar1=0.125), r=[sm["egc"]], w=[sm["qs"]])
        k.op("dve", lambda e: e.tensor_tensor(out=sm["wsc"][0:n, :], in0=sm["egc"][0:n, :], in1=sm["beta"][0:n, :], op=ALU.mult), r=[sm["egc"], sm["beta"]], w=[sm["wsc"]])
        nlev = 6 if n > 64 else (5 if n > 32 else (4 if n > 16 else 3))
        def head_gen(h, m, dv, wT_, kd2):
            hp, pb = h // 2, 64 * (h % 2)
            Sb_ = o["Sb"][slot][h]
            kTh = r32(qkvT[pb:pb + 64, 2 + hp, c0:c0 + n]); qTh = r32(qkvT[pb:pb + 64, hp, c0:c0 + n])
            kb_, qb_ = qkvb[2 + hp], qkvb[hp]
            k.op("dve", lambda e: e.tensor_scalar_mul(out=m["Gb"][0:n, 0:n], in0=ones[0:n, 0:n], scalar1=sm["g"][0:n, h:h + 1]), r=[ones, sm["g"]], w=[m["Gb"]])
            k.op("act", lambda e: e.activation(out=m["NGb"][0:n, 0:n], in_=m["Gb"][0:n, 0:n], func=AF.Copy, scale=-1.0), r=[m["Gb"]], w=[m["NGb"]])
            pd = k.psn()
            k.op("pe", lambda e: e.matmul(pd[0:n, 0:n], lhsT=U[0:n, 0:n], rhs=m["Gb"][0:n, 0:n], start=True, stop=False), r=[U, m["Gb"]], w=[pd])
            k.op("pe", lambda e: e.matmul(pd[0:n, 0:n], lhsT=m["NGb"][0:n, 0:n], rhs=U[0:n, 0:n], start=False, stop=True), r=[U, m["NGb"]], w=[pd])
            k.op("pe", lambda e: e.matmul(pd[0:n, 128:128 + n], lhsT=kTh, rhs=kTh, start=True, stop=True), r=[kb_], w=[pd])
            k.op("pe", lambda e: e.matmul(pd[0:n, 256:256 + n], lhsT=kTh, rhs=qTh, start=True, stop=True), r=[kb_, qb_], w=[pd])
            k.op("dve", lambda e: e.tensor_tensor(out=m["E2"][0:n, 0:n], in0=pd[0:n, 0:n], in1=NEG_SL[0:n, 0:n], op=ALU.add), r=[pd, NEG_SL], w=[m["E2"]])
            k.op("act", lambda e: e.activation(out=m["E2"][0:n, 0:n], in_=m["E2"][0:n, 0:n], func=AF.Exp), w=[m["E2"]])
            k.op("dve", lambda e: e.scalar_tensor_tensor(out=m["E1"][0:n, 0:n], in0=pd[0:n, 0:n], scalar=-1.0, in1=NEG_UI[0:n, 0:n],
                                                         op0=ALU.mult, op1=ALU.add), r=[pd, NEG_UI], w=[m["E1"]])
            k.op("act", lambda e: e.activation(out=m["E1"][0:n, 0:n], in_=m["E1"][0:n, 0:n], func=AF.Exp), w=[m["E1"]])
            k.op("dve", lambda e: e.scalar_tensor_tensor(out=r32(m["PT"][0:n, 0:n]), in0=pd[0:n, 128:128 + n], scalar=sm["nbeta"][0:n, h:h + 1],
                                                         in1=m["E2"][0:n, 0:n], op0=ALU.mult, op1=ALU.mult), r=[pd, sm["nbeta"], m["E2"]], w=[m["PT"]])
            k.op("dve", lambda e: e.scalar_tensor_tensor(out=r32(m["QKm"][0:n, 0:n]), in0=pd[0:n, 256:256 + n], scalar=0.125,
                                                         in1=m["E1"][0:n, 0:n], op0=ALU.mult, op1=ALU.mult), r=[pd, m["E1"]], w=[m["QKm"]])
            yield
            pt = k.psn()
            k.op("pe", lambda e: e.transpose(pt[0:n, 0:n], m["PT"][0:n, 0:n], ident[0:n, 0:n]), r=[m["PT"], ident], w=[pt])
            k.op("act", lambda e: e.activation(out=r32(m["P"][0:n, 0:n]), in_=pt[0:n, 0:n], func=AF.Copy), r=[pt], w=[m["P"]])
            big = n > 16
            if big:
                k.op("dve", lambda e: e.tensor_tensor(out=r32(m["Qa"][0:n, 0:n]), in0=pt[0:n, 0:n], in1=BD16[0:n, 0:n], op=ALU.mult), r=[pt, BD16], w=[m["Qa"]])
                k.op("pool", lambda e: e.tensor_tensor(out=r32(m["Qb"][0:n, 0:n]), in0=m["PT"][0:n, 0:n], in1=BD16[0:n, 0:n], op=ALU.mult), r=[m["PT"], BD16], w=[m["Qb"]])
                P, PT_, Pn, PTn = "Qa", "Qb", "Pn", "PTn"
            else:
                P, PT_, Pn, PTn = "P", "PT", "Pn", "PTn"
            k.op("dve", lambda e: e.tensor_tensor(out=r32(m["X"][0:n, 0:n]), in0=m[P][0:n, 0:n], in1=ident[0:n, 0:n], op=ALU.add), r=[m[P], ident], w=[m["X"]])
            k.op("pool", lambda e: e.tensor_tensor(out=r32(m["XT"][0:n, 0:n]), in0=m[PT_][0:n, 0:n], in1=ident[0:n, 0:n], op=ALU.add), r=[m[PT_], ident], w=[m["XT"]])
            for lev in range(1, 4):
                last = (lev == 3) and not big
                lastp = lev == 3
                pp = k.psn()
                k.op("pe", lambda e, P=P, PT_=PT_: e.matmul(pp[0:n, 0:n], lhsT=r32(m[PT_][0:n, 0:n]), rhs=r32(m[P][0:n, 0:n]), start=True, stop=True),
                     r=[m[PT_], m[P]], w=[pp])
                if not lastp:
                    k.op("pe", lambda e, P=P, PT_=PT_: e.matmul(pp[0:n, 128:128 + n], lhsT=r32(m[P][0:n, 0:n]), rhs=r32(m[PT_][0:n, 0:n]), start=True, stop=True),
                         r=[m[PT_], m[P]], w=[pp])
                k.op("act", lambda e, Pn=Pn: e.activation(out=r32(m[Pn][0:n, 0:n]), in_=pp[0:n, 0:n], func=AF.Copy), r=[pp], w=[m[Pn]])
                if not lastp:
                    k.op("dve", lambda e, PTn=PTn: e.tensor_copy(out=r32(m[PTn][0:n, 0:n]), in_=pp[0:n, 128:128 + n]), r=[pp], w=[m[PTn]])
                px = k.psn()
                k.op("pe", lambda e, Pn=Pn: e.matmul(px[0:n, 0:n], lhsT=r32(m["XT"][0:n, 0:n]), rhs=r32(m[Pn][0:n, 0:n]), start=True, stop=True),
                     r=[m["XT"], m[Pn]], w=[px])
                if not last:
                    k.op("pe", lambda e, Pn=Pn: e.matmul(px[0:n, 128:128 + n], lhsT=r32(m[Pn][0:n, 0:n]), rhs=r32(m["XT"][0:n, 0:n]), start=True, stop=True),
                         r=[m["XT"], m[Pn]], w=[px])
                k.op("dve", lambda e: e.tensor_tensor(out=r32(m["X"][0:n, 0:n]), in0=m["X"][0:n, 0:n], in1=px[0:n, 0:n], op=ALU.add), r=[px], w=[m["X"]])
                if not last:
                    k.op("dve", lambda e: e.tensor_tensor(out=r32(m["XT"][0:n, 0:n]), in0=m["XT"][0:n, 0:n], in1=px[0:n, 128:128 + n], op=ALU.add), r=[px], w=[m["XT"]])
                P, Pn = Pn, P
                PT_, PTn = PTn, PT_
                yield
            if big:
                for li, OFF in enumerate((OFF32, OFF64, OFF128)):
                    lastl = li == 2
                    k.op("dve", lambda e, OFF=OFF: e.tensor_tensor(out=r32(m["Qa"][:, :]), in0=m["P"][:, :], in1=OFF[:, :], op=ALU.mult), r=[m["P"], OFF], w=[m["Qa"]])
                    k.op("pool", lambda e, OFF=OFF: e.tensor_tensor(out=r32(m["Qb"][:, :]), in0=m["PT"][:, :], in1=OFF[:, :], op=ALU.mult), r=[m["PT"], OFF], w=[m["Qb"]])
                    pw = k.psn()
                    k.op("pe", lambda e: e.matmul(pw[:, 0:128], lhsT=r32(m["Qb"][:, :]), rhs=r32(m["X"][:, :]), start=True, stop=True), r=[m["Qb"], m["X"]], w=[pw])
                    if not lastl:
                        k.op("pe", lambda e: e.matmul(pw[:, 128:256], lhsT=r32(m["Qa"][:, :]), rhs=r32(m["XT"][:, :]), start=True, stop=True), r=[m["Qa"], m["XT"]], w=[pw])
                    k.op("act", lambda e: e.activation(out=r32(m["Pn"][:, :]), in_=pw[:, 0:128], func=AF.Copy), r=[pw], w=[m["Pn"]])
                    if not lastl:
                        k.op("dve", lambda e: e.tensor_copy(out=r32(m["PTn"][:, :]), in_=pw[:, 128:256]), r=[pw], w=[m["PTn"]])
                    pz = k.psn()
                    k.op("pe", lambda e: e.matmul(pz[:, 0:128], lhsT=r32(m["XT"][:, :]), rhs=r32(m["Pn"][:, :]), start=True, stop=True), r=[m["XT"], m["Pn"]], w=[pz])
                    if not lastl:
                        k.op("pe", lambda e: e.matmul(pz[:, 128:256], lhsT=r32(m["X"][:, :]), rhs=r32(m["PTn"][:, :]), start=True, stop=True), r=[m["X"], m["PTn"]], w=[pz])
                    k.op("dve", lambda e: e.tensor_tensor(out=r32(m["X"][:, :]), in0=m["X"][:, :], in1=pz[:, 0:128], op=ALU.add), r=[pz], w=[m["X"]])
                    if not lastl:
                        k.op("dve", lambda e: e.tensor_tensor(out=r32(m["XT"][:, :]), in0=m["XT"][:, :], in1=pz[:, 128:256], op=ALU.add), r=[pz], w=[m["XT"]])
                    yield
            k.op("dve", lambda e: e.tensor_scalar_mul(out=r32(dv["rU"][0:n, :]), in0=vtok[0:n, ti, h * 64:(h + 1) * 64], scalar1=sm["beta"][0:n, h:h + 1]),
                 r=[vtb, sm["beta"]], w=[dv["rU"]])
            k.op("act", lambda e: e.activation(out=r32(dv["rW"][0:n, :]), in_=ktok[0:n, ti, h * 64:(h + 1) * 64], func=AF.Identity, scale=sm["wsc"][0:n, h:h + 1]),
                 r=[ktb, sm["wsc"]], w=[dv["rW"]])
            k.op("act", lambda e: e.activation(out=r32(kd2[0:n, 0:64]), in_=ktok[0:n, ti, h * 64:(h + 1) * 64], func=AF.Identity, scale=sm["ekl"][0:n, h:h + 1]),
                 r=[ktb, sm["ekl"]], w=[kd2])
            k.op("dve", lambda e: e.tensor_scalar_mul(out=r32(kd2[0:n, 64:128]), in0=ktok[0:n, ti, h * 64:(h + 1) * 64], scalar1=sm["ekl"][0:n, h:h + 1]),
                 r=[ktb, sm["ekl"]], w=[kd2])
            pu = k.psn()
            k.op("pe", lambda e: e.matmul(pu[0:n, 0:64], lhsT=r32(m["X"][0:n, 0:n]), rhs=r32(dv["rU"][0:n, :]), start=True, stop=True), r=[m["X"], dv["rU"]], w=[pu])
            k.op("pe", lambda e: e.matmul(pu[0:64, 128:128 + n], lhsT=r32(dv["rW"][0:n, :]), rhs=r32(m["X"][0:n, 0:n]), start=True, stop=True), r=[m["X"], dv["rW"]], w=[pu])
            k.op("act", lambda e: e.activation(out=dv["u"][0:n, :], in_=pu[0:n, 0:64], func=AF.Copy), r=[pu], w=[dv["u"]])
            k.op("dve", lambda e: e.tensor_copy(out=r32(wT_[0:64, 0:n]), in_=pu[0:64, 128:128 + n]), r=[pu], w=[wT_])
            yield
            Sh = r32(S[pb:pb + 64, h, :])
            Sh0 = r32(S[0:64, h, :])
            pv_ = k.psn()
            k.op("pe", lambda e: e.matmul(pv_[0:n, 0:64], lhsT=r32(wT_[0:64, 0:n]), rhs=Sh0, start=True, stop=True), r=[wT_, S, Sb_], w=[pv_])
            k.op("pe", lambda e: e.matmul(pv_[0:n, 64:128], lhsT=qTh, rhs=Sh, start=True, stop=True), r=[qb_, S, Sb_], w=[pv_])
            k.op("dve", lambda e: e.tensor_tensor(out=r32(dv["vn"][0:n, :]), in0=dv["u"][0:n, :], in1=pv_[0:n, 0:64], op=ALU.subtract), r=[dv["u"], pv_], w=[dv["vn"]])
            k.op("act", lambda e: e.activation(out=dv["t1"][0:n, :], in_=pv_[0:n, 64:128], func=AF.Identity, scale=sm["qs"][0:n, h:h + 1]), r=[pv_, sm["qs"]], w=[dv["t1"]])
            po_ = k.psn()
            k.op("pe", lambda e: e.matmul(po_[0:n, 0:64], lhsT=r32(m["QKm"][0:n, 0:n]), rhs=r32(dv["vn"][0:n, :]), start=True, stop=True), r=[m["QKm"], dv["vn"]], w=[po_])
            k.op("pe", lambda e: e.matmul(po_[0:128, 64:128], lhsT=r32(kd2[0:n, 0:128]), rhs=r32(dv["vn"][0:n, :]), start=True, stop=True), r=[kd2, dv["vn"]], w=[po_])
            k.op("dve", lambda e: e.tensor_tensor(out=ob[0:n, h, :], in0=dv["t1"][0:n, :], in1=po_[0:n, 0:64], op=ALU.add), r=[dv["t1"], po_], w=[ob])
            k.op("dve", lambda e: e.scalar_tensor_tensor(out=r32(S[:, h, :]), in0=S[:, h, :], scalar=sm["egl"][:, h:h + 1], in1=po_[:, 64:128],
                                                         op0=ALU.mult, op1=ALU.add), r=[sm["egl"], po_, S], w=[Sb_])
            yield
        for pair in ((0, 1), (2, 3)):
            hg = [head_gen(h_, DM[gi], DV[gi], wTt[gi], kd2s[gi]) for gi, h_ in enumerate(pair)]
            while hg:
                for g_ in list(hg):
                    try:
                        next(g_)
                    except StopIteration:
                        hg.remove(g_)
                yield
        obsq = sg.h[:, :].rearrange("p (a b) -> p a b", b=64)
        k.op("pool", lambda e: e.tensor_tensor(out=obsq[0:n, :, :], in0=ob[0:n, :, :], in1=ob[0:n, :, :], op=ALU.mult), r=[ob], w=[sg])
        k.op("dve", lambda e: e.tensor_reduce(out=sm["ssq"][0:n, :], in_=obsq[0:n, :, :], axis=AX.X, op=ALU.add), r=[sg], w=[sm["ssq"]])
        k.op("act", lambda e: e.activation(out=sm["rs4"][0:n, :], in_=sm["ssq"][0:n, :], func=AF.Sqrt, bias=RMS_EPS, scale=1.0 / 64), r=[sm["ssq"]], w=[sm["rs4"]])
        k.op("dve", lambda e: e.reciprocal(out=sm["rs4"][0:n, :], in_=sm["rs4"][0:n, :]), w=[sm["rs4"]])
        k.op("dve", lambda e: e.tensor_tensor(out=ob[0:n, :, :], in0=ob[0:n, :, :], in1=sm["rs4"][0:n, :].unsqueeze(2).to_broadcast([n, 4, 64]), op=ALU.mult),
             r=[sm["rs4"]], w=[ob])
        k.op("pool", lambda e: e.tensor_tensor(out=ob[0:n, :, :], in0=ob[0:n, :, :], in1=o["normw"][0:n, :].unsqueeze(1).to_broadcast([n, 4, 64]), op=ALU.mult),
             r=[o["normw"]], w=[ob])
        k.op("act", lambda e: e.activation(out=sg[0:n, :], in_=gtok[0:n, ti, 8:264], func=AF.Silu), r=[gtb[ti]], w=[sg])
        k.op("dve", lambda e: e.tensor_tensor(out=obb[0:n, :], in0=ob[0:n, :, :].rearrange("p a b -> p (a b)"), in1=sg[0:n, :], op=ALU.mult), r=[ob, sg], w=[obb])
        p = k.psn()
        pbf = p[:, :].bitcast(BF16)
        for mm in range(2):
            k.op("pe", lambda e, mm=mm: e.transpose(pbf[:, mm * 128:mm * 128 + n], obb[0:n, mm * 128:(mm + 1) * 128], identb[0:n, 0:n]), r=[obb, identb], w=[p])
        copy(evac_eng(), mixT[:, 4:6, c0:c0 + n], pbf[:, 0:256].rearrange("p (m c) -> p m c", c=128)[:, :, 0:n], [p], [mixb[4 + ti]])

    def s5_sub(l, c0, m_, slot, first_in_seg):
        o = L[l]
        hr_, hi_ = o["hr"][slot], o["hi"][slot]
        rc, rs = o["rotc"], o["rots"]
        k.op("dve", lambda e: e.tensor_tensor(out=w0r[:], in0=o["c1"][:], in1=hr_[:], op=ALU.mult), r=[o["c1"], hr_], w=[w0r])
        k.op("dve", lambda e: e.tensor_tensor(out=w0t[:], in0=o["s1"][:], in1=hi_[:], op=ALU.mult), r=[o["s1"], hi_], w=[w0t])
        k.op("dve", lambda e: e.tensor_tensor(out=w0r[:], in0=w0r[:], in1=w0t[:], op=ALU.subtract), r=[w0t], w=[w0r])
        k.op("dve", lambda e: e.tensor_tensor(out=w0i[:], in0=o["c1"][:], in1=hi_[:], op=ALU.mult), r=[o["c1"], hi_], w=[w0i])
        k.op("dve", lambda e: e.tensor_tensor(out=w0t[:], in0=o["s1"][:], in1=hr_[:], op=ALU.mult), r=[o["s1"], hr_], w=[w0t])
        k.op("dve", lambda e: e.tensor_tensor(out=w0i[:], in0=w0i[:], in1=w0t[:], op=ALU.add), r=[w0t], w=[w0i])
        pY = [k.ps[6], k.ps[6]]
        def tile_gen(i, t, tb):
            hf, j = i // 4, i % 4
            px = k.psn()
            if j < 3:
                urh = r32(uT[32 * j:32 * j + 32, hf, c0:c0 + m_])
                lr_, li_ = r32(o["BBr"][32 * j:32 * j + 32, hf, :]), r32(o["BBi"][32 * j:32 * j + 32, hf, :])
            else:
                urh = r32(uT[64:128, hf, c0:c0 + m_])
                lr_, li_ = r32(o["BBr"][64:128, 2 + hf, :]), r32(o["BBi"][64:128, 2 + hf, :])
            k.op("pe", lambda e: e.matmul(px[:, 0:m_], lhsT=lr_, rhs=urh, start=True, stop=True), r=[o["BBr"], uTb], w=[px])
            k.op("pe", lambda e: e.matmul(px[:, 128:128 + m_], lhsT=li_, rhs=urh, start=True, stop=True), r=[o["BBi"], uTb], w=[px])
            k.op("act", lambda e: e.activation(out=t["xr"][:, 0:m_], in_=px[:, 0:m_], func=AF.Copy), r=[px], w=[t["xr"]])
            k.op("act", lambda e: e.activation(out=t["xi"][:, 0:m_], in_=px[:, 128:128 + m_], func=AF.Copy), r=[px], w=[t["xi"]])
            cc, ss = rc[:, i, 0:m_], rs[:, i, 0:m_]
            yield
            k.op("pool", lambda e: e.tensor_tensor(out=t["t1"][:, 0:m_], in0=t["xr"][:, 0:m_], in1=cc, op=ALU.mult), r=[t["xr"], rc], w=[t["t1"]])
            k.op("pool", lambda e: e.tensor_tensor(out=t["t2"][:, 0:m_], in0=t["xi"][:, 0:m_], in1=ss, op=ALU.mult), r=[t["xi"], rs], w=[t["t2"]])
            k.op("pool", lambda e: e.tensor_tensor(out=t["t3"][:, 0:m_], in0=t["xi"][:, 0:m_], in1=cc, op=ALU.mult), r=[t["xi"], rc], w=[t["t3"]])
            k.op("pool", lambda e: e.tensor_tensor(out=t["t4"][:, 0:m_], in0=t["xr"][:, 0:m_], in1=ss, op=ALU.mult), r=[t["xr"], rs], w=[t["t4"]])
            k.op("pool", lambda e: e.tensor_tensor(out=t["zr"][:, 0:m_], in0=t["t1"][:, 0:m_], in1=t["t2"][:, 0:m_], op=ALU.add), r=[t["t1"], t["t2"]], w=[t["zr"]])
            k.op("pool", lambda e: e.tensor_tensor(out=t["zi"][:, 0:m_], in0=t["t3"][:, 0:m_], in1=t["t4"][:, 0:m_], op=ALU.subtract), r=[t["t3"], t["t4"]], w=[t["zi"]])
            yield
            rb_ = o["rmag"][:, i:i + 1].to_broadcast([128, m_])
            k.op("dve", lambda e: e.tensor_tensor_scan(out=t["wr"][:, 0:m_], data0=rb_, data1=t["zr"][:, 0:m_], initial=w0r[:, i:i + 1], op0=ALU.mult, op1=ALU.add),
                 r=[o["rmag"], t["zr"], w0r], w=[t["wr"]])
            k.op("dve", lambda e: e.tensor_tensor_scan(out=t["wi"][:, 0:m_], data0=rb_, data1=t["zi"][:, 0:m_], initial=w0i[:, i:i + 1], op0=ALU.mult, op1=ALU.add),
                 r=[o["rmag"], t["zi"], w0i], w=[t["wi"]])
            yield
            k.op("pool", lambda e: e.tensor_tensor(out=t["t1"][:, 0:m_], in0=t["wr"][:, 0:m_], in1=cc, op=ALU.mult), r=[t["wr"], rc], w=[t["t1"]])
            k.op("pool", lambda e: e.tensor_tensor(out=t["t2"][:, 0:m_], in0=t["wi"][:, 0:m_], in1=ss, op=ALU.mult), r=[t["wi"], rs], w=[t["t2"]])
            k.op("pool", lambda e: e.tensor_tensor(out=t["t3"][:, 0:m_], in0=t["wi"][:, 0:m_], in1=cc, op=ALU.mult), r=[t["wi"], rc], w=[t["t3"]])
            k.op("pool", lambda e: e.tensor_tensor(out=t["t4"][:, 0:m_], in0=t["wr"][:, 0:m_], in1=ss, op=ALU.mult), r=[t["wr"], rs], w=[t["t4"]])
            k.op("pool", lambda e: e.tensor_tensor(out=t["hr"][:, 0:m_], in0=t["t1"][:, 0:m_], in1=t["t2"][:, 0:m_], op=ALU.subtract), r=[t["t1"], t["t2"]], w=[t["hr"]])
            k.op("pool", lambda e: e.tensor_tensor(out=t["hi"][:, 0:m_], in0=t["t3"][:, 0:m_], in1=t["t4"][:, 0:m_], op=ALU.add), r=[t["t3"], t["t4"]], w=[t["hi"]])
            yield
            k.op("pool", lambda e: e.tensor_copy(out=tb["hr"][:, 0:m_], in_=t["hr"][:, 0:m_]), r=[t["hr"]], w=[tb["hr"]])
            k.op("pool", lambda e: e.tensor_copy(out=tb["hi"][:, 0:m_], in_=t["hi"][:, 0:m_]), r=[t["hi"]], w=[tb["hi"]])
            k.op("pool", lambda e: e.tensor_copy(out=hr_[:, i:i + 1], in_=t["hr"][:, m_ - 1:m_]), r=[t["hr"]], w=[hr_])
            k.op("pool", lambda e: e.tensor_copy(out=hi_[:, i:i + 1], in_=t["hi"][:, m_ - 1:m_]), r=[t["hi"]], w=[hi_])
            yield
            py = pY[hf]
            co = 64 * hf
            if j < 2:
                k.op("pe", lambda e: e.matmul(py[32 * j:32 * j + 32, co:co + m_], lhsT=o["CTr"][:, i, :], rhs=tb["hr"][:, 0:m_], start=True, stop=False), r=[o["CTr"], tb["hr"]], w=[py])
                k.op("pe", lambda e: e.matmul(py[32 * j:32 * j + 32, co:co + m_], lhsT=o["CTi"][:, i, :], rhs=tb["hi"][:, 0:m_], start=False, stop=True), r=[o["CTi"], tb["hi"]], w=[py])
            else:
                k.op("pe", lambda e: e.matmul(py[64:128, co:co + m_], lhsT=o["CTpr"][:, hf, j - 2, :], rhs=tb["hr"][:, 0:m_], start=(j == 2), stop=False), r=[o["CTpr"], tb["hr"]], w=[py])
                k.op("pe", lambda e: e.matmul(py[64:128, co:co + m_], lhsT=o["CTpi"][:, hf, j - 2, :], rhs=tb["hi"][:, 0:m_], start=False, stop=(j == 3)), r=[o["CTpi"], tb["hi"]], w=[py])
            yield
        for grp in ((0, 1, 2), (3, 4, 5), (6, 7)):
            tg = [tile_gen(i, S5sets[gi], S5bsets[gi]) for gi, i in enumerate(grp)]
            while tg:
                for g_ in list(tg):
                    try:
                        next(g_)
                    except StopIteration:
                        tg.remove(g_)
                yield
        for hf in range(2):
            y, t_ = ysb[hf], yt[hf]
            k.op("dve", lambda e: e.scalar_tensor_tensor(out=y[:, 0:m_], in0=uT[:, hf, c0:c0 + m_], scalar=o["dcol"][:, hf:hf + 1], in1=pY[hf][:, 64 * hf:64 * hf + m_],
                                                         op0=ALU.mult, op1=ALU.add), r=[uTb, o["dcol"], pY[hf]], w=[y])
            k.op("pool", lambda e: e.tensor_tensor(out=t_[:, 0:m_], in0=y[:, 0:m_], in1=y[:, 0:m_], op=ALU.mult), r=[y], w=[t_])
            k.op("dve", lambda e: e.tensor_scalar(out=t_[:, 0:m_], in0=t_[:, 0:m_], scalar1=0.044715, scalar2=1.0, op0=ALU.mult, op1=ALU.add), w=[t_])
            k.op("pool", lambda e: e.tensor_tensor(out=t_[:, 0:m_], in0=t_[:, 0:m_], in1=y[:, 0:m_], op=ALU.mult), r=[y], w=[t_])
            k.op("act", lambda e: e.activation(out=t_[:, 0:m_], in_=t_[:, 0:m_], func=AF.Sigmoid, scale=1.5957691216057308), w=[t_])
            k.op("dve", lambda e: e.tensor_tensor(out=zT[:, hf, c0:c0 + m_], in0=y[:, 0:m_], in1=t_[:, 0:m_], op=ALU.mult), r=[y, t_], w=[zT])

    def block_layer(l, blk, tiles, segs, NT, is_sample, emit):
        o = L[l]
        k.fence([hTb] + wstb, mixer_bufs)
        k.fence(s5_views, xTb)
        k.fence(dn2_views, [kvst[0], kvst[1]])
        k.fence(s5c_views, [ctmp[0], ctm2[0]])
        transposes_to_xT(tiles)
        xr_ = list(xTb)
        chk(2)
        WIN = [(0, 512), (512, 512), (1024, 512), (1536, 512), (2048, 520), (2568, 256)]

        def type_a(slab, et, evac):
            p = k.psn()
            for kt in range(8):
                k.op("pe", lambda e, kt=kt: e.matmul(p[:, 0:NT], lhsT=slab[:, kt, et * 128:(et + 1) * 128], rhs=xT[:, kt, 0:NT],
                                                     start=(kt == 0), stop=(kt == 7)), r=[slab] + xr_, w=[p], sig=(kt == 7))
            evac(p)

        def type_b(slab, cb, W, ti, evac):
            c0, n = tiles[ti]
            p = k.psn()
            for kt in range(8):
                k.op("pe", lambda e, kt=kt: e.matmul(p[0:n, 0:W], lhsT=xT[:, kt, c0:c0 + n], rhs=slab[:, kt, cb:cb + W],
                                                     start=(kt == 0), stop=(kt == 7)), r=[slab, xTb[ti]], w=[p], sig=(kt == 7))
            evac(p)

        kbase = 0 if is_sample else ((4 * blk) % 8) * 128
        sA = load_slab(win_s.h[l, :, :, 0:512], 512, win_s, si)
        for et in range(4):
            type_a(sA, et, lambda p, et=et: copy(evac_eng(), qT[:, et, 0:NT], p[:, 0:NT], [p], [qT]))
        chk(2.1)
        sB = load_slab(win_s.h[l, :, :, 512:1024], 512, win_s, si)
        for et in range(4):
            if is_sample:
                type_a(sB, et, lambda p, et=et: copy(evac_eng(), o["kT"][:, et, 640:640 + NT], p[:, 0:NT], [p], [o["kT"]]))
            else:
                type_a(sB, et, lambda p, et=et: copy(evac_eng(), o["kT"][:, et, kbase:kbase + NT], p[:, 0:NT], [p], [o["kT"]]))
        if emit:
            for ti, (c0, n) in enumerate(tiles):
                def ev(p, ti=ti, c0=c0, n=n):
                    st_ = kvst[ti % 2]
                    copy(evac_eng(), st_[0:n, :], p[0:n, 0:512], [p], [st_])
                    dst = (o_ks.ap()[l, c0:c0 + n, :] if is_sample else o_kp.ap()[l, c0:c0 + n, :])
                    k.dma("sp", dst, st_[0:n, :], r=[st_])
                type_b(sB, 0, 512, ti, ev)
        chk(2.2)
        sC = load_slab(win_s.h[l, :, :, 1024:1536], 512, win_s, si)
        for ti, (c0, n) in enumerate(tiles):
            def ev(p, ti=ti, c0=c0, n=n):
                slot = (4 + ti) if is_sample else (4 * blk + ti) % 8
                vdst = o["vr"][0:n, slot, :].rearrange("p (h c) -> p h c", c=66)[:, :, 0:64]
                if DBG[0] != 1:
                    copy("act", vdst, p[0:n, 0:512].rearrange("p (h c) -> p h c", c=64), [p], [o["vr"]])
                if emit and DBG[0] != 2:
                    st_ = kvst[ti % 2]
                    copy("dve", st_[0:n, :], p[0:n, 0:512], [p], [st_])
                    dst = (o_vs.ap()[l, c0:c0 + n, :] if is_sample else o_vp.ap()[l, c0:c0 + n, :])
                    k.dma("sp", dst, st_[0:n, :], r=[st_])
            type_b(sC, 0, 512, ti, ev)
        chk(2.3)
        sD = load_slab(win_s.h[l, :, :, 1536:2048], 512, win_s, si)
        slabE = [None]

        def conv_tile(ct, p):
            cb_ = cbuf[ct % 2]; t1_, t2_ = ctmp[ct % 2], ctm2[ct % 2]
            off = 0
            for (c0, n, slot) in segs:
                k.op("pool", lambda e, off=off, slot=slot: e.tensor_copy(out=cb_[:, off:off + 3], in_=o["convh"][slot][:, ct, :]), r=[o["convh"][slot]], w=[cb_])
                copy(evac_eng(), cb_[:, off + 3:off + 3 + n], p[:, c0:c0 + n], [p], [cb_])
                off += n + 3
            off = 0
            for (c0, n, slot) in segs:
                k.op("dve", lambda e, off=off, c0=c0, n=n: e.tensor_scalar_mul(out=t1_[:, c0:c0 + n], in0=cb_[:, off:off + n], scalar1=o["convw"][:, ct, 0:1]), r=[cb_, o["convw"]], w=[t1_])
                for jj in range(1, 4):
                    k.op("dve", lambda e, off=off, c0=c0, n=n, jj=jj: e.scalar_tensor_tensor(
                        out=t1_[:, c0:c0 + n], in0=cb_[:, off + jj:off + jj + n], scalar=o["convw"][:, ct, jj:jj + 1], in1=t1_[:, c0:c0 + n],
                        op0=ALU.mult, op1=ALU.add), r=[cb_, o["convw"]], w=[t1_])
                k.op("pool", lambda e, off=off, n=n, slot=slot: e.tensor_copy(out=o["convh"][slot][:, ct, :], in_=cb_[:, off + n:off + n + 3]), r=[cb_], w=[o["convh"][slot]])
                off += n + 3
            if ct < 4:
                k.op("act", lambda e: e.activation(out=t1_[:, 0:NT], in_=t1_[:, 0:NT], func=AF.Silu, bias=o["convb"][:, ct:ct + 1], scale=1.0), r=[o["convb"]], w=[t1_])
                k.op("pool", lambda e: e.tensor_tensor(out=r32(sqr[:, 0:NT]), in0=t1_[:, 0:NT], in1=t1_[:, 0:NT], op=ALU.mult), r=[t1_], w=[sqr])
                pn = k.psn()
                k.op("pe", lambda e: e.matmul(pn[:, 0:NT], lhsT=r32(bones[:]), rhs=r32(sqr[:, 0:NT]), start=True, stop=True), r=[bones, sqr], w=[pn])
                k.op("act", lambda e: e.activation(out=t2_[:, 0:NT], in_=pn[:, 0:NT], func=AF.Sqrt, bias=RMS_EPS, scale=1.0), r=[pn], w=[t2_])
                k.op("dve", lambda e: e.reciprocal(out=t2_[:, 0:NT], in_=t2_[:, 0:NT]), w=[t2_])
                k.op("dve", lambda e: e.tensor_tensor(out=r32(qkvT[:, ct, 0:NT]), in0=t1_[:, 0:NT], in1=t2_[:, 0:NT], op=ALU.mult), r=[t1_, t2_], w=[qkvb[ct]])
            else:
                k.op("act", lambda e: e.activation(out=r32(qkvT[:, ct, 0:NT]), in_=t1_[:, 0:NT], func=AF.Silu, bias=o["convb"][:, ct:ct + 1], scale=1.0),
                     r=[t1_, o["convb"]], w=[qkvb[ct]])
            if ct >= 2:
                dstt, dstb = (ktok, ktb) if ct < 4 else (vtok, vtb)
                for ti, (c0, n) in enumerate(tiles):
                    pt = k.psn()
                    k.op("pe", lambda e, c0=c0, n=n: e.transpose(pt[0:n, 0:128], qkvT[:, ct, c0:c0 + n], ident[:, :]), r=[qkvb[ct], ident], w=[pt])
                    copy(evac_eng(), r32(dstt[0:n, ti, (ct % 2) * 128:(ct % 2) * 128 + 128]), pt[0:n, 0:128], [pt], [dstb])

        for et in range(4):
            type_a(sD, et, lambda p, et=et: conv_tile(et, p))
        chk(2.4)
        sE = load_slab(win_s.h[l, :, :, 2048:2568], 520, win_s, si)
        for et in range(2):
            type_a(sE, et, lambda p, et=et: conv_tile(4 + et, p))
        for ti, (c0, n) in enumerate(tiles):
            type_b(sE, 256, 264, ti, lambda p, ti=ti, n=n: copy(evac_eng(), gtok[0:n, ti, :], p[0:n, 0:264], [p], [gtb[ti]]))
        chk(2.5)
        sF = load_slab(win_s.h[l, :, :, 2568:2824], 256, win_s, si)
        for et in range(2):
            type_a(sF, et, lambda p, et=et: copy(evac_eng(), r32(uT[:, et, 0:NT]), p[:, 0:NT], [p], [uTb]))

        chk(3)
        k.fence(xTb, s5_views)
        k.fence([ctmp[0], ctm2[0]], s5c_views)
        k.fence([kvst[0], kvst[1]], dn2_views)

        def gen_att():
            if not is_sample:
                for ti, (c0, n) in enumerate(tiles):
                    G = 4 * blk + ti
                    kts = []
                    for j in range(5):
                        gt_ = G - 4 + j
                        if gt_ >= 0:
                            kts.append((j, (gt_ % 8) * 128, 128, gt_ % 8))
                    yield from attention(l, c0, n, kts, ti)
            else:
                for s in range(NS):
                    c0, n = tiles[s]
                    for q2 in range(2):
                        k.dma("sp", lnpA[q2].rearrange("p (a c) -> p a c", c=512), dap(ck, (l * NS + s) * 262144 + q2 * 131072, [[512, 128], [65536, 2], [1, 512]]), w=[lnp[q2]])
                    for kt in range(4):
                        p = k.psn()
                        for hp in range(4):
                            k.op("pe", lambda e, kt=kt, hp=hp: e.transpose(p[:, hp * 128:(hp + 1) * 128], lnpA[kt // 2][:, (kt % 2) * 512 + hp * 128:(kt % 2) * 512 + (hp + 1) * 128], ident[:]),
                                 r=[lnp[kt // 2], ident], w=[p])
                        copy(evac_eng(), o["kT"][:, :, kt * 128:(kt + 1) * 128], p[:, :].rearrange("p (a c) -> p a c", c=128), [p], [o["kT"]])
                    for q2 in range(2):
                        k.dma("sp", lnpA[q2].rearrange("p (a c) -> p a c", c=512), dap(cv, (l * NS + s) * 262144 + q2 * 131072, [[512, 128], [65536, 2], [1, 512]]), w=[lnp[q2]])
                    for kt in range(4):
                        copy(evac_eng(), o["vr"][:, kt, :].rearrange("p (h c) -> p h c", c=66)[:, :, 0:64],
                             lnpA[kt // 2][:, (kt % 2) * 512:(kt % 2) * 512 + 512].rearrange("p (h c) -> p h c", c=64), [lnp[kt // 2]], [o["vr"]])
                    kts = [(j, j * 128, 128, j) for j in range(4)] + [(4, 640 + c0, n, 4 + s)]
                    yield from attention(l, c0, n, kts, s)


        def gen_dn():
            for ti, (c0, n) in enumerate(tiles):
                yield from deltanet_tile(l, ti, c0, n, segs[ti][2] if is_sample else 0)

        def gen_s5():
            for (c0, n, slot) in segs:
                for s0 in range(0, n, 64):
                    yield from s5_sub(l, c0 + s0, min(64, n - s0), slot, s0 == 0)
            k.fence(s5c_views, [ctmp[0], ctm2[0]])
            for hf2 in range(2):
                p = k.psn()
                for hf in range(2):
                    k.op("pe", lambda e, hf=hf: e.matmul(p[:, 0:NT], lhsT=o["gluw"][:, hf, hf2 * 128:(hf2 + 1) * 128], rhs=zT[:, hf, 0:NT], start=(hf == 0), stop=(hf == 1)),
                         r=[o["gluw"], zT], w=[p])
                f_ = fup[hf2]
                k.op("act", lambda e: e.activation(out=f_[:, 0:NT], in_=p[:, 0:NT], func=AF.Sigmoid, bias=o["glub"][:, hf2:hf2 + 1], scale=1.0), r=[p, o["glub"]], w=[f_])
                k.op("dve", lambda e: e.tensor_tensor(out=mixT[:, 6 + hf2, 0:NT], in0=zT[:, hf2, 0:NT], in1=f_[:, 0:NT], op=ALU.mult), r=[zT, f_], w=[mixb[6 + hf2]])


        gens = [(gen_dn(), 2), (gen_s5(), 4), (gen_att(), 1)]
        if STOP[0] <= 5 or SERIAL[0]:
            for g_, _w in (gens[2], gens[0], gens[1]):
                for _ in g_:
                    pass
                chk(4 if g_ is gens[2][0] else (5 if g_ is gens[0][0] else 6))
        else:
            active = list(gens)
            while active:
                for item in list(active):
                    g_, w_ = item
                    for _ in range(w_):
                        try:
                            next(g_)
                        except StopIteration:
                            active.remove(item)
                            break
        chk(6)
        for half in range(2):
            sO = load_slab(wout_s.h[l, :, :, half * 512:(half + 1) * 512], 512, wout_s, si)
            for ti, (c0, n) in enumerate(tiles):
                p = k.psn()
                for kt in range(8):
                    k.op("pe", lambda e, kt=kt: e.matmul(p[0:n, 0:512], lhsT=mixT[:, kt, c0:c0 + n], rhs=sO[:, kt, 0:512], start=(kt == 0), stop=(kt == 7)),
                         r=[sO] + mixb, w=[p], sig=(kt == 7))
                k.op("dve", lambda e, ti=ti, n=n: e.scalar_tensor_tensor(out=xtok[ti][0:n, half * 512:(half + 1) * 512], in0=xtok[ti][0:n, half * 512:(half + 1) * 512],
                                                                       scalar=ALPHA, in1=p[0:n, 0:512], op0=ALU.mult, op1=ALU.add), r=[p], w=[xtok[ti]])
        for ti, (c0, n) in enumerate(tiles):
            layer_norm(ti, n, ln1g, ln1b, l)
        k.fence(s5_views, xTb)
        transposes_to_xT(tiles)
        chk(7)
        k.fence(mixer_bufs, [hTb])
        for fs in range(8):
            sU = load_slab(wup_s.h[l, :, :, fs * 512:(fs + 1) * 512], 512, wup_s, si)
            for q in range(4):
                ft = fs * 4 + q
                p = k.psn()
                for kt in range(8):
                    k.op("pe", lambda e, kt=kt: e.matmul(p[:, 0:NT], lhsT=sU[:, kt, q * 128:(q + 1) * 128], rhs=xT[:, kt, 0:NT], start=(kt == 0), stop=(kt == 7)),
                         r=[sU] + list(xTb), w=[p], sig=(kt == 7))
                f_ = fup[ft % 2]
                k.op("act", lambda e, ft=ft: e.activation(out=f_[:, 0:NT], in_=p[:, 0:NT], func=AF.Relu, bias=o["bup"][:, ft:ft + 1], scale=1.0), r=[p, o["bup"]], w=[f_])
                k.op("pool" if ft % 2 else "dve", lambda e, ft=ft: e.tensor_tensor(out=hT[:, ft, 0:NT], in0=f_[:, 0:NT], in1=f_[:, 0:NT], op=ALU.mult), r=[f_], w=[hTb])
        for half in range(2):
            accs = [k.psn() for _ in tiles]
            for fs in range(4):
                sDn = load_slab(wdn_s.h[l, :, fs * 8:(fs + 1) * 8, half * 512:(half + 1) * 512], 512, wdn_s, si)
                for ti, (c0, n) in enumerate(tiles):
                    for f8 in range(8):
                        k.op("pe", lambda e, f8=f8, ti=ti, c0=c0, n=n: e.matmul(accs[ti][0:n, 0:512], lhsT=hT[:, fs * 8 + f8, c0:c0 + n], rhs=sDn[:, f8, 0:512],
                                                                              start=(fs == 0 and f8 == 0), stop=(fs == 3 and f8 == 7)), r=[sDn, hTb], w=[accs[ti]], sig=(f8 == 7))
            for ti, (c0, n) in enumerate(tiles):
                k.op("dve", lambda e, ti=ti, n=n: e.scalar_tensor_tensor(out=xtok[ti][0:n, half * 512:(half + 1) * 512], in0=xtok[ti][0:n, half * 512:(half + 1) * 512],
                                                                       scalar=ALPHA, in1=accs[ti][0:n, 0:512], op0=ALU.mult, op1=ALU.add), r=[accs[ti]], w=[xtok[ti]])
        for ti, (c0, n) in enumerate(tiles):
            layer_norm(ti, n, ln2g, ln2b, l)

    def emit_states(l, is_sample):
        o = L[l]
        slots = range(1, NSLOT) if is_sample else [0]
        for s in slots:
            sq = s - 1
            if is_sample:
                d_conv = (o_convs, (l * NS + sq) * 2304)
                d_ssm = dap(o_ssms, (l * NS + sq) * 16384, [[64, 64], [4096, 4], [1, 64]])
                d_re = dap(o_cres, (l * NS + sq) * 1024, [[1, 128], [128, 8]])
                d_im = dap(o_cims, (l * NS + sq) * 1024, [[1, 128], [128, 8]])
            else:
                d_conv = (o_convp, l * 2304)
                d_ssm = dap(o_ssmp, l * 16384, [[64, 64], [4096, 4], [1, 64]])
                d_re = dap(o_crep, l * 1024, [[1, 128], [128, 8]])
                d_im = dap(o_cimp, l * 1024, [[1, 128], [128, 8]])
            for ct in range(6):
                k.dma("sp", dap(d_conv[0], d_conv[1] + ct * 128, [[1, 128], [768, 3]]), o["convh"][s][:, ct, :], r=[o["convh"][s]], allow_slow_non_contiguous=True)
            k.dma("sp", d_ssm, o["S"][s][0:64, :, :], r=[o["S"][s]] + o["Sb"][s])
            k.dma("sp", d_re, o["hr"][s][:], r=[o["hr"][s]], allow_slow_non_contiguous=True)
            k.dma("sp", d_im, o["hi"][s][:], r=[o["hi"][s]], allow_slow_non_contiguous=True)

    ptiles = [(i * 128, 128) for i in range(4)]

    def main_loop():
        for blk in range(NB):
            for ti in range(4):
                k.dma("sp", xtok[ti][:, :], xp.ap()[blk * 512 + ti * 128:blk * 512 + (ti + 1) * 128, :], w=[xtok[ti]])
            for l in range(DEPTH):
                block_layer(l, blk, ptiles, [(0, 512, 0)], 512, False, blk == NB - 1)
                if blk == NB - 1:
                    emit_states(l, False)
            for ti in range(4):
                k.dma("pool", yp.ap()[blk * 512 + ti * 128:blk * 512 + (ti + 1) * 128, :], xtok[ti][:, :], r=[xtok[ti]])
        chk(20)
        stiles = [(s * SL, SL) for s in range(NS)]
        for s in range(NS):
            k.dma("sp", xtok[s][0:SL, :], xs.ap()[s * SL:(s + 1) * SL, :], w=[xtok[s]])
        for l in range(DEPTH):
            block_layer(l, 0, stiles, [(s * SL, SL, 1 + s) for s in range(NS)], NSK, True, True)
            emit_states(l, True)
        for s in range(NS):
            k.dma("pool", ys.ap()[s * SL:(s + 1) * SL, :], xtok[s][0:SL, :], r=[xtok[s]])

    try:
        main_loop()
    except StopBuild:
        pass
    k.finish()
    return nc


_NC_CACHE = {}


def run(inputs, SEQ, DEPTH, NCORES=8):
    key = (SEQ, DEPTH)
    if key not in _NC_CACHE:
        _NC_CACHE[key] = build(SEQ, DEPTH)
    nc = _NC_CACHE[key]
    f = lambda a: np.ascontiguousarray(np.asarray(a, dtype=np.float32))
    I = {n: f(v) for n, v in inputs.items()}
    m = np.arange(768)
    idx = np.clip(639 - m, -256, 256) + 256
    fext = np.ascontiguousarray(I["a_rel_bias"][:, :, idx])
    shared = {
        "w_in": I["w_in"], "fext": fext, "arb": I["a_rel_bias"], "conv_w": I["b_conv_w"], "conv_b": I["b_conv_b"],
        "a_log": I["b_a_log"], "dt_bias": I["b_dt_bias"], "norm_w": I["b_norm_w"],
        "ca_re": I["c_a_re"].reshape(DEPTH, 1024), "ca_im": I["c_a_im"].reshape(DEPTH, 1024), "clogdt": I["c_log_dt"],
        "cb_re": I["c_b_re"].reshape(DEPTH, 1024, 16), "cb_im": I["c_b_im"].reshape(DEPTH, 1024, 16),
        "cc_re": I["c_c_re"].reshape(DEPTH, 256, 64), "cc_im": I["c_c_im"].reshape(DEPTH, 256, 64),
        "c_d": I["c_d"].reshape(DEPTH, 256), "glu_w": I["c_glu_w"], "glu_b": I["c_glu_b"],
        "w_out": I["w_out"], "ln1g": I["ln1_g"], "ln1b": I["ln1_b"], "w_up": I["w_up"], "b_up": I["b_up"],
        "w_dn": I["w_down"], "ln2g": I["ln2_g"], "ln2b": I["ln2_b"],
    }
    in_maps = []
    for c in range(NCORES):
        sl = slice(4 * c, 4 * c + 4)
        mp = dict(shared)
        mp["xp"] = I["x_prompt"][c]
        mp["xs"] = np.ascontiguousarray(I["x_sample"][sl].reshape(64, D))
        mp["ck"] = np.ascontiguousarray(I["cache_a_k"][:, sl].reshape(DEPTH, 4, 512, 512))
        mp["cv"] = np.ascontiguousarray(I["cache_a_v"][:, sl].reshape(DEPTH, 4, 512, 512))
        mp["sconv"] = np.ascontiguousarray(I["state_b_conv"][:, sl])
        mp["sssm"] = np.ascontiguousarray(I["state_b_ssm"][:, sl])
        mp["scre"] = np.ascontiguousarray(I["state_c_re"][:, sl].reshape(DEPTH, 4, 1024))
        mp["scim"] = np.ascontiguousarray(I["state_c_im"][:, sl].reshape(DEPTH, 4, 1024))
        in_maps.append(mp)
    res = run_bass_kernel_spmd(nc, in_maps, core_ids=list(range(NCORES)))
    R = res.results
    B = NCORES
    st = lambda n: np.stack([R[c][n] for c in range(B)])
    y_p = st("yp")
    y_s = st("ys").reshape(B * 4, 16, D)
    kp = np.stack([R[c]["o_kp"] for c in range(B)], axis=1).reshape(DEPTH, B, 512, 8, 64)
    vp = np.stack([R[c]["o_vp"] for c in range(B)], axis=1).reshape(DEPTH, B, 512, 8, 64)
    convp = np.stack([R[c]["o_convp"] for c in range(B)], axis=1)
    ssmp = np.stack([R[c]["o_ssmp"] for c in range(B)], axis=1)
    crep = np.stack([R[c]["o_crep"] for c in range(B)], axis=1).reshape(DEPTH, B, 16, 64)
    cimp = np.stack([R[c]["o_cimp"] for c in range(B)], axis=1).reshape(DEPTH, B, 16, 64)
    ks = np.concatenate([R[c]["o_ks"].reshape(DEPTH, 4, 16, 8, 64) for c in range(B)], axis=1)
    vs = np.concatenate([R[c]["o_vs"].reshape(DEPTH, 4, 16, 8, 64) for c in range(B)], axis=1)
    convs = np.concatenate([R[c]["o_convs"] for c in range(B)], axis=1)
    ssms = np.concatenate([R[c]["o_ssms"] for c in range(B)], axis=1)
    cres = np.concatenate([R[c]["o_cres"].reshape(DEPTH, 4, 16, 64) for c in range(B)], axis=1)
    cims = np.concatenate([R[c]["o_cims"].reshape(DEPTH, 4, 16, 64) for c in range(B)], axis=1)
    outs = (y_p, y_s, kp, vp, convp, ssmp, crep, cimp, ks, vs, convs, ssms, cres, cims)
    return tuple(np.ascontiguousarray(a.astype(np.float32)) for a in outs)


def kernel(**inputs):
    return run(inputs, 4096, 2)
```

```python
import math
import numpy as np
import concourse.bass as bass
import concourse.mybir as mybir
from concourse.bass_utils import run_bass_kernel_spmd

F32 = mybir.dt.float32
F32R = mybir.dt.float32r
BF16 = mybir.dt.bfloat16
ALU = mybir.AluOpType
AF = mybir.ActivationFunctionType
AX = mybir.AxisListType

D = 1024
INW = 2824
DFF = 4096
NEG = -30000.0
ALPHA = 4.0 ** 0.25
LN_EPS = 1e-5
RMS_EPS = 1e-6
PI = math.pi


class Buf:
    def __init__(self, name=""):
        self.w = {}
        self.r = {}
        self.dsem = None
        self.dcnt = 0
        self.name = name
        self.psum = False


class T(Buf):
    def __init__(self, h, name):
        super().__init__(name)
        self.h = h

    def __getitem__(self, k):
        return self.h[k]


class V(Buf):
    def __init__(self, ap, name):
        super().__init__(name)
        self.h = ap

    def __getitem__(self, k):
        return self.h[k]


class Eng:
    def __init__(self, name, e, sem):
        self.name = name
        self.e = e
        self.sem = sem
        self.cnt = 0
        self.known = {}


class KB:
    def __init__(self, nc):
        self.nc = nc
        self.E = {}
        for nm, e in [("pe", nc.tensor), ("act", nc.scalar), ("dve", nc.vector),
                      ("pool", nc.gpsimd), ("sp", nc.sync)]:
            self.E[nm] = Eng(nm, e, nc.alloc_semaphore("s_" + nm))
        self.sems = {}
        self.dma_bufs = []
        self.nps = 0
        self.ps = [self.psum("psb%d" % i) for i in range(8)]
        self.uid = 0

    def sb(self, name, shape, dt):
        return T(self.nc.alloc_sbuf_tensor(name, list(shape), dt), name)

    def psum(self, name):
        t = T(self.nc.alloc_psum_tensor(name, [128, 512], F32), name)
        t.psum = True
        return t

    def psn(self):
        p = self.ps[(0, 1, 2, 3, 7)[self.nps % 5]]
        self.nps += 1
        return p

    def _deps(self, E, r, w):
        deps = {}

        def add(d):
            for key, (sem, val) in d.items():
                if key not in deps or deps[key][1] < val:
                    deps[key] = (sem, val)
        for b in r:
            add(b.w)
            if b.psum:
                add({kk: vv for kk, vv in b.r.items() if kk != E.name})
        for b in w:
            add(b.w)
            add(b.r)
        for key, (sem, val) in deps.items():
            if E.name == "pe" and key == "pe" and val > E.cnt:
                continue
            if E.known.get(key, 0) < val:
                E.e.wait_ge(sem, val)
                E.known[key] = val

    def op(self, en, fn, r=(), w=(), sig=True):
        E = self.E[en]
        self._deps(E, r, w)
        ins = fn(E.e)
        key = E.name
        if sig:
            E.cnt += 1
            ins.then_inc(E.sem, 1)
            E.unsig = False
            st = (E.sem, E.cnt)
        else:
            assert en == "pe"
            E.unsig = True
            st = (E.sem, E.cnt + 1)
        for b in w:
            b.w = {key: st}
            b.r = {}
        for b in r:
            if b not in w:
                b.r[key] = st

    def dma(self, q, out, in_, r=(), w=(), own=None, part=False, **kw):
        E = self.E[q]
        self._deps(E, r, w)
        if own is None:
            own = w[0] if w else r[0]
        qt = "sw" if q == "pool" else "hw"
        if own.dsem is None:
            own.dsem = {}
        if qt not in own.dsem:
            self.uid += 1
            own.dsem[qt] = [self.nc.alloc_semaphore("d%d" % self.uid), 0]
            self.dma_bufs.append(own.dsem[qt])
        ent = own.dsem[qt]
        ins = E.e.dma_start(out=out, in_=in_, **kw)
        ent[1] += 16
        ins.then_inc(ent[0], 16)
        key = "d%d%s" % (id(own), qt)
        st = (ent[0], ent[1])
        for b in w:
            if part:
                b.w[key] = st
            else:
                b.w = {key: st}
                b.r = {}
        for b in r:
            b.r[key] = st

    def fence(self, srcs, dsts):
        for d in dsts:
            for s in srcs:
                for dd in (s.w, s.r):
                    for key, (sem, val) in dd.items():
                        if key not in d.r or d.r[key][1] < val:
                            d.r[key] = (sem, val)

    def finish(self):
        sp = self.E["sp"]
        assert not getattr(self.E["pe"], "unsig", False)
        for ent in self.dma_bufs:
            sp.e.wait_ge(ent[0], ent[1])
        for nm in ("pe", "act", "dve", "pool"):
            e = self.E[nm]
            if e.cnt:
                sp.e.wait_ge(e.sem, e.cnt)


def r32(ap):
    return ap.bitcast(F32R)


STOP = [99]
SERIAL = [0]
DBG = [0]


class StopBuild(Exception):
    pass


def chk(level):
    if STOP[0] <= level:
        raise StopBuild()


def build(SEQ, DEPTH, NS=4, SL=16):
    nc = bass.Bass("TRN2", target_bir_lowering=False)
    k = KB(nc)
    NB = SEQ // 512
    NSK = NS * SL

    def din(name, shape, dt=F32):
        return nc.dram_tensor(name, list(shape), dt, kind="ExternalInput")

    def dout(name, shape):
        return nc.dram_tensor(name, list(shape), F32, kind="ExternalOutput")

    def dscr(name, shape, dt):
        return T(nc.dram_tensor(name, list(shape), dt, kind="Internal"), name)

    xp = din("xp", [SEQ, D]); xs = din("xs", [NSK, D])
    ck = din("ck", [DEPTH, NS, 512, 512]); cv = din("cv", [DEPTH, NS, 512, 512])
    sconv = din("sconv", [DEPTH, NS, 3, 768]); sssm = din("sssm", [DEPTH, NS, 4, 64, 64])
    scre = din("scre", [DEPTH, NS, 1024]); scim = din("scim", [DEPTH, NS, 1024])
    w_in = din("w_in", [DEPTH, D, INW]); fext = din("fext", [DEPTH, 8, 768])
    arb = din("arb", [DEPTH, 8, 513])
    conv_w = din("conv_w", [DEPTH, 4, 768]); conv_b = din("conv_b", [DEPTH, 768])
    a_log = din("a_log", [DEPTH, 4]); dt_bias = din("dt_bias", [DEPTH, 4]); norm_w = din("norm_w", [DEPTH, 64])
    ca_re = din("ca_re", [DEPTH, 1024]); ca_im = din("ca_im", [DEPTH, 1024]); clogdt = din("clogdt", [DEPTH, 16])
    cb_re = din("cb_re", [DEPTH, 1024, 16]); cb_im = din("cb_im", [DEPTH, 1024, 16])
    cc_re = din("cc_re", [DEPTH, 256, 64]); cc_im = din("cc_im", [DEPTH, 256, 64])
    c_d = din("c_d", [DEPTH, 256]); glu_w = din("glu_w", [DEPTH, 256, 256]); glu_b = din("glu_b", [DEPTH, 256])
    w_out = din("w_out", [DEPTH, D, D]); ln1g = din("ln1g", [DEPTH, D]); ln1b = din("ln1b", [DEPTH, D])
    w_up = din("w_up", [DEPTH, D, DFF]); b_up = din("b_up", [DEPTH, DFF]); w_dn = din("w_dn", [DEPTH, DFF, D])
    ln2g = din("ln2g", [DEPTH, D]); ln2b = din("ln2b", [DEPTH, D])

    yp = dout("yp", [SEQ, D]); ys = dout("ys", [NSK, D])
    o_kp = dout("o_kp", [DEPTH, 512, 512]); o_vp = dout("o_vp", [DEPTH, 512, 512])
    o_convp = dout("o_convp", [DEPTH, 3, 768]); o_ssmp = dout("o_ssmp", [DEPTH, 4, 64, 64])
    o_crep = dout("o_crep", [DEPTH, 1024]); o_cimp = dout("o_cimp", [DEPTH, 1024])
    o_ks = dout("o_ks", [DEPTH, NSK, 512]); o_vs = dout("o_vs", [DEPTH, NSK, 512])
    o_convs = dout("o_convs", [DEPTH, NS, 3, 768]); o_ssms = dout("o_ssms", [DEPTH, NS, 4, 64, 64])
    o_cres = dout("o_cres", [DEPTH, NS, 1024]); o_cims = dout("o_cims", [DEPTH, NS, 1024])

    win_s = dscr("win_s", [DEPTH, 128, 8, INW], BF16)
    wout_s = dscr("wout_s", [DEPTH, 128, 8, D], BF16)
    wup_s = dscr("wup_s", [DEPTH, 128, 8, DFF], BF16)
    wdn_s = dscr("wdn_s", [DEPTH, 128, 32, D], BF16)

    def dap(t, off, pat):
        mx = off + sum(st * (c - 1) for st, c in pat)
        tot = 1
        for d_ in t.shape:
            tot *= d_
        assert 0 <= off and mx < tot, (t.name, off, pat, mx, tot)
        return bass.AP(t, off, [list(p) for p in pat])

    ident = k.sb("ident", [128, 128], F32); identb = k.sb("identb", [128, 128], BF16)
    ones = k.sb("ones", [128, 128], F32); zeros = k.sb("zeros", [128, 128], F32)
    U = k.sb("U", [128, 128], F32); NEG_SL = k.sb("NEG_SL", [128, 128], F32); NEG_UI = k.sb("NEG_UI", [128, 128], F32)
    mask0 = k.sb("mask0", [128, 128], F32); bones = k.sb("bones", [128, 128], F32)
    k.op("pool", lambda e: e.memset(ones[:], 1.0), w=[ones])
    k.op("pool", lambda e: e.memset(zeros[:], 0.0), w=[zeros])
    k.op("pool", lambda e: e.affine_select(out=ident[:], in_=ones[:], pattern=[[1, 128]], compare_op=ALU.is_equal,
                                           fill=0.0, base=0, channel_multiplier=-1), r=[ones], w=[ident])
    k.op("pool", lambda e: e.affine_select(out=U[:], in_=ones[:], pattern=[[1, 128]], compare_op=ALU.is_ge,
                                           fill=0.0, base=0, channel_multiplier=-1), r=[ones], w=[U])
    k.op("pool", lambda e: e.affine_select(out=NEG_UI[:], in_=zeros[:], pattern=[[1, 128]], compare_op=ALU.is_ge,
                                           fill=NEG, base=0, channel_multiplier=-1), r=[zeros], w=[NEG_UI])
    k.op("pool", lambda e: e.affine_select(out=NEG_SL[:], in_=zeros[:], pattern=[[-1, 128]], compare_op=ALU.is_gt,
                                           fill=NEG, base=0, channel_multiplier=1), r=[zeros], w=[NEG_SL])
    k.op("dve", lambda e: e.tensor_copy(out=identb[:], in_=ident[:]), r=[ident], w=[identb])
    k.op("pool", lambda e: e.memset(mask0[:], 0.0), w=[mask0])
    k.op("pool", lambda e: e.memset(mask0[0:64, 64:128], NEG), w=[mask0])
    k.op("pool", lambda e: e.tensor_copy(out=r32(bones[:]), in_=zeros[:]), r=[zeros], w=[bones])
    k.op("pool", lambda e: e.tensor_copy(out=r32(bones[0:64, 0:64]), in_=ones[0:64, 0:64]), r=[ones], w=[bones])
    k.op("pool", lambda e: e.tensor_copy(out=r32(bones[64:128, 64:128]), in_=ones[64:128, 64:128]), r=[ones], w=[bones])

    oa = k.sb("oa", [128, 8, 64], BF16)
    Et = V(oa.h[:, :, :].rearrange("p a b -> p (a b)").bitcast(F32)[:, 0:128], "Et")
    BDm = {}
    for G in (16, 32, 64):
        ng = 128 // G
        k.op("pool", lambda e: e.memset(Et[0:ng, :], 1.0), w=[Et])
        k.op("pool", lambda e, G=G, ng=ng: e.affine_select(out=Et[0:ng, :], in_=Et[0:ng, :], pattern=[[1, 128]], compare_op=ALU.is_ge,
                                                         fill=0.0, base=0, channel_multiplier=-G), w=[Et])
        k.op("pool", lambda e, G=G, ng=ng: e.affine_select(out=Et[0:ng, :], in_=Et[0:ng, :], pattern=[[-1, 128]], compare_op=ALU.is_ge,
                                                         fill=0.0, base=G - 1, channel_multiplier=G), w=[Et])
        pm_ = k.psn()
        k.op("pe", lambda e, ng=ng: e.matmul(pm_[:, 0:128], lhsT=Et[0:ng, :], rhs=Et[0:ng, :], start=True, stop=True), r=[Et], w=[pm_])
        BDm[G] = k.sb("BD%d" % G, [128, 128], F32)
        k.op("dve", lambda e, G=G: e.tensor_copy(out=BDm[G][:], in_=pm_[:, 0:128]), r=[pm_], w=[BDm[G]])
    BD16 = BDm[16]; OFF32 = BDm[32]; OFF64 = BDm[64]
    OFF128 = k.sb("OFF128", [128, 128], F32)
    k.op("dve", lambda e: e.tensor_tensor(out=OFF128[:], in0=ones[:], in1=BDm[64][:], op=ALU.subtract), r=[ones, BDm[64]], w=[OFF128])
    k.op("dve", lambda e: e.tensor_tensor(out=OFF64[:], in0=BDm[64][:], in1=BDm[32][:], op=ALU.subtract), r=[BDm[32]], w=[OFF64])
    k.op("dve", lambda e: e.tensor_tensor(out=OFF32[:], in0=BDm[32][:], in1=BDm[16][:], op=ALU.subtract), r=[BDm[16]], w=[OFF32])
    k.fence([Et], [oa])
    if STOP[0] <= -3:
        k.finish()
        return nc
    NSLOT = 1 + NS
    L = []
    tmpH = None
    for l in range(DEPTH):
        o = {}
        o["kT"] = k.sb("kT%d" % l, [128, 4, 1024], BF16)
        o["vr"] = k.sb("vr%d" % l, [128, 8, 528], BF16)
        o["biasM"] = k.sb("biasM%d" % l, [128, 8, 3, 128], BF16)
        o["cbias"] = k.sb("cbias%d" % l, [128, 8], F32)
        o["convw"] = k.sb("convw%d" % l, [128, 6, 4], F32)
        o["convb"] = k.sb("convb%d" % l, [128, 6], F32)
        o["negA"] = k.sb("negA%d" % l, [128, 4], F32)
        o["dtb"] = k.sb("dtb%d" % l, [128, 4], F32)
        o["normw"] = k.sb("normw%d" % l, [128, 64], F32)
        o["convh"] = [k.sb("convh%d_%d" % (l, s), [128, 6, 3], F32) for s in range(NSLOT)]
        o["S"] = [k.sb("S%d_%d" % (l, s), [128, 4, 64], F32) for s in range(NSLOT)]
        o["Sb"] = [[Buf("Sb%d_%d_%d" % (l, s, h_)) for h_ in range(4)] for s in range(NSLOT)]
        o["hr"] = [k.sb("hr%d_%d" % (l, s), [128, 8], F32) for s in range(NSLOT)]
        o["hi"] = [k.sb("hi%d_%d" % (l, s), [128, 8], F32) for s in range(NSLOT)]
        o["rotc"] = k.sb("rotc%d" % l, [128, 8, 64], F32)
        o["rots"] = k.sb("rots%d" % l, [128, 8, 64], F32)
        o["rmag"] = k.sb("rmag%d" % l, [128, 8], F32)
        o["c1"] = k.sb("c1_%d" % l, [128, 8], F32)
        o["s1"] = k.sb("s1_%d" % l, [128, 8], F32)
        o["BBr"] = k.sb("BBr%d" % l, [128, 4, 128], F32)
        o["BBi"] = k.sb("BBi%d" % l, [128, 4, 128], F32)
        o["CTr"] = k.sb("CTr%d" % l, [128, 8, 32], BF16)
        o["CTi"] = k.sb("CTi%d" % l, [128, 8, 32], BF16)
        o["CTpr"] = k.sb("CTpr%d" % l, [128, 2, 2, 64], BF16)
        o["CTpi"] = k.sb("CTpi%d" % l, [128, 2, 2, 64], BF16)
        o["dcol"] = k.sb("dcol%d" % l, [128, 2], F32)
        o["glub"] = k.sb("glub%d" % l, [128, 2], F32)
        o["gluw"] = k.sb("gluw%d" % l, [128, 2, 256], BF16)
        o["bup"] = k.sb("bup%d" % l, [128, 32], F32)
        L.append(o)

    xtok = [k.sb("xtok%d" % i, [128, 1024], F32) for i in range(4)]
    xT = k.sb("xT", [128, 8, 512], BF16)
    xTb = [Buf("xTb%d" % i) for i in range(4)]
    wsl = [k.sb("wsl%d" % i, [128, 8, 528], BF16) for i in range(2)]
    qT = k.sb("qT", [128, 4, 512], BF16)
    tmpH = k.sb("tmpH", [128, 3, 128], F32)
    rec = k.sb("rec", [128, 8], F32)
    arena = k.sb("arena", [128, 16384], BF16)
    AR = arena.h

    def av(b0, nbytes, dt):
        v = AR[:, b0 // 2:(b0 + nbytes) // 2]
        return v if dt == BF16 else v.bitcast(dt)
    hT = AR[:, :].rearrange("p (f t) -> p f t", t=512)
    mixT = av(0, 8192, BF16).rearrange("p (m t) -> p m t", t=512)
    ktok = av(8192, 4096, F32).rearrange("p (a c) -> p a c", c=256)
    vtok = av(12288, 4096, F32).rearrange("p (a c) -> p a c", c=256)
    gtok = V(av(16384, 4224, F32).rearrange("p (a c) -> p a c", c=264), "gtok")
    cbuf = [V(av(20608, 2112, F32), "cbuf")] * 2
    sc = [V(av(22720, 2048, F32), "sc")] * 2
    pT = [V(av(24768, 1280, BF16).rearrange("p (a c) -> p a c", c=128), "pT")] * 2
    sg = V(av(26048, 1024, F32), "sg")
    ob = V(av(27072, 1024, F32).rearrange("p (a c) -> p a c", c=64), "ob")
    obb = V(av(28096, 512, BF16), "obb")
    zT = V(av(28608, 2048, BF16).rearrange("p (a c) -> p a c", c=512), "zT")
    qkvT_t = k.sb("qkvT", [128, 6, 512], F32)
    qkvT = qkvT_t.h
    uT_t = k.sb("uT", [128, 2, 512], F32)
    uT = uT_t.h
    sqr = k.sb("sqr", [128, 512], F32)
    hTb = Buf("hT"); qkvb = [Buf("qkv%d" % i) for i in range(6)]; mixb = [Buf("mix%d" % i) for i in range(8)]
    ktb = Buf("ktok"); vtb = Buf("vtok"); uTb = Buf("uT")
    gtb = [Buf("gt%d" % i) for i in range(4)]
    mixer_bufs = mixb + [ktb, vtb, gtok, cbuf[0], sc[0], pT[0], sg, ob, obb, zT] + gtb
    ctmp = [k.sb("ctmp%d" % i, [128, 512], F32) for i in range(1)] * 2
    ctm2 = [k.sb("ctm2%d" % i, [128, 512], F32) for i in range(1)] * 2
    kvst = [k.sb("kvst%d" % i, [128, 512], F32) for i in range(2)]
    lnpA = [wsl[i].h[:, :, :].rearrange("p a b -> p (a b)").bitcast(F32)[:, 0:1024] for i in range(2)]
    lnp = wsl
    stats = k.sb("stats", [128, 2, 6], F32); mv = k.sb("mv", [128, 2], F32)
    rstd = k.sb("rstd", [128, 1], F32); nmr = k.sb("nmr", [128, 1], F32)
    small = {n: k.sb("sm_" + n, [128, 4], F32) for n in
             ["beta", "nbeta", "xa", "ea", "sp", "g", "gcs", "gln", "gl64", "egc", "ekl", "egl", "qs", "wsc", "ssq", "rs4"]}
    DM = [{n: k.sb("dm%d_%s" % (i, n), [128, 128], F32) for n in
           ["Gb", "NGb", "E2", "E1", "P", "PT", "X", "XT", "Pn", "PTn", "QKm", "Qa", "Qb"]} for i in range(1)]
    DV = [{n: k.sb("dv%d_%s" % (i, n), [128, 64], F32) for n in ["rU", "rW", "u", "vn", "t1"]} for i in range(1)]
    wTt = [k.sb("wTt%d" % i, [64, 128], F32) for i in range(2)]
    kd2s = [k.sb("kd2_%d" % i, [128, 128], F32) for i in range(2)]
    dm2 = {n: k.sb("dm1_%s" % n, [128, 128], F32) for n in ["P", "PT", "X", "XT", "Pn", "PTn", "QKm", "Qa", "Qb"]}
    for j_, n_ in enumerate(["Gb", "NGb", "E2", "E1"]):
        dm2[n_] = V(kvst[0].h[:, j_ * 128:(j_ + 1) * 128], "dm1_" + n_)
    DM.append(dm2)
    dv2 = {n: k.sb("dv1_%s" % n, [128, 64], F32) for n in ["rU", "rW", "vn"]}
    dv2["u"] = V(kvst[1].h[:, 0:64], "dv1_u"); dv2["t1"] = V(kvst[1].h[:, 64:128], "dv1_t1")
    DV.append(dv2)
    dn2_views = [dm2[n_] for n_ in ["Gb", "NGb", "E2", "E1"]] + [dv2["u"], dv2["t1"]]
    _n12 = ["xr", "xi", "t1", "t2", "t3", "t4", "zr", "zi", "wr", "wi", "hr", "hi"]
    S5c = {}
    for j_, n_ in enumerate(_n12):
        host = ctmp[0].h if j_ < 8 else ctm2[0].h
        jj_ = j_ if j_ < 8 else j_ - 8
        S5c[n_] = V(host[:, jj_ * 64:(jj_ + 1) * 64], "s5c_" + n_)
    S5cb = {n_: V(ctm2[0].h[:, 256 + j_ * 32:256 + (j_ + 1) * 32].bitcast(BF16), "s5cb_" + n_) for j_, n_ in enumerate(["hr", "hi"])}
    s5c_views = list(S5c.values()) + list(S5cb.values())
    S5 = [S5c, S5c]
    S5b = [S5cb, S5cb]
    xTf = xT.h[:, :, :].rearrange("p a b -> p (a b)").bitcast(F32)
    S5sets = [S5[0]]; S5bsets = [S5b[0]]; s5_views = []
    for si_ in range(2):
        base_ = si_ * 832
        d1 = {n_: V(xTf[:, base_ + j_ * 64:base_ + (j_ + 1) * 64], "s5v%d_%s" % (si_, n_)) for j_, n_ in
              enumerate(["xr", "xi", "t1", "t2", "t3", "t4", "zr", "zi", "wr", "wi", "hr", "hi"])}
        d2 = {n_: V(xTf[:, base_ + 768 + j_ * 32:base_ + 768 + (j_ + 1) * 32].bitcast(BF16), "s5vb%d_%s" % (si_, n_)) for j_, n_ in enumerate(["hr", "hi"])}
        S5sets.append(d1); S5bsets.append(d2)
        s5_views += list(d1.values()) + list(d2.values())
    w0r = k.sb("w0r", [128, 8], F32); w0i = k.sb("w0i", [128, 8], F32); w0t = k.sb("w0t", [128, 8], F32)
    ysb = [k.sb("ysb%d" % i, [128, 64], F32) for i in range(2)]
    yt = [k.sb("yt%d" % i, [128, 64], F32) for i in range(2)]
    fup = [ctmp[0], ctm2[0]]
    wst = [AR[:, i * 4096:(i + 1) * 4096].bitcast(F32) for i in range(2)]
    wstb = [Buf("wst%d" % i) for i in range(2)]
    wsb = [xtok[i].h[:, :].bitcast(BF16) for i in range(2)]
    print("SBUF bytes remaining:", nc.sbuf_bytes_remaining)

    SH = {}
    for i_, n_ in enumerate(["bre", "bim", "bbr", "bbi"]):
        SH[n_] = V(xT.h[:, 0:4, :].rearrange("p a b -> p (a b)").bitcast(F32)[:, i_ * 128:(i_ + 1) * 128].rearrange("p (a b) -> p a b", b=16), "sh_" + n_)
    SH["bt_"] = V(ctm2[0].h[:, 384:512].rearrange("p (a b) -> p a b", b=16), "sh_bt")
    SH["ra"] = V(ctmp[0].h[:, 0:256].rearrange("p (a b) -> p a b", b=32), "sh_ra")
    SH["rb"] = V(ctmp[0].h[:, 256:512].rearrange("p (a b) -> p a b", b=32), "sh_rb")
    SH["PTm"] = V(ctm2[0].h[:, 0:128], "sh_PTm")
    SH["Cd"] = V(ctm2[0].h[:, 128:384].rearrange("p (a b) -> p a b", b=128), "sh_Cd")
    oaf = oa.h[:, :, :].rearrange("p a b -> p (a b)").bitcast(F32)
    sh_small = []
    for j_, n_ in enumerate(["are", "aim", "dtr", "th", "lre", "lim", "cfr", "cfi", "t80", "t81", "t82", "t83"]):
        SH[n_] = V(oaf[:, 128 + 8 * j_:128 + 8 * (j_ + 1)], "sh_" + n_)
        sh_small.append(SH[n_])
    SH["ti32"] = V(oaf[:, 128 + 96:128 + 104].bitcast(mybir.dt.int32), "sh_ti32")
    sh_small.append(SH["ti32"])
    cnt = {"ev": 0}

    def evac_eng():
        cnt["ev"] += 1
        return "act" if cnt["ev"] % 2 else "dve"

    def copy(en, out, in_, r, w):
        if en == "act":
            k.op("act", lambda e: e.activation(out=out, in_=in_, func=AF.Copy), r=r, w=w)
        else:
            k.op(en, lambda e: e.tensor_copy(out=out, in_=in_), r=r, w=w)

    for l in range(DEPTH):
        o = L[l]
        for h in range(8):
            k.dma("sp", tmpH[:], dap(fext, (l * 8 + h) * 768 + 256, [[1, 128], [128, 3], [1, 128]]), w=[tmpH])
            for jj in range(3):
                rev = bass.AP(tmpH.h, jj * 128 + 127, [list(tmpH[:].ap[0]), [-1, 128]])
                k.op("pool", lambda e, rev=rev, jj=jj, h=h: e.tensor_copy(out=o["biasM"][:, h, jj, :], in_=rev),
                     r=[tmpH], w=[o["biasM"]])
        if STOP[0] <= -2:
            k.finish()
            return nc
        k.op("pool", lambda e: e.memset(o["biasM"][64:128, :, 2, 0:64], NEG), w=[o["biasM"]])
        k.dma("sp", o["cbias"][:], dap(arb, l * 8 * 513 + 512, [[0, 128], [513, 8]]), w=[o["cbias"]],
              allow_slow_non_contiguous=True)
        k.op("pool", lambda e: e.memset(o["vr"][:], 1.0), w=[o["vr"]])
        for ct in range(6):
            k.dma("sp", o["convw"][:, ct, :], dap(conv_w, l * 4 * 768 + ct * 128, [[1, 128], [768, 4]]), w=[o["convw"]], part=(ct > 0),
                  allow_slow_non_contiguous=True)
        k.dma("sp", o["convb"][:], dap(conv_b, l * 768, [[1, 128], [128, 6]]), w=[o["convb"]],
              allow_slow_non_contiguous=True)
        k.dma("sp", o["negA"][:], dap(a_log, l * 4, [[0, 128], [1, 4]]), w=[o["negA"]])
        k.dma("sp", o["dtb"][:], dap(dt_bias, l * 4, [[0, 128], [1, 4]]), w=[o["dtb"]])
        k.dma("sp", o["normw"][:], dap(norm_w, l * 64, [[0, 128], [1, 64]]), w=[o["normw"]])
        k.op("act", lambda e: e.activation(out=o["negA"][:], in_=o["negA"][:], func=AF.Exp), w=[o["negA"]])
        k.op("act", lambda e: e.activation(out=o["negA"][:], in_=o["negA"][:], func=AF.Copy, scale=-1.0), w=[o["negA"]])
        for s in range(NSLOT):
            if s == 0:
                k.op("pool", lambda e: e.memset(o["convh"][0][:], 0.0), w=[o["convh"][0]])
                for h_ in range(4):
                    k.op("pool", lambda e, h_=h_: e.tensor_copy(out=r32(o["S"][0][:, h_, :]), in_=zeros[:, 0:64]), r=[zeros], w=[o["S"][0]])
                k.op("pool", lambda e: e.memset(o["hr"][0][:], 0.0), w=[o["hr"][0]])
                k.op("pool", lambda e: e.memset(o["hi"][0][:], 0.0), w=[o["hi"][0]])
            else:
                sq = s - 1
                for ct in range(6):
                    k.dma("sp", o["convh"][s][:, ct, :], dap(sconv, (l * NS + sq) * 2304 + ct * 128, [[1, 128], [768, 3]]),
                          w=[o["convh"][s]], part=(ct > 0), allow_slow_non_contiguous=True)
                st_ = kvst[sq % 2]
                for hh_ in range(2):
                    k.dma("sp", st_[64 * hh_:64 * hh_ + 64, 0:256].rearrange("p (h e) -> p h e", e=64),
                          dap(sssm, (l * NS + sq) * 16384, [[64, 64], [4096, 4], [1, 64]]), w=[st_], part=(hh_ == 1))
                k.op("dve", lambda e, st_=st_, s=s: e.tensor_copy(
                    out=r32(o["S"][s][:]), in_=st_[:, 0:256].rearrange("p (h e) -> p h e", e=64)),
                    r=[st_], w=[o["S"][s]])
                k.dma("sp", o["hr"][s][:], dap(scre, (l * NS + sq) * 1024, [[1, 128], [128, 8]]), w=[o["hr"][s]],
                      allow_slow_non_contiguous=True)
                k.dma("sp", o["hi"][s][:], dap(scim, (l * NS + sq) * 1024, [[1, 128], [128, 8]]), w=[o["hi"][s]],
                      allow_slow_non_contiguous=True)
        if STOP[0] <= -1:
            k.finish()
            return nc
        are, aim, dtr, th, lre, lim, cfr, cfi = [SH[n_] for n_ in ["are", "aim", "dtr", "th", "lre", "lim", "cfr", "cfi"]]
        t8 = [SH["t80"], SH["t81"], SH["t82"], SH["t83"]]
        k.dma("sp", are[:], dap(ca_re, l * 1024, [[1, 128], [128, 8]]), w=[are], allow_slow_non_contiguous=True)
        k.dma("sp", aim[:], dap(ca_im, l * 1024, [[1, 128], [128, 8]]), w=[aim], allow_slow_non_contiguous=True)
        for gg in range(2):
            k.dma("sp", dtr[64 * gg:64 * gg + 64, :], dap(clogdt, l * 16 + gg, [[0, 64], [2, 8]]), w=[dtr], part=(gg == 1),
                  allow_slow_non_contiguous=True)
        k.op("act", lambda e: e.activation(out=dtr[:], in_=dtr[:], func=AF.Exp), w=[dtr])
        k.op("dve", lambda e: e.tensor_tensor(out=th[:], in0=dtr[:], in1=aim[:], op=ALU.mult), r=[dtr, aim], w=[th])
        k.op("dve", lambda e: e.tensor_tensor(out=t8[0][:], in0=dtr[:], in1=are[:], op=ALU.mult), r=[dtr, are], w=[t8[0]])
        k.op("act", lambda e: e.activation(out=o["rmag"][:], in_=t8[0][:], func=AF.Exp), r=[t8[0]], w=[o["rmag"]])
        ti32 = SH["ti32"]
        for (dst_, shift) in ((t8[1], 0.0), (t8[2], 0.5 * PI)):
            k.op("dve", lambda e, dst_=dst_, shift=shift: e.tensor_scalar_add(out=dst_[:], in0=th[:], scalar1=shift), r=[th], w=[dst_])
            k.op("dve", lambda e, dst_=dst_: e.tensor_scalar_mul(out=ti32[:], in0=dst_[:], scalar1=1.0 / (2 * PI)), r=[dst_], w=[ti32])
            k.op("dve", lambda e: e.tensor_copy(out=t8[3][:], in_=ti32[:]), r=[ti32], w=[t8[3]])
            k.op("dve", lambda e, dst_=dst_: e.scalar_tensor_tensor(out=dst_[:], in0=t8[3][:], scalar=-2 * PI, in1=dst_[:], op0=ALU.mult, op1=ALU.add), r=[t8[3]], w=[dst_])
            k.op("dve", lambda e, dst_=dst_: e.tensor_scalar(out=t8[3][:], in0=dst_[:], scalar1=PI, scalar2=-2 * PI, op0=ALU.is_gt, op1=ALU.mult), r=[dst_], w=[t8[3]])
            k.op("dve", lambda e, dst_=dst_: e.tensor_tensor(out=dst_[:], in0=dst_[:], in1=t8[3][:], op=ALU.add), r=[t8[3]], w=[dst_])
            k.op("dve", lambda e, dst_=dst_: e.tensor_scalar(out=t8[3][:], in0=dst_[:], scalar1=-PI, scalar2=2 * PI, op0=ALU.is_lt, op1=ALU.mult), r=[dst_], w=[t8[3]])
            k.op("dve", lambda e, dst_=dst_: e.tensor_tensor(out=dst_[:], in0=dst_[:], in1=t8[3][:], op=ALU.add), r=[t8[3]], w=[dst_])
        k.op("act", lambda e: e.activation(out=o["s1"][:], in_=t8[1][:], func=AF.Sin), r=[t8[1]], w=[o["s1"]])
        k.op("act", lambda e: e.activation(out=o["c1"][:], in_=t8[2][:], func=AF.Sin), r=[t8[2]], w=[o["c1"]])
        k.op("dve", lambda e: e.tensor_tensor(out=lre[:], in0=o["rmag"][:], in1=o["c1"][:], op=ALU.mult), r=[o["rmag"], o["c1"]], w=[lre])
        k.op("dve", lambda e: e.tensor_tensor(out=lim[:], in0=o["rmag"][:], in1=o["s1"][:], op=ALU.mult), r=[o["rmag"], o["s1"]], w=[lim])
        rc, rs = o["rotc"], o["rots"]
        k.op("pool", lambda e: e.memset(rc[:, :, 0:1], 1.0), w=[rc])
        k.op("pool", lambda e: e.memset(rs[:, :, 0:1], 0.0), w=[rs])
        k.op("dve", lambda e: e.tensor_copy(out=rc[:, :, 1:2], in_=o["c1"][:].unsqueeze(2)), r=[o["c1"]], w=[rc])
        k.op("dve", lambda e: e.tensor_copy(out=rs[:, :, 1:2], in_=o["s1"][:].unsqueeze(2)), r=[o["s1"]], w=[rs])
        ra, rb = SH["ra"], SH["rb"]
        K_ = 1
        while K_ < 63:
            n2 = min(K_, 63 - K_)
            cK = rc[:, :, K_:K_ + 1].to_broadcast([128, 8, n2]); sK = rs[:, :, K_:K_ + 1].to_broadcast([128, 8, n2])
            cs_ = rc[:, :, 1:1 + n2]; ss_ = rs[:, :, 1:1 + n2]
            d0 = K_ + 1
            k.op("dve", lambda e: e.tensor_tensor(out=ra[:, :, 0:n2], in0=cs_, in1=cK, op=ALU.mult), r=[rc], w=[ra])
            k.op("dve", lambda e: e.tensor_tensor(out=rb[:, :, 0:n2], in0=ss_, in1=sK, op=ALU.mult), r=[rs], w=[rb])
            k.op("dve", lambda e: e.tensor_tensor(out=rc[:, :, d0:d0 + n2], in0=ra[:, :, 0:n2], in1=rb[:, :, 0:n2], op=ALU.subtract), r=[ra, rb], w=[rc])
            k.op("dve", lambda e: e.tensor_tensor(out=ra[:, :, 0:n2], in0=cs_, in1=sK, op=ALU.mult), r=[rc, rs], w=[ra])
            k.op("dve", lambda e: e.tensor_tensor(out=rb[:, :, 0:n2], in0=ss_, in1=cK, op=ALU.mult), r=[rs, rc], w=[rb])
            k.op("dve", lambda e: e.tensor_tensor(out=rs[:, :, d0:d0 + n2], in0=ra[:, :, 0:n2], in1=rb[:, :, 0:n2], op=ALU.add), r=[ra, rb], w=[rs])
            K_ += n2
        k.op("dve", lambda e: e.tensor_tensor(out=t8[0][:], in0=are[:], in1=are[:], op=ALU.mult), r=[are], w=[t8[0]])
        k.op("dve", lambda e: e.tensor_tensor(out=t8[1][:], in0=aim[:], in1=aim[:], op=ALU.mult), r=[aim], w=[t8[1]])
        k.op("dve", lambda e: e.tensor_tensor(out=t8[0][:], in0=t8[0][:], in1=t8[1][:], op=ALU.add), r=[t8[1]], w=[t8[0]])
        k.op("dve", lambda e: e.reciprocal(out=t8[0][:], in_=t8[0][:]), w=[t8[0]])
        k.op("dve", lambda e: e.tensor_scalar_add(out=t8[1][:], in0=lre[:], scalar1=-1.0), r=[lre], w=[t8[1]])
        k.op("dve", lambda e: e.tensor_tensor(out=t8[2][:], in0=t8[1][:], in1=are[:], op=ALU.mult), r=[t8[1], are], w=[t8[2]])
        k.op("dve", lambda e: e.tensor_tensor(out=t8[3][:], in0=lim[:], in1=aim[:], op=ALU.mult), r=[lim, aim], w=[t8[3]])
        k.op("dve", lambda e: e.tensor_tensor(out=t8[2][:], in0=t8[2][:], in1=t8[3][:], op=ALU.add), r=[t8[3]], w=[t8[2]])
        k.op("dve", lambda e: e.tensor_tensor(out=cfr[:], in0=t8[2][:], in1=t8[0][:], op=ALU.mult), r=[t8[2], t8[0]], w=[cfr])
        k.op("dve", lambda e: e.tensor_tensor(out=t8[2][:], in0=lim[:], in1=are[:], op=ALU.mult), r=[lim, are], w=[t8[2]])
        k.op("dve", lambda e: e.tensor_tensor(out=t8[3][:], in0=t8[1][:], in1=aim[:], op=ALU.mult), r=[t8[1], aim], w=[t8[3]])
        k.op("dve", lambda e: e.tensor_tensor(out=t8[2][:], in0=t8[2][:], in1=t8[3][:], op=ALU.subtract), r=[t8[3]], w=[t8[2]])
        k.op("dve", lambda e: e.tensor_tensor(out=cfi[:], in0=t8[2][:], in1=t8[0][:], op=ALU.mult), r=[t8[2], t8[0]], w=[cfi])
        bre, bim, bbr, bbi, bt_ = [SH[n_] for n_ in ["bre", "bim", "bbr", "bbi", "bt_"]]
        k.dma("sp", bre[:], dap(cb_re, l * 16384, [[16, 128], [2048, 8], [1, 16]]), w=[bre])
        k.dma("sp", bim[:], dap(cb_im, l * 16384, [[16, 128], [2048, 8], [1, 16]]), w=[bim])
        cfrb = cfr[:].unsqueeze(2).to_broadcast([128, 8, 16]); cfib = cfi[:].unsqueeze(2).to_broadcast([128, 8, 16])
        k.op("dve", lambda e: e.tensor_tensor(out=bbr[:], in0=bre[:], in1=cfrb, op=ALU.mult), r=[bre, cfr], w=[bbr])
        k.op("dve", lambda e: e.tensor_tensor(out=bt_[:], in0=bim[:], in1=cfib, op=ALU.mult), r=[bim, cfi], w=[bt_])
        k.op("dve", lambda e: e.tensor_tensor(out=bbr[:], in0=bbr[:], in1=bt_[:], op=ALU.subtract), r=[bt_], w=[bbr])
        k.op("dve", lambda e: e.tensor_tensor(out=bbi[:], in0=bim[:], in1=cfrb, op=ALU.mult), r=[bim, cfr], w=[bbi])
        k.op("dve", lambda e: e.tensor_tensor(out=bt_[:], in0=bre[:], in1=cfib, op=ALU.mult), r=[bre, cfi], w=[bt_])
        k.op("dve", lambda e: e.tensor_tensor(out=bbi[:], in0=bbi[:], in1=bt_[:], op=ALU.add), r=[bt_], w=[bbi])
        PTm = SH["PTm"]
        for (src, dst) in ((bbr, o["BBr"]), (bbi, o["BBi"])):
            for i in range(8):
                hf, j = i // 4, i % 4
                k.op("pool", lambda e: e.memset(PTm[:], 0.0), w=[PTm])
                k.op("pool", lambda e, src=src, i=i, j=j: e.tensor_copy(out=PTm[0:64, 32 * j:32 * j + 16], in_=src[0:64, i, :]), r=[src], w=[PTm])
                k.op("pool", lambda e, src=src, i=i, j=j: e.tensor_copy(out=PTm[64:128, 32 * j + 16:32 * j + 32], in_=src[64:128, i, :]), r=[src], w=[PTm])
                p = k.psn()
                k.op("pe", lambda e, p=p: e.transpose(p[:, 0:128], PTm[:], ident[:]), r=[PTm, ident], w=[p])
                if j < 3:
                    k.op("dve", lambda e, p=p, dst=dst, hf=hf, j=j: e.tensor_copy(
                        out=r32(dst[32 * j:32 * j + 32, hf, :]), in_=p[32 * j:32 * j + 32, 0:128]), r=[p], w=[dst])
                else:
                    k.op("dve", lambda e, p=p, dst=dst, hf=hf, j=j: e.tensor_copy(
                        out=r32(dst[64:128, 2 + hf, :]), in_=p[64:128, 0:128]), r=[p], w=[dst])
        Cd = SH["Cd"]
        for (srcd, dst, neg) in ((cc_re, o["CTr"], False), (cc_im, o["CTi"], True)):
            for dup in range(2):
                k.dma("sp", Cd[:, :, 64 * dup:64 * dup + 64], dap(srcd, l * 16384, [[64, 128], [8192, 2], [1, 64]]),
                      w=[Cd], part=(dup == 1))
            k.op("pool", lambda e, dst=dst: e.memset(dst[:], 0.0), w=[dst])
            for hf in range(2):
                p = k.psn()
                k.op("pe", lambda e, p=p, hf=hf: e.transpose(p[:, 0:128], Cd[:, hf, :], ident[:]), r=[Cd, ident], w=[p])
                pv = p[:, 0:128].rearrange("p (j c) -> p j c", c=32)
                if neg:
                    k.op("act", lambda e, pv=pv, dst=dst, hf=hf: e.activation(out=dst[0:64, 4 * hf:4 * hf + 4, 0:16], in_=pv[0:64, :, 0:16], func=AF.Copy, scale=-1.0), r=[p], w=[dst])
                    k.op("act", lambda e, pv=pv, dst=dst, hf=hf: e.activation(out=dst[64:128, 4 * hf:4 * hf + 4, 16:32], in_=pv[64:128, :, 16:32], func=AF.Copy, scale=-1.0), r=[p], w=[dst])
                else:
                    k.op("dve", lambda e, pv=pv, dst=dst, hf=hf: e.tensor_copy(out=dst[0:64, 4 * hf:4 * hf + 4, 0:16], in_=pv[0:64, :, 0:16]), r=[p], w=[dst])
                    k.op("dve", lambda e, pv=pv, dst=dst, hf=hf: e.tensor_copy(out=dst[64:128, 4 * hf:4 * hf + 4, 16:32], in_=pv[64:128, :, 16:32]), r=[p], w=[dst])
        for (cs_, cp_) in ((o["CTr"], o["CTpr"]), (o["CTi"], o["CTpi"])):
            k.op("pool", lambda e, cp_=cp_: e.memset(cp_[:], 0.0), w=[cp_])
            for hf in range(2):
                k.op("pool", lambda e, cs_=cs_, cp_=cp_, hf=hf: e.tensor_copy(out=cp_[:, hf, 0, 0:32], in_=cs_[:, 4 * hf + 2, :]), r=[cs_], w=[cp_])
                k.op("pool", lambda e, cs_=cs_, cp_=cp_, hf=hf: e.tensor_copy(out=cp_[:, hf, 1, 32:64], in_=cs_[:, 4 * hf + 3, :]), r=[cs_], w=[cp_])
        k.dma("sp", o["dcol"][:], dap(c_d, l * 256, [[1, 128], [128, 2]]), w=[o["dcol"]], allow_slow_non_contiguous=True)
        k.dma("sp", o["glub"][:], dap(glu_b, l * 256, [[1, 128], [128, 2]]), w=[o["glub"]], allow_slow_non_contiguous=True)
        k.dma("sp", o["bup"][:], dap(b_up, l * 4096, [[1, 128], [128, 32]]), w=[o["bup"]], allow_slow_non_contiguous=True)
        gst = kvst[0]
        k.dma("sp", gst[:, 0:512].rearrange("p (a c) -> p a c", c=256), dap(glu_w, l * 65536, [[256, 128], [32768, 2], [1, 256]]), w=[gst])
        k.op("dve", lambda e: e.tensor_copy(out=o["gluw"][:], in_=gst[:, 0:512].rearrange("p (a c) -> p a c", c=256)), r=[gst], w=[o["gluw"]])

    if STOP[0] <= 0:
        k.finish()
        return nc
    k.fence([SH[n_] for n_ in ["bre", "bim", "bbr", "bbi", "bt_", "ra", "rb", "PTm", "Cd"]], [ctmp[0], ctm2[0]] + xTb)
    k.fence(sh_small, [oa])
    ci = 0
    ceng = ["dve", "act", "pool"]
    for l in range(DEPTH):
        jobs = []
        for kt in range(8):
            for (c0, W) in ((0, 1412), (1412, 1412)):
                jobs.append((w_in, (l * D + kt * 128) * INW + c0, INW, W, win_s, win_s.h[l, :, kt, c0:c0 + W]))
        for kt in range(8):
            jobs.append((w_out, (l * D + kt * 128) * D, D, D, wout_s, wout_s.h[l, :, kt, :]))
        for kt in range(8):
            for c0 in (0, 2048):
                jobs.append((w_up, (l * D + kt * 128) * DFF + c0, DFF, 2048, wup_s, wup_s.h[l, :, kt, c0:c0 + 2048]))
        for ft in range(32):
            jobs.append((w_dn, (l * DFF + ft * 128) * D, D, D, wdn_s, wdn_s.h[l, :, ft, :]))
        for (src, off, RS, W, scr, dstap) in jobs:
            a, ab = wst[ci % 2], wstb[ci % 2]
            b, bb = wsb[ci % 2], xtok[ci % 2]
            k.dma("sp", a[:, 0:W], dap(src, off, [[RS, 128], [1, W]]), w=[ab])
            copy(ceng[ci % 3], b[:, 0:W], a[:, 0:W], [ab], [bb])
            k.dma("pool", dstap, b[:, 0:W], r=[bb], w=[scr], own=bb, part=True)
            ci += 1

    if STOP[0] <= 1:
        k.finish()
        return nc
    def load_slab(scr_ap, W, scr, si):
        s = wsl[si[0] % 2]
        si[0] += 1
        k.dma("sp", s[:, :, 0:W], scr_ap, r=[scr], w=[s])
        return s

    si = [0]

    def transposes_to_xT(tiles):
        for ti, (c0, n) in enumerate(tiles):
            for half in range(2):
                p = k.psn()
                for q in range(4):
                    kt = half * 4 + q
                    k.op("pe", lambda e, p=p, q=q, kt=kt, ti=ti, n=n: e.transpose(
                        p[:, q * 128:q * 128 + n], xtok[ti][0:n, kt * 128:(kt + 1) * 128], ident[0:n, 0:n]),
                        r=[xtok[ti], ident], w=[p], sig=(q == 3))
                en = evac_eng()
                copy(en, xT[:, half * 4:half * 4 + 4, c0:c0 + n],
                     p[:, :].rearrange("p (q c) -> p q c", c=128)[:, :, 0:n], [p], [xTb[ti]])

    def layer_norm(ti, n, gsrc, bsrc, l):
        x = xtok[ti]
        g_t, b_t = lnp[0], lnp[1]
        gA, bA = lnpA[0], lnpA[1]
        if ti == 0:
            k.dma("sp", gA, dap(gsrc, l * D, [[0, 128], [1, D]]), w=[g_t])
            k.dma("sp", bA, dap(bsrc, l * D, [[0, 128], [1, D]]), w=[b_t])
        for hh in range(2):
            k.op("dve", lambda e, hh=hh: e.bn_stats(out=stats[0:n, hh, :], in_=x[0:n, hh * 512:(hh + 1) * 512]), r=[x], w=[stats])
        k.op("dve", lambda e: e.bn_aggr(out=mv[0:n, :], in_=stats[0:n, :, :].rearrange("p a b -> p (a b)")), r=[stats], w=[mv])
        k.op("act", lambda e: e.activation(out=rstd[0:n, :], in_=mv[0:n, 1:2], func=AF.Sqrt, bias=LN_EPS, scale=1.0), r=[mv], w=[rstd])
        k.op("dve", lambda e: e.reciprocal(out=rstd[0:n, :], in_=rstd[0:n, :]), w=[rstd])
        k.op("dve", lambda e: e.scalar_tensor_tensor(out=nmr[0:n, :], in0=mv[0:n, 0:1], scalar=-1.0, in1=rstd[0:n, :],
                                                     op0=ALU.mult, op1=ALU.mult), r=[mv, rstd], w=[nmr])
        k.op("act", lambda e: e.activation(out=x[0:n, :], in_=x[0:n, :], func=AF.Identity, bias=nmr[0:n, :], scale=rstd[0:n, :]),
             r=[nmr, rstd], w=[x])
        k.op("dve", lambda e: e.tensor_tensor(out=x[0:n, :], in0=x[0:n, :], in1=gA[0:n, :], op=ALU.mult), r=[g_t], w=[x])
        k.op("pool", lambda e: e.tensor_tensor(out=x[0:n, :], in0=x[0:n, :], in1=bA[0:n, :], op=ALU.add), r=[b_t], w=[x])

    def attention(l, qcols, nq, ktiles, ti):
        o = L[l]
        q0 = qcols
        poA, poB = k.ps[4], k.ps[5]
        for h in range(8):
            hp, pb = h // 2, 64 * (h % 2)
            po = poA if h < 4 else poB
            pc = (h % 4) * 66
            psA, psB = k.psn(), k.psn()
            s_, p_ = sc[h % 2], pT[h % 2]
            for qi_, (j, kc, nk, vs_) in enumerate(ktiles):
                dst = psB[0:nk, 0:nq] if j == 0 else psA[0:nk, (j - 1) * 128:(j - 1) * 128 + nq]
                k.op("pe", lambda e, dst=dst, kc=kc, nk=nk: e.matmul(
                    dst, lhsT=o["kT"][pb:pb + 64, hp, kc:kc + nk], rhs=qT[pb:pb + 64, hp, q0:q0 + nq], start=True, stop=True),
                    r=[o["kT"], qT], w=[psB if j == 0 else psA], sig=(qi_ == len(ktiles) - 1))
            for (j, kc, nk, vs_) in ktiles:
                if j == 0:
                    k.op("dve", lambda e, nk=nk: e.scalar_tensor_tensor(out=s_[0:nk, 0:nq], in0=psB[0:nk, 0:nq], scalar=0.125,
                                                                       in1=mask0[0:nk, 0:nq], op0=ALU.mult, op1=ALU.add),
                         r=[psB, mask0], w=[s_])
                    k.op("act", lambda e, nk=nk: e.activation(out=p_[0:nk, 0, 0:nq], in_=s_[0:nk, 0:nq], func=AF.Exp,
                                                              bias=o["cbias"][0:nk, h:h + 1], scale=1.0), r=[s_, o["cbias"]], w=[p_])
                elif j == 1:
                    k.op("act", lambda e, nk=nk: e.activation(out=p_[0:nk, 1, 0:nq], in_=psA[0:nk, 0:nq], func=AF.Exp,
                                                              bias=o["cbias"][0:nk, h:h + 1], scale=0.125), r=[psA, o["cbias"]], w=[p_])
                else:
                    cs = (j - 1) * 128
                    k.op("dve", lambda e, nk=nk, cs=cs, j=j: e.scalar_tensor_tensor(
                        out=s_[0:nk, cs:cs + nq], in0=psA[0:nk, cs:cs + nq], scalar=0.125, in1=o["biasM"][0:nk, h, j - 2, 0:nq],
                        op0=ALU.mult, op1=ALU.add), r=[psA, o["biasM"]], w=[s_])
                    k.op("act", lambda e, nk=nk, cs=cs, j=j: e.activation(out=p_[0:nk, j, 0:nq], in_=s_[0:nk, cs:cs + nq], func=AF.Exp),
                         r=[s_], w=[p_])
            for idx, (j, kc, nk, vs_) in enumerate(ktiles):
                k.op("pe", lambda e, j=j, nk=nk, vs_=vs_, idx=idx: e.matmul(
                    po[0:nq, pc:pc + 65], lhsT=p_[0:nk, j, 0:nq], rhs=o["vr"][0:nk, vs_, h * 66:h * 66 + 65],
                    start=(idx == 0), stop=(idx == len(ktiles) - 1)), r=[p_, o["vr"]], w=[po], sig=(idx == len(ktiles) - 1))
            yield
        for half, po in enumerate((poA, poB)):
            pv = po[0:nq, 0:264].rearrange("p (h c) -> p h c", c=66)
            k.op("dve", lambda e, pv=pv, half=half: e.reciprocal(out=rec[0:nq, half * 4:half * 4 + 4], in_=pv[:, :, 64]), r=[po], w=[rec])
            k.op("dve", lambda e, pv=pv, half=half: e.tensor_tensor(
                out=oa[0:nq, half * 4:half * 4 + 4, :], in0=pv[:, :, 0:64],
                in1=rec[0:nq, half * 4:half * 4 + 4].unsqueeze(2).to_broadcast([nq, 4, 64]), op=ALU.mult), r=[po, rec], w=[oa])
        p = k.psn()
        pb_ = p[:, :].bitcast(BF16)
        for m in range(4):
            k.op("pe", lambda e, m=m: e.transpose(pb_[:, m * 128:m * 128 + nq],
                                                   oa[0:nq, 2 * m:2 * m + 2, :].rearrange("p a b -> p (a b)"), identb[0:nq, 0:nq]),
                 r=[oa, identb], w=[p], sig=(m == 3))
        copy(evac_eng(), mixT[:, 0:4, q0:q0 + nq], pb_[:, 0:512].rearrange("p (m c) -> p m c", c=128)[:, :, 0:nq], [p], [mixb[ti]])

    def deltanet_tile(l, ti, c0, n, slot):
        o = L[l]
        sm = small
        gt = gtok[0:n, ti, :]
        S = o["S"][slot]
        k.op("act", lambda e: e.activation(out=sm["beta"][0:n, :], in_=gtok[0:n, ti, 0:4], func=AF.Sigmoid), r=[gtb[ti]], w=[sm["beta"]])
        k.op("dve", lambda e: e.tensor_scalar_mul(out=sm["nbeta"][0:n, :], in0=sm["beta"][0:n, :], scalar1=-1.0), r=[sm["beta"]], w=[sm["nbeta"]])
        k.op("dve", lambda e: e.tensor_tensor(out=sm["xa"][0:n, :], in0=gtok[0:n, ti, 4:8], in1=o["dtb"][0:n, :], op=ALU.add), r=[gtb[ti], o["dtb"]], w=[sm["xa"]])
        k.op("act", lambda e: e.activation(out=sm["ea"][0:n, :], in_=sm["xa"][0:n, :], func=AF.Exp), r=[sm["xa"]], w=[sm["ea"]])
        k.op("act", lambda e: e.activation(out=sm["sp"][0:n, :], in_=sm["ea"][0:n, :], func=AF.Ln, bias=1.0, scale=1.0), r=[sm["ea"]], w=[sm["sp"]])
        k.op("dve", lambda e: e.tensor_tensor(out=sm["g"][0:n, :], in0=sm["sp"][0:n, :], in1=o["negA"][0:n, :], op=ALU.mult), r=[sm["sp"], o["negA"]], w=[sm["g"]])
        p = k.psn()
        k.op("pe", lambda e: e.matmul(p[0:n, 0:4], lhsT=U[0:n, 0:n], rhs=sm["g"][0:n, :], start=True, stop=True), r=[U, sm["g"]], w=[p])
        k.op("pe", lambda e: e.matmul(p[0:n, 4:8], lhsT=ones[0:n, 0:n], rhs=sm["g"][0:n, :], start=True, stop=True), r=[ones, sm["g"]], w=[p])
        k.op("pe", lambda e: e.matmul(p[0:128, 8:12], lhsT=ones[0:n, 0:128], rhs=sm["g"][0:n, :], start=True, stop=True), r=[ones, sm["g"]], w=[p])
        k.op("dve", lambda e: e.tensor_copy(out=sm["gcs"][0:n, :], in_=p[0:n, 0:4]), r=[p], w=[sm["gcs"]])
        k.op("act", lambda e: e.activation(out=sm["egc"][0:n, :], in_=p[0:n, 0:4], func=AF.Exp), r=[p], w=[sm["egc"]])
        k.op("act", lambda e: e.activation(out=sm["egl"][:, :], in_=p[:, 8:12], func=AF.Exp), r=[p], w=[sm["egl"]])
        k.op("dve", lambda e: e.tensor_tensor(out=sm["gln"][0:n, :], in0=p[0:n, 4:8], in1=sm["gcs"][0:n, :], op=ALU.subtract), r=[p, sm["gcs"]], w=[sm["gln"]])
        k.op("act", lambda e: e.activation(out=sm["ekl"][0:n, :], in_=sm["gln"][0:n, :], func=AF.Exp), r=[sm["gln"]], w=[sm["ekl"]])
        k.op("dve", lambda e: e.tensor_scalar_mul(out=sm["qs"][0:n, :], in0=sm["egc"][0:n, :], scalar1=0.125), r=[sm["egc"]], w=[sm["qs"]])
        k.op("dve", lambda e: e.tensor_tensor(out=sm["wsc"][0:n, :], in0=sm["egc"][0:n, :], in1=sm["beta"][0:n, :], op=ALU.mult), r=[sm["egc"], sm["beta"]], w=[sm["wsc"]])
        nlev = 6 if n > 64 else (5 if n > 32 else (4 if n > 16 else 3))
        def head_gen(h, m, dv, wT_, kd2):
            hp, pb = h // 2, 64 * (h % 2)
            Sb_ = o["Sb"][slot][h]
            kTh = r32(qkvT[pb:pb + 64, 2 + hp, c0:c0 + n]); qTh = r32(qkvT[pb:pb + 64, hp, c0:c0 + n])
            kb_, qb_ = qkvb[2 + hp], qkvb[hp]
            k.op("dve", lambda e: e.tensor_scalar_mul(out=m["Gb"][0:n, 0:n], in0=ones[0:n, 0:n], scalar1=sm["g"][0:n, h:h + 1]), r=[ones, sm["g"]], w=[m["Gb"]])
            k.op("act", lambda e: e.activation(out=m["NGb"][0:n, 0:n], in_=m["Gb"][0:n, 0:n], func=AF.Copy, scale=-1.0), r=[m["Gb"]], w=[m["NGb"]])
            pd = k.psn()
            k.op("pe", lambda e: e.matmul(pd[0:n, 0:n], lhsT=U[0:n, 0:n], rhs=m["Gb"][0:n, 0:n], start=True, stop=False), r=[U, m["Gb"]], w=[pd])
            k.op("pe", lambda e: e.matmul(pd[0:n, 0:n], lhsT=m["NGb"][0:n, 0:n], rhs=U[0:n, 0:n], start=False, stop=True), r=[U, m["NGb"]], w=[pd])
            k.op("pe", lambda e: e.matmul(pd[0:n, 128:128 + n], lhsT=kTh, rhs=kTh, start=True, stop=True), r=[kb_], w=[pd])
            k.op("pe", lambda e: e.matmul(pd[0:n, 256:256 + n], lhsT=kTh, rhs=qTh, start=True, stop=True), r=[kb_, qb_], w=[pd])
            k.op("dve", lambda e: e.tensor_tensor(out=m["E2"][0:n, 0:n], in0=pd[0:n, 0:n], in1=NEG_SL[0:n, 0:n], op=ALU.add), r=[pd, NEG_SL], w=[m["E2"]])
            k.op("act", lambda e: e.activation(out=m["E2"][0:n, 0:n], in_=m["E2"][0:n, 0:n], func=AF.Exp), w=[m["E2"]])
            k.op("dve", lambda e: e.scalar_tensor_tensor(out=m["E1"][0:n, 0:n], in0=pd[0:n, 0:n], scalar=-1.0, in1=NEG_UI[0:n, 0:n],
                                                         op0=ALU.mult, op1=ALU.add), r=[pd, NEG_UI], w=[m["E1"]])
            k.op("act", lambda e: e.activation(out=m["E1"][0:n, 0:n], in_=m["E1"][0:n, 0:n], func=AF.Exp), w=[m["E1"]])
            k.op("dve", lambda e: e.scalar_tensor_tensor(out=r32(m["PT"][0:n, 0:n]), in0=pd[0:n, 128:128 + n], scalar=sm["nbeta"][0:n, h:h + 1],
                                                         in1=m["E2"][0:n, 0:n], op0=ALU.mult, op1=ALU.mult), r=[pd, sm["nbeta"], m["E2"]], w=[m["PT"]])
            k.op("dve", lambda e: e.scalar_tensor_tensor(out=r32(m["QKm"][0:n, 0:n]), in0=pd[0:n, 256:256 + n], scalar=0.125,
                                                         in1=m["E1"][0:n, 0:n], op0=ALU.mult, op1=ALU.mult), r=[pd, m["E1"]], w=[m["QKm"]])
            yield
            pt = k.psn()
            k.op("pe", lambda e: e.transpose(pt[0:n, 0:n], m["PT"][0:n, 0:n], ident[0:n, 0:n]), r=[m["PT"], ident], w=[pt])
            k.op("act", lambda e: e.activation(out=r32(m["P"][0:n, 0:n]), in_=pt[0:n, 0:n], func=AF.Copy), r=[pt], w=[m["P"]])
            big = n > 16
            if big:
                k.op("dve", lambda e: e.tensor_tensor(out=r32(m["Qa"][0:n, 0:n]), in0=pt[0:n, 0:n], in1=BD16[0:n, 0:n], op=ALU.mult), r=[pt, BD16], w=[m["Qa"]])
                k.op("pool", lambda e: e.tensor_tensor(out=r32(m["Qb"][0:n, 0:n]), in0=m["PT"][0:n, 0:n], in1=BD16[0:n, 0:n], op=ALU.mult), r=[m["PT"], BD16], w=[m["Qb"]])
                P, PT_, Pn, PTn = "Qa", "Qb", "Pn", "PTn"
            else:
                P, PT_, Pn, PTn = "P", "PT", "Pn", "PTn"
            k.op("dve", lambda e: e.tensor_tensor(out=r32(m["X"][0:n, 0:n]), in0=m[P][0:n, 0:n], in1=ident[0:n, 0:n], op=ALU.add), r=[m[P], ident], w=[m["X"]])
            k.op("pool", lambda e: e.tensor_tensor(out=r32(m["XT"][0:n, 0:n]), in0=m[PT_][0:n, 0:n], in1=ident[0:n, 0:n], op=ALU.add), r=[m[PT_], ident], w=[m["XT"]])
            for lev in range(1, 4):
                last = (lev == 3) and not big
                lastp = lev == 3
                pp = k.psn()
                k.op("pe", lambda e, P=P, PT_=PT_: e.matmul(pp[0:n, 0:n], lhsT=r32(m[PT_][0:n, 0:n]), rhs=r32(m[P][0:n, 0:n]), start=True, stop=True),
                     r=[m[PT_], m[P]], w=[pp])
                if not lastp:
                    k.op("pe", lambda e, P=P, PT_=PT_: e.matmul(pp[0:n, 128:128 + n], lhsT=r32(m[P][0:n, 0:n]), rhs=r32(m[PT_][0:n, 0:n]), start=True, stop=True),
                         r=[m[PT_], m[P]], w=[pp])
                k.op("act", lambda e, Pn=Pn: e.activation(out=r32(m[Pn][0:n, 0:n]), in_=pp[0:n, 0:n], func=AF.Copy), r=[pp], w=[m[Pn]])
                if not lastp:
                    k.op("dve", lambda e, PTn=PTn: e.tensor_copy(out=r32(m[PTn][0:n, 0:n]), in_=pp[0:n, 128:128 + n]), r=[pp], w=[m[PTn]])
                px = k.psn()
                k.op("pe", lambda e, Pn=Pn: e.matmul(px[0:n, 0:n], lhsT=r32(m["XT"][0:n, 0:n]), rhs=r32(m[Pn][0:n, 0:n]), start=True, stop=True),
                     r=[m["XT"], m[Pn]], w=[px])
                if not last:
                    k.op("pe", lambda e, Pn=Pn: e.matmul(px[0:n, 128:128 + n], lhsT=r32(m[Pn][0:n, 0:n]), rhs=r32(m["XT"][0:n, 0:n]), start=True, stop=True),
                         r=[m["XT"], m[Pn]], w=[px])
                k.op("dve", lambda e: e.tensor_tensor(out=r32(m["X"][0:n, 0:n]), in0=m["X"][0:n, 0:n], in1=px[0:n, 0:n], op=ALU.add), r=[px], w=[m["X"]])
                if not last:
                    k.op("dve", lambda e: e.tensor_tensor(out=r32(m["XT"][0:n, 0:n]), in0=m["XT"][0:n, 0:n], in1=px[0:n, 128:128 + n], op=ALU.add), r=[px], w=[m["XT"]])
                P, Pn = Pn, P
                PT_, PTn = PTn, PT_
                yield
            if big:
                for li, OFF in enumerate((OFF32, OFF64, OFF128)):
                    lastl = li == 2
                    k.op("dve", lambda e, OFF=OFF: e.tensor_tensor(out=r32(m["Qa"][:, :]), in0=m["P"][:, :], in1=OFF[:, :], op=ALU.mult), r=[m["P"], OFF], w=[m["Qa"]])
                    k.op("pool", lambda e, OFF=OFF: e.tensor_tensor(out=r32(m["Qb"][:, :]), in0=m["PT"][:, :], in1=OFF[:, :], op=ALU.mult), r=[m["PT"], OFF], w=[m["Qb"]])
                    pw = k.psn()
                    k.op("pe", lambda e: e.matmul(pw[:, 0:128], lhsT=r32(m["Qb"][:, :]), rhs=r32(m["X"][:, :]), start=True, stop=True), r=[m["Qb"], m["X"]], w=[pw])
                    if not lastl:
                        k.op("pe", lambda e: e.matmul(pw[:, 128:256], lhsT=r32(m["Qa"][:, :]), rhs=r32(m["XT"][:, :]), start=True, stop=True), r=[m["Qa"], m["XT"]], w=[pw])
                    k.op("act", lambda e: e.activation(out=r32(m["Pn"][:, :]), in_=pw[:, 0:128], func=AF.Copy), r=[pw], w=[m["Pn"]])
                    if not lastl:
                        k.op("dve", lambda e: e.tensor_copy(out=r32(m["PTn"][:, :]), in_=pw[:, 128:256]), r=[pw], w=[m["PTn"]])
                    pz = k.psn()
                    k.op("pe", lambda e: e.matmul(pz[:, 0:128], lhsT=r32(m["XT"][:, :]), rhs=r32(m["Pn"][:, :]), start=True, stop=True), r=[m["XT"], m["Pn"]], w=[pz])
                    if not lastl:
                        k.op("pe", lambda e: e.matmul(pz[:, 128:256], lhsT=r32(m["X"][:, :]), rhs=r32(m["PTn"][:, :]), start=True, stop=True), r=[m["X"], m["PTn"]], w=[pz])
                    k.op("dve", lambda e: e.tensor_tensor(out=r32(m["X"][:, :]), in0=m["X"][:, :], in1=pz[:, 0:128], op=ALU.add), r=[pz], w=[m["X"]])
                    if not lastl:
                        k.op("dve", lambda e: e.tensor_tensor(out=r32(m["XT"][:, :]), in0=m["XT"][:, :], in1=pz[:, 128:256], op=ALU.add), r=[pz], w=[m["XT"]])
                    yield
            k.op("dve", lambda e: e.tensor_scalar_mul(out=r32(dv["rU"][0:n, :]), in0=vtok[0:n, ti, h * 64:(h + 1) * 64], scalar1=sm["beta"][0:n, h:h + 1]),
                 r=[vtb, sm["beta"]], w=[dv["rU"]])
            k.op("act", lambda e: e.activation(out=r32(dv["rW"][0:n, :]), in_=ktok[0:n, ti, h * 64:(h + 1) * 64], func=AF.Identity, scale=sm["wsc"][0:n, h:h + 1]),
                 r=[ktb, sm["wsc"]], w=[dv["rW"]])
            k.op("act", lambda e: e.activation(out=r32(kd2[0:n, 0:64]), in_=ktok[0:n, ti, h * 64:(h + 1) * 64], func=AF.Identity, scale=sm["ekl"][0:n, h:h + 1]),
                 r=[ktb, sm["ekl"]], w=[kd2])
            k.op("dve", lambda e: e.tensor_scalar_mul(out=r32(kd2[0:n, 64:128]), in0=ktok[0:n, ti, h * 64:(h + 1) * 64], scalar1=sm["ekl"][0:n, h:h + 1]),
                 r=[ktb, sm["ekl"]], w=[kd2])
            pu = k.psn()
            k.op("pe", lambda e: e.matmul(pu[0:n, 0:64], lhsT=r32(m["X"][0:n, 0:n]), rhs=r32(dv["rU"][0:n, :]), start=True, stop=True), r=[m["X"], dv["rU"]], w=[pu])
            k.op("pe", lambda e: e.matmul(pu[0:64, 128:128 + n], lhsT=r32(dv["rW"][0:n, :]), rhs=r32(m["X"][0:n, 0:n]), start=True, stop=True), r=[m["X"], dv["rW"]], w=[pu])
            k.op("act", lambda e: e.activation(out=dv["u"][0:n, :], in_=pu[0:n, 0:64], func=AF.Copy), r=[pu], w=[dv["u"]])
            k.op("dve", lambda e: e.tensor_copy(out=r32(wT_[0:64, 0:n]), in_=pu[0:64, 128:128 + n]), r=[pu], w=[wT_])
            yield
            Sh = r32(S[pb:pb + 64, h, :])
            Sh0 = r32(S[0:64, h, :])
            pv_ = k.psn()
            k.op("pe", lambda e: e.matmul(pv_[0:n, 0:64], lhsT=r32(wT_[0:64, 0:n]), rhs=Sh0, start=True, stop=True), r=[wT_, S, Sb_], w=[pv_])
            k.op("pe", lambda e: e.matmul(pv_[0:n, 64:128], lhsT=qTh, rhs=Sh, start=True, stop=True), r=[qb_, S, Sb_], w=[pv_])
            k.op("dve", lambda e: e.tensor_tensor(out=r32(dv["vn"][0:n, :]), in0=dv["u"][0:n, :], in1=pv_[0:n, 0:64], op=ALU.subtract), r=[dv["u"], pv_], w=[dv["vn"]])
            k.op("act", lambda e: e.activation(out=dv["t1"][0:n, :], in_=pv_[0:n, 64:128], func=AF.Identity, scale=sm["qs"][0:n, h:h + 1]), r=[pv_, sm["qs"]], w=[dv["t1"]])
            po_ = k.psn()
            k.op("pe", lambda e: e.matmul(po_[0:n, 0:64], lhsT=r32(m["QKm"][0:n, 0:n]), rhs=r32(dv["vn"][0:n, :]), start=True, stop=True), r=[m["QKm"], dv["vn"]], w=[po_])
            k.op("pe", lambda e: e.matmul(po_[0:128, 64:128], lhsT=r32(kd2[0:n, 0:128]), rhs=r32(dv["vn"][0:n, :]), start=True, stop=True), r=[kd2, dv["vn"]], w=[po_])
            k.op("dve", lambda e: e.tensor_tensor(out=ob[0:n, h, :], in0=dv["t1"][0:n, :], in1=po_[0:n, 0:64], op=ALU.add), r=[dv["t1"], po_], w=[ob])
            k.op("dve", lambda e: e.scalar_tensor_tensor(out=r32(S[:, h, :]), in0=S[:, h, :], scalar=sm["egl"][:, h:h + 1], in1=po_[:, 64:128],
                                                         op0=ALU.mult, op1=ALU.add), r=[sm["egl"], po_, S], w=[Sb_])
            yield
        for pair in ((0, 1), (2, 3)):
            hg = [head_gen(h_, DM[gi], DV[gi], wTt[gi], kd2s[gi]) for gi, h_ in enumerate(pair)]
            while hg:
                for g_ in list(hg):
                    try:
                        next(g_)
                    except StopIteration:
                        hg.remove(g_)
                yield
        obsq = sg.h[:, :].rearrange("p (a b) -> p a b", b=64)
        k.op("pool", lambda e: e.tensor_tensor(out=obsq[0:n, :, :], in0=ob[0:n, :, :], in1=ob[0:n, :, :], op=ALU.mult), r=[ob], w=[sg])
        k.op("dve", lambda e: e.tensor_reduce(out=sm["ssq"][0:n, :], in_=obsq[0:n, :, :], axis=AX.X, op=ALU.add), r=[sg], w=[sm["ssq"]])
        k.op("act", lambda e: e.activation(out=sm["rs4"][0:n, :], in_=sm["ssq"][0:n, :], func=AF.Sqrt, bias=RMS_EPS, scale=1.0 / 64), r=[sm["ssq"]], w=[sm["rs4"]])
        k.op("dve", lambda e: e.reciprocal(out=sm["rs4"][0:n, :], in_=sm["rs4"][0:n, :]), w=[sm["rs4"]])
        k.op("dve", lambda e: e.tensor_tensor(out=ob[0:n, :, :], in0=ob[0:n, :, :], in1=sm["rs4"][0:n, :].unsqueeze(2).to_broadcast([n, 4, 64]), op=ALU.mult),
             r=[sm["rs4"]], w=[ob])
        k.op("pool", lambda e: e.tensor_tensor(out=ob[0:n, :, :], in0=ob[0:n, :, :], in1=o["normw"][0:n, :].unsqueeze(1).to_broadcast([n, 4, 64]), op=ALU.mult),
             r=[o["normw"]], w=[ob])
        k.op("act", lambda e: e.activation(out=sg[0:n, :], in_=gtok[0:n, ti, 8:264], func=AF.Silu), r=[gtb[ti]], w=[sg])
        k.op("dve", lambda e: e.tensor_tensor(out=obb[0:n, :], in0=ob[0:n, :, :].rearrange("p a b -> p (a b)"), in1=sg[0:n, :], op=ALU.mult), r=[ob, sg], w=[obb])
        p = k.psn()
        pbf = p[:, :].bitcast(BF16)
        for mm in range(2):
            k.op("pe", lambda e, mm=mm: e.transpose(pbf[:, mm * 128:mm * 128 + n], obb[0:n, mm * 128:(mm + 1) * 128], identb[0:n, 0:n]), r=[obb, identb], w=[p])
        copy(evac_eng(), mixT[:, 4:6, c0:c0 + n], pbf[:, 0:256].rearrange("p (m c) -> p m c", c=128)[:, :, 0:n], [p], [mixb[4 + ti]])

    def s5_sub(l, c0, m_, slot, first_in_seg):
        o = L[l]
        hr_, hi_ = o["hr"][slot], o["hi"][slot]
        rc, rs = o["rotc"], o["rots"]
        k.op("dve", lambda e: e.tensor_tensor(out=w0r[:], in0=o["c1"][:], in1=hr_[:], op=ALU.mult), r=[o["c1"], hr_], w=[w0r])
        k.op("dve", lambda e: e.tensor_tensor(out=w0t[:], in0=o["s1"][:], in1=hi_[:], op=ALU.mult), r=[o["s1"], hi_], w=[w0t])
        k.op("dve", lambda e: e.tensor_tensor(out=w0r[:], in0=w0r[:], in1=w0t[:], op=ALU.subtract), r=[w0t], w=[w0r])
        k.op("dve", lambda e: e.tensor_tensor(out=w0i[:], in0=o["c1"][:], in1=hi_[:], op=ALU.mult), r=[o["c1"], hi_], w=[w0i])
        k.op("dve", lambda e: e.tensor_tensor(out=w0t[:], in0=o["s1"][:], in1=hr_[:], op=ALU.mult), r=[o["s1"], hr_], w=[w0t])
        k.op("dve", lambda e: e.tensor_tensor(out=w0i[:], in0=w0i[:], in1=w0t[:], op=ALU.add), r=[w0t], w=[w0i])
        pY = [k.ps[6], k.ps[6]]
        def tile_gen(i, t, tb):
            hf, j = i // 4, i % 4
            px = k.psn()
            if j < 3:
                urh = r32(uT[32 * j:32 * j + 32, hf, c0:c0 + m_])
                lr_, li_ = r32(o["BBr"][32 * j:32 * j + 32, hf, :]), r32(o["BBi"][32 * j:32 * j + 32, hf, :])
            else:
                urh = r32(uT[64:128, hf, c0:c0 + m_])
                lr_, li_ = r32(o["BBr"][64:128, 2 + hf, :]), r32(o["BBi"][64:128, 2 + hf, :])
            k.op("pe", lambda e: e.matmul(px[:, 0:m_], lhsT=lr_, rhs=urh, start=True, stop=True), r=[o["BBr"], uTb], w=[px])
            k.op("pe", lambda e: e.matmul(px[:, 128:128 + m_], lhsT=li_, rhs=urh, start=True, stop=True), r=[o["BBi"], uTb], w=[px])
            k.op("act", lambda e: e.activation(out=t["xr"][:, 0:m_], in_=px[:, 0:m_], func=AF.Copy), r=[px], w=[t["xr"]])
            k.op("act", lambda e: e.activation(out=t["xi"][:, 0:m_], in_=px[:, 128:128 + m_], func=AF.Copy), r=[px], w=[t["xi"]])
            cc, ss = rc[:, i, 0:m_], rs[:, i, 0:m_]
            yield
            k.op("pool", lambda e: e.tensor_tensor(out=t["t1"][:, 0:m_], in0=t["xr"][:, 0:m_], in1=cc, op=ALU.mult), r=[t["xr"], rc], w=[t["t1"]])
            k.op("pool", lambda e: e.tensor_tensor(out=t["t2"][:, 0:m_], in0=t["xi"][:, 0:m_], in1=ss, op=ALU.mult), r=[t["xi"], rs], w=[t["t2"]])
            k.op("pool", lambda e: e.tensor_tensor(out=t["t3"][:, 0:m_], in0=t["xi"][:, 0:m_], in1=cc, op=ALU.mult), r=[t["xi"], rc], w=[t["t3"]])
            k.op("pool", lambda e: e.tensor_tensor(out=t["t4"][:, 0:m_], in0=t["xr"][:, 0:m_], in1=ss, op=ALU.mult), r=[t["xr"], rs], w=[t["t4"]])
            k.op("pool", lambda e: e.tensor_tensor(out=t["zr"][:, 0:m_], in0=t["t1"][:, 0:m_], in1=t["t2"][:, 0:m_], op=ALU.add), r=[t["t1"], t["t2"]], w=[t["zr"]])
            k.op("pool", lambda e: e.tensor_tensor(out=t["zi"][:, 0:m_], in0=t["t3"][:, 0:m_], in1=t["t4"][:, 0:m_], op=ALU.subtract), r=[t["t3"], t["t4"]], w=[t["zi"]])
            yield
            rb_ = o["rmag"][:, i:i + 1].to_broadcast([128, m_])
            k.op("dve", lambda e: e.tensor_tensor_scan(out=t["wr"][:, 0:m_], data0=rb_, data1=t["zr"][:, 0:m_], initial=w0r[:, i:i + 1], op0=ALU.mult, op1=ALU.add),
                 r=[o["rmag"], t["zr"], w0r], w=[t["wr"]])
            k.op("dve", lambda e: e.tensor_tensor_scan(out=t["wi"][:, 0:m_], data0=rb_, data1=t["zi"][:, 0:m_], initial=w0i[:, i:i + 1], op0=ALU.mult, op1=ALU.add),
                 r=[o["rmag"], t["zi"], w0i], w=[t["wi"]])
            yield
            k.op("pool", lambda e: e.tensor_tensor(out=t["t1"][:, 0:m_], in0=t["wr"][:, 0:m_], in1=cc, op=ALU.mult), r=[t["wr"], rc], w=[t["t1"]])
            k.op("pool", lambda e: e.tensor_tensor(out=t["t2"][:, 0:m_], in0=t["wi"][:, 0:m_], in1=ss, op=ALU.mult), r=[t["wi"], rs], w=[t["t2"]])
            k.op("pool", lambda e: e.tensor_tensor(out=t["t3"][:, 0:m_], in0=t["wi"][:, 0:m_], in1=cc, op=ALU.mult), r=[t["wi"], rc], w=[t["t3"]])
            k.op("pool", lambda e: e.tensor_tensor(out=t["t4"][:, 0:m_], in0=t["wr"][:, 0:m_], in1=ss, op=ALU.mult), r=[t["wr"], rs], w=[t["t4"]])
            k.op("pool", lambda e: e.tensor_tensor(out=t["hr"][:, 0:m_], in0=t["t1"][:, 0:m_], in1=t["t2"][:, 0:m_], op=ALU.subtract), r=[t["t1"], t["t2"]], w=[t["hr"]])
            k.op("pool", lambda e: e.tensor_tensor(out=t["hi"][:, 0:m_], in0=t["t3"][:, 0:m_], in1=t["t4"][:, 0:m_], op=ALU.add), r=[t["t3"], t["t4"]], w=[t["hi"]])
            yield
            k.op("pool", lambda e: e.tensor_copy(out=tb["hr"][:, 0:m_], in_=t["hr"][:, 0:m_]), r=[t["hr"]], w=[tb["hr"]])
            k.op("pool", lambda e: e.tensor_copy(out=tb["hi"][:, 0:m_], in_=t["hi"][:, 0:m_]), r=[t["hi"]], w=[tb["hi"]])
            k.op("pool", lambda e: e.tensor_copy(out=hr_[:, i:i + 1], in_=t["hr"][:, m_ - 1:m_]), r=[t["hr"]], w=[hr_])
            k.op("pool", lambda e: e.tensor_copy(out=hi_[:, i:i + 1], in_=t["hi"][:, m_ - 1:m_]), r=[t["hi"]], w=[hi_])
            yield
            py = pY[hf]
            co = 64 * hf
            if j < 2:
                k.op("pe", lambda e: e.matmul(py[32 * j:32 * j + 32, co:co + m_], lhsT=o["CTr"][:, i, :], rhs=tb["hr"][:, 0:m_], start=True, stop=False), r=[o["CTr"], tb["hr"]], w=[py])
                k.op("pe", lambda e: e.matmul(py[32 * j:32 * j + 32, co:co + m_], lhsT=o["CTi"][:, i, :], rhs=tb["hi"][:, 0:m_], start=False, stop=True), r=[o["CTi"], tb["hi"]], w=[py])
            else:
                k.op("pe", lambda e: e.matmul(py[64:128, co:co + m_], lhsT=o["CTpr"][:, hf, j - 2, :], rhs=tb["hr"][:, 0:m_], start=(j == 2), stop=False), r=[o["CTpr"], tb["hr"]], w=[py])
                k.op("pe", lambda e: e.matmul(py[64:128, co:co + m_], lhsT=o["CTpi"][:, hf, j - 2, :], rhs=tb["hi"][:, 0:m_], start=False, stop=(j == 3)), r=[o["CTpi"], tb["hi"]], w=[py])
            yield
        for grp in ((0, 1, 2), (3, 4, 5), (6, 7)):
            tg = [tile_gen(i, S5sets[gi], S5bsets[gi]) for gi, i in enumerate(grp)]
            while tg:
                for g_ in list(tg):
                    try:
                        next(g_)
                    except StopIteration:
                        tg.remove(g_)
                yield
        for hf in range(2):
            y, t_ = ysb[hf], yt[hf]
            k.op("dve", lambda e: e.scalar_tensor_tensor(out=y[:, 0:m_], in0=uT[:, hf, c0:c0 + m_], scalar=o["dcol"][:, hf:hf + 1], in1=pY[hf][:, 64 * hf:64 * hf + m_],
                                                         op0=ALU.mult, op1=ALU.add), r=[uTb, o["dcol"], pY[hf]], w=[y])
            k.op("pool", lambda e: e.tensor_tensor(out=t_[:, 0:m_], in0=y[:, 0:m_], in1=y[:, 0:m_], op=ALU.mult), r=[y], w=[t_])
            k.op("dve", lambda e: e.tensor_scalar(out=t_[:, 0:m_], in0=t_[:, 0:m_], scalar1=0.044715, scalar2=1.0, op0=ALU.mult, op1=ALU.add), w=[t_])
            k.op("pool", lambda e: e.tensor_tensor(out=t_[:, 0:m_], in0=t_[:, 0:m_], in1=y[:, 0:m_], op=ALU.mult), r=[y], w=[t_])
            k.op("act", lambda e: e.activation(out=t_[:, 0:m_], in_=t_[:, 0:m_], func=AF.Exp, scale=-1.5957691216057308), w=[t_])
            k.op("dve", lambda e: e.tensor_scalar_add(out=t_[:, 0:m_], in0=t_[:, 0:m_], scalar1=1.0), w=[t_])
            k.op("dve", lambda e: e.reciprocal(out=t_[:, 0:m_], in_=t_[:, 0:m_]), w=[t_])
            k.op("dve", lambda e: e.tensor_tensor(out=zT[:, hf, c0:c0 + m_], in0=y[:, 0:m_], in1=t_[:, 0:m_], op=ALU.mult), r=[y, t_], w=[zT])

    def block_layer(l, blk, tiles, segs, NT, is_sample, emit):
        o = L[l]
        k.fence([hTb] + wstb, mixer_bufs)
        k.fence(s5_views, xTb)
        k.fence(dn2_views, [kvst[0], kvst[1]])
        k.fence(s5c_views, [ctmp[0], ctm2[0]])
        transposes_to_xT(tiles)
        xr_ = list(xTb)
        chk(2)
        WIN = [(0, 512), (512, 512), (1024, 512), (1536, 512), (2048, 520), (2568, 256)]

        def type_a(slab, et, evac):
            p = k.psn()
            for kt in range(8):
                k.op("pe", lambda e, kt=kt: e.matmul(p[:, 0:NT], lhsT=slab[:, kt, et * 128:(et + 1) * 128], rhs=xT[:, kt, 0:NT],
                                                     start=(kt == 0), stop=(kt == 7)), r=[slab] + xr_, w=[p], sig=(kt == 7))
            evac(p)

        def type_b(slab, cb, W, ti, evac):
            c0, n = tiles[ti]
            p = k.psn()
            for kt in range(8):
                k.op("pe", lambda e, kt=kt: e.matmul(p[0:n, 0:W], lhsT=xT[:, kt, c0:c0 + n], rhs=slab[:, kt, cb:cb + W],
                                                     start=(kt == 0), stop=(kt == 7)), r=[slab, xTb[ti]], w=[p], sig=(kt == 7))
            evac(p)

        kbase = 0 if is_sample else ((4 * blk) % 8) * 128
        sA = load_slab(win_s.h[l, :, :, 0:512], 512, win_s, si)
        for et in range(4):
            type_a(sA, et, lambda p, et=et: copy(evac_eng(), qT[:, et, 0:NT], p[:, 0:NT], [p], [qT]))
        chk(2.1)
        sB = load_slab(win_s.h[l, :, :, 512:1024], 512, win_s, si)
        for et in range(4):
            if is_sample:
                type_a(sB, et, lambda p, et=et: copy(evac_eng(), o["kT"][:, et, 640:640 + NT], p[:, 0:NT], [p], [o["kT"]]))
            else:
                type_a(sB, et, lambda p, et=et: copy(evac_eng(), o["kT"][:, et, kbase:kbase + NT], p[:, 0:NT], [p], [o["kT"]]))
        if emit:
            for ti, (c0, n) in enumerate(tiles):
                def ev(p, ti=ti, c0=c0, n=n):
                    st_ = kvst[ti % 2]
                    copy(evac_eng(), st_[0:n, :], p[0:n, 0:512], [p], [st_])
                    dst = (o_ks.ap()[l, c0:c0 + n, :] if is_sample else o_kp.ap()[l, c0:c0 + n, :])
                    k.dma("sp", dst, st_[0:n, :], r=[st_])
                type_b(sB, 0, 512, ti, ev)
        chk(2.2)
        sC = load_slab(win_s.h[l, :, :, 1024:1536], 512, win_s, si)
        for ti, (c0, n) in enumerate(tiles):
            def ev(p, ti=ti, c0=c0, n=n):
                slot = (4 + ti) if is_sample else (4 * blk + ti) % 8
                vdst = o["vr"][0:n, slot, :].rearrange("p (h c) -> p h c", c=66)[:, :, 0:64]
                if DBG[0] != 1:
                    copy("act", vdst, p[0:n, 0:512].rearrange("p (h c) -> p h c", c=64), [p], [o["vr"]])
                if emit and DBG[0] != 2:
                    st_ = kvst[ti % 2]
                    copy("dve", st_[0:n, :], p[0:n, 0:512], [p], [st_])
                    dst = (o_vs.ap()[l, c0:c0 + n, :] if is_sample else o_vp.ap()[l, c0:c0 + n, :])
                    k.dma("sp", dst, st_[0:n, :], r=[st_])
            type_b(sC, 0, 512, ti, ev)
        chk(2.3)
        sD = load_slab(win_s.h[l, :, :, 1536:2048], 512, win_s, si)
        slabE = [None]

        def conv_tile(ct, p):
            cb_ = cbuf[ct % 2]; t1_, t2_ = ctmp[ct % 2], ctm2[ct % 2]
            off = 0
            for (c0, n, slot) in segs:
                k.op("pool", lambda e, off=off, slot=slot: e.tensor_copy(out=cb_[:, off:off + 3], in_=o["convh"][slot][:, ct, :]), r=[o["convh"][slot]], w=[cb_])
                copy(evac_eng(), cb_[:, off + 3:off + 3 + n], p[:, c0:c0 + n], [p], [cb_])
                off += n + 3
            off = 0
            for (c0, n, slot) in segs:
                k.op("dve", lambda e, off=off, c0=c0, n=n: e.tensor_scalar_mul(out=t1_[:, c0:c0 + n], in0=cb_[:, off:off + n], scalar1=o["convw"][:, ct, 0:1]), r=[cb_, o["convw"]], w=[t1_])
                for jj in range(1, 4):
                    k.op("dve", lambda e, off=off, c0=c0, n=n, jj=jj: e.scalar_tensor_tensor(
                        out=t1_[:, c0:c0 + n], in0=cb_[:, off + jj:off + jj + n], scalar=o["convw"][:, ct, jj:jj + 1], in1=t1_[:, c0:c0 + n],
                        op0=ALU.mult, op1=ALU.add), r=[cb_, o["convw"]], w=[t1_])
                k.op("pool", lambda e, off=off, n=n, slot=slot: e.tensor_copy(out=o["convh"][slot][:, ct, :], in_=cb_[:, off + n:off + n + 3]), r=[cb_], w=[o["convh"][slot]])
                off += n + 3
            if ct < 4:
                k.op("act", lambda e: e.activation(out=t1_[:, 0:NT], in_=t1_[:, 0:NT], func=AF.Silu, bias=o["convb"][:, ct:ct + 1], scale=1.0), r=[o["convb"]], w=[t1_])
                k.op("pool", lambda e: e.tensor_tensor(out=r32(sqr[:, 0:NT]), in0=t1_[:, 0:NT], in1=t1_[:, 0:NT], op=ALU.mult), r=[t1_], w=[sqr])
                pn = k.psn()
                k.op("pe", lambda e: e.matmul(pn[:, 0:NT], lhsT=r32(bones[:]), rhs=r32(sqr[:, 0:NT]), start=True, stop=True), r=[bones, sqr], w=[pn])
                k.op("act", lambda e: e.activation(out=t2_[:, 0:NT], in_=pn[:, 0:NT], func=AF.Sqrt, bias=RMS_EPS, scale=1.0), r=[pn], w=[t2_])
                k.op("dve", lambda e: e.reciprocal(out=t2_[:, 0:NT], in_=t2_[:, 0:NT]), w=[t2_])
                k.op("dve", lambda e: e.tensor_tensor(out=r32(qkvT[:, ct, 0:NT]), in0=t1_[:, 0:NT], in1=t2_[:, 0:NT], op=ALU.mult), r=[t1_, t2_], w=[qkvb[ct]])
            else:
                k.op("act", lambda e: e.activation(out=r32(qkvT[:, ct, 0:NT]), in_=t1_[:, 0:NT], func=AF.Silu, bias=o["convb"][:, ct:ct + 1], scale=1.0),
                     r=[t1_, o["convb"]], w=[qkvb[ct]])
            if ct >= 2:
                dstt, dstb = (ktok, ktb) if ct < 4 else (vtok, vtb)
                for ti, (c0, n) in enumerate(tiles):
                    pt = k.psn()
                    k.op("pe", lambda e, c0=c0, n=n: e.transpose(pt[0:n, 0:128], qkvT[:, ct, c0:c0 + n], ident[:, :]), r=[qkvb[ct], ident], w=[pt])
                    copy(evac_eng(), r32(dstt[0:n, ti, (ct % 2) * 128:(ct % 2) * 128 + 128]), pt[0:n, 0:128], [pt], [dstb])

        for et in range(4):
            type_a(sD, et, lambda p, et=et: conv_tile(et, p))
        chk(2.4)
        sE = load_slab(win_s.h[l, :, :, 2048:2568], 520, win_s, si)
        for et in range(2):
            type_a(sE, et, lambda p, et=et: conv_tile(4 + et, p))
        for ti, (c0, n) in enumerate(tiles):
            type_b(sE, 256, 264, ti, lambda p, ti=ti, n=n: copy(evac_eng(), gtok[0:n, ti, :], p[0:n, 0:264], [p], [gtb[ti]]))
        chk(2.5)
        sF = load_slab(win_s.h[l, :, :, 2568:2824], 256, win_s, si)
        for et in range(2):
            type_a(sF, et, lambda p, et=et: copy(evac_eng(), r32(uT[:, et, 0:NT]), p[:, 0:NT], [p], [uTb]))

        chk(3)
        k.fence(xTb, s5_views)
        k.fence([ctmp[0], ctm2[0]], s5c_views)
        k.fence([kvst[0], kvst[1]], dn2_views)

        def gen_att():
            if not is_sample:
                for ti, (c0, n) in enumerate(tiles):
                    G = 4 * blk + ti
                    kts = []
                    for j in range(5):
                        gt_ = G - 4 + j
                        if gt_ >= 0:
                            kts.append((j, (gt_ % 8) * 128, 128, gt_ % 8))
                    yield from attention(l, c0, n, kts, ti)
            else:
                for s in range(NS):
                    c0, n = tiles[s]
                    for q2 in range(2):
                        k.dma("sp", lnpA[q2].rearrange("p (a c) -> p a c", c=512), dap(ck, (l * NS + s) * 262144 + q2 * 131072, [[512, 128], [65536, 2], [1, 512]]), w=[lnp[q2]])
                    for kt in range(4):
                        p = k.psn()
                        for hp in range(4):
                            k.op("pe", lambda e, kt=kt, hp=hp: e.transpose(p[:, hp * 128:(hp + 1) * 128], lnpA[kt // 2][:, (kt % 2) * 512 + hp * 128:(kt % 2) * 512 + (hp + 1) * 128], ident[:]),
                                 r=[lnp[kt // 2], ident], w=[p])
                        copy(evac_eng(), o["kT"][:, :, kt * 128:(kt + 1) * 128], p[:, :].rearrange("p (a c) -> p a c", c=128), [p], [o["kT"]])
                    for q2 in range(2):
                        k.dma("sp", lnpA[q2].rearrange("p (a c) -> p a c", c=512), dap(cv, (l * NS + s) * 262144 + q2 * 131072, [[512, 128], [65536, 2], [1, 512]]), w=[lnp[q2]])
                    for kt in range(4):
                        copy(evac_eng(), o["vr"][:, kt, :].rearrange("p (h c) -> p h c", c=66)[:, :, 0:64],
                             lnpA[kt // 2][:, (kt % 2) * 512:(kt % 2) * 512 + 512].rearrange("p (h c) -> p h c", c=64), [lnp[kt // 2]], [o["vr"]])
                    kts = [(j, j * 128, 128, j) for j in range(4)] + [(4, 640 + c0, n, 4 + s)]
                    yield from attention(l, c0, n, kts, s)


        def gen_dn():
            for ti, (c0, n) in enumerate(tiles):
                yield from deltanet_tile(l, ti, c0, n, segs[ti][2] if is_sample else 0)

        def gen_s5():
            for (c0, n, slot) in segs:
                for s0 in range(0, n, 64):
                    yield from s5_sub(l, c0 + s0, min(64, n - s0), slot, s0 == 0)
            k.fence(s5c_views, [ctmp[0], ctm2[0]])
            for hf2 in range(2):
                p = k.psn()
                for hf in range(2):
                    k.op("pe", lambda e, hf=hf: e.matmul(p[:, 0:NT], lhsT=o["gluw"][:, hf, hf2 * 128:(hf2 + 1) * 128], rhs=zT[:, hf, 0:NT], start=(hf == 0), stop=(hf == 1)),
                         r=[o["gluw"], zT], w=[p])
                f_ = fup[hf2]
                k.op("act", lambda e: e.activation(out=f_[:, 0:NT], in_=p[:, 0:NT], func=AF.Sigmoid, bias=o["glub"][:, hf2:hf2 + 1], scale=1.0), r=[p, o["glub"]], w=[f_])
                k.op("dve", lambda e: e.tensor_tensor(out=mixT[:, 6 + hf2, 0:NT], in0=zT[:, hf2, 0:NT], in1=f_[:, 0:NT], op=ALU.mult), r=[zT, f_], w=[mixb[6 + hf2]])


        gens = [(gen_dn(), 2), (gen_s5(), 4), (gen_att(), 1)]
        if STOP[0] <= 5 or SERIAL[0]:
            for g_, _w in (gens[2], gens[0], gens[1]):
                for _ in g_:
                    pass
                chk(4 if g_ is gens[2][0] else (5 if g_ is gens[0][0] else 6))
        else:
            active = list(gens)
            while active:
                for item in list(active):
                    g_, w_ = item
                    for _ in range(w_):
                        try:
                            next(g_)
                        except StopIteration:
                            active.remove(item)
                            break
        chk(6)
        for half in range(2):
            sO = load_slab(wout_s.h[l, :, :, half * 512:(half + 1) * 512], 512, wout_s, si)
            for ti, (c0, n) in enumerate(tiles):
                p = k.psn()
                for kt in range(8):
                    k.op("pe", lambda e, kt=kt: e.matmul(p[0:n, 0:512], lhsT=mixT[:, kt, c0:c0 + n], rhs=sO[:, kt, 0:512], start=(kt == 0), stop=(kt == 7)),
                         r=[sO] + mixb, w=[p], sig=(kt == 7))
                k.op("dve", lambda e, ti=ti, n=n: e.scalar_tensor_tensor(out=xtok[ti][0:n, half * 512:(half + 1) * 512], in0=xtok[ti][0:n, half * 512:(half + 1) * 512],
                                                                       scalar=ALPHA, in1=p[0:n, 0:512], op0=ALU.mult, op1=ALU.add), r=[p], w=[xtok[ti]])
        for ti, (c0, n) in enumerate(tiles):
            layer_norm(ti, n, ln1g, ln1b, l)
        k.fence(s5_views, xTb)
        transposes_to_xT(tiles)
        chk(7)
        k.fence(mixer_bufs, [hTb])
        for fs in range(8):
            sU = load_slab(wup_s.h[l, :, :, fs * 512:(fs + 1) * 512], 512, wup_s, si)
            for q in range(4):
                ft = fs * 4 + q
                p = k.psn()
                for kt in range(8):
                    k.op("pe", lambda e, kt=kt: e.matmul(p[:, 0:NT], lhsT=sU[:, kt, q * 128:(q + 1) * 128], rhs=xT[:, kt, 0:NT], start=(kt == 0), stop=(kt == 7)),
                         r=[sU] + list(xTb), w=[p], sig=(kt == 7))
                f_ = fup[ft % 2]
                k.op("act", lambda e, ft=ft: e.activation(out=f_[:, 0:NT], in_=p[:, 0:NT], func=AF.Relu, bias=o["bup"][:, ft:ft + 1], scale=1.0), r=[p, o["bup"]], w=[f_])
                k.op("pool" if ft % 2 else "dve", lambda e, ft=ft: e.tensor_tensor(out=hT[:, ft, 0:NT], in0=f_[:, 0:NT], in1=f_[:, 0:NT], op=ALU.mult), r=[f_], w=[hTb])
        for half in range(2):
            accs = [k.psn() for _ in tiles]
            for fs in range(4):
                sDn = load_slab(wdn_s.h[l, :, fs * 8:(fs + 1) * 8, half * 512:(half + 1) * 512], 512, wdn_s, si)
                for ti, (c0, n) in enumerate(tiles):
                    for f8 in range(8):
                        k.op("pe", lambda e, f8=f8, ti=ti, c0=c0, n=n: e.matmul(accs[ti][0:n, 0:512], lhsT=hT[:, fs * 8 + f8, c0:c0 + n], rhs=sDn[:, f8, 0:512],
                                                                              start=(fs == 0 and f8 == 0), stop=(fs == 3 and f8 == 7)), r=[sDn, hTb], w=[accs[ti]], sig=(f8 == 7))
            for ti, (c0, n) in enumerate(tiles):
                k.op("dve", lambda e, ti=ti, n=n: e.scalar_tensor_tensor(out=xtok[ti][0:n, half * 512:(half + 1) * 512], in0=xtok[ti][0:n, half * 512:(half + 1) * 512],
                                                                       scalar=ALPHA, in1=accs[ti][0:n, 0:512], op0=ALU.mult, op1=ALU.add), r=[accs[ti]], w=[xtok[ti]])
        for ti, (c0, n) in enumerate(tiles):
            layer_norm(ti, n, ln2g, ln2b, l)

    def emit_states(l, is_sample):
        o = L[l]
        slots = range(1, NSLOT) if is_sample else [0]
        for s in slots:
            sq = s - 1
            if is_sample:
                d_conv = (o_convs, (l * NS + sq) * 2304)
                d_ssm = dap(o_ssms, (l * NS + sq) * 16384, [[64, 64], [4096, 4], [1, 64]])
                d_re = dap(o_cres, (l * NS + sq) * 1024, [[1, 128], [128, 8]])
                d_im = dap(o_cims, (l * NS + sq) * 1024, [[1, 128], [128, 8]])
            else:
                d_conv = (o_convp, l * 2304)
                d_ssm = dap(o_ssmp, l * 16384, [[64, 64], [4096, 4], [1, 64]])
                d_re = dap(o_crep, l * 1024, [[1, 128], [128, 8]])
                d_im = dap(o_cimp, l * 1024, [[1, 128], [128, 8]])
            for ct in range(6):
                k.dma("sp", dap(d_conv[0], d_conv[1] + ct * 128, [[1, 128], [768, 3]]), o["convh"][s][:, ct, :], r=[o["convh"][s]], allow_slow_non_contiguous=True)
            k.dma("sp", d_ssm, o["S"][s][0:64, :, :], r=[o["S"][s]] + o["Sb"][s])
            k.dma("sp", d_re, o["hr"][s][:], r=[o["hr"][s]], allow_slow_non_contiguous=True)
            k.dma("sp", d_im, o["hi"][s][:], r=[o["hi"][s]], allow_slow_non_contiguous=True)

    ptiles = [(i * 128, 128) for i in range(4)]

    def main_loop():
        for blk in range(NB):
            for ti in range(4):
                k.dma("sp", xtok[ti][:, :], xp.ap()[blk * 512 + ti * 128:blk * 512 + (ti + 1) * 128, :], w=[xtok[ti]])
            for l in range(DEPTH):
                block_layer(l, blk, ptiles, [(0, 512, 0)], 512, False, blk == NB - 1)
                if blk == NB - 1:
                    emit_states(l, False)
            for ti in range(4):
                k.dma("pool", yp.ap()[blk * 512 + ti * 128:blk * 512 + (ti + 1) * 128, :], xtok[ti][:, :], r=[xtok[ti]])
        chk(20)
        stiles = [(s * SL, SL) for s in range(NS)]
        for s in range(NS):
            k.dma("sp", xtok[s][0:SL, :], xs.ap()[s * SL:(s + 1) * SL, :], w=[xtok[s]])
        for l in range(DEPTH):
            block_layer(l, 0, stiles, [(s * SL, SL, 1 + s) for s in range(NS)], NSK, True, True)
            emit_states(l, True)
        for s in range(NS):
            k.dma("pool", ys.ap()[s * SL:(s + 1) * SL, :], xtok[s][0:SL, :], r=[xtok[s]])

    try:
        main_loop()
    except StopBuild:
        pass
    k.finish()
    return nc


_NC_CACHE = {}


def run(inputs, SEQ, DEPTH, NCORES=8):
    key = (SEQ, DEPTH)
    if key not in _NC_CACHE:
        _NC_CACHE[key] = build(SEQ, DEPTH)
    nc = _NC_CACHE[key]
    f = lambda a: np.ascontiguousarray(np.asarray(a, dtype=np.float32))
    I = {n: f(v) for n, v in inputs.items()}
    m = np.arange(768)
    idx = np.clip(639 - m, -256, 256) + 256
    fext = np.ascontiguousarray(I["a_rel_bias"][:, :, idx])
    shared = {
        "w_in": I["w_in"], "fext": fext, "arb": I["a_rel_bias"], "conv_w": I["b_conv_w"], "conv_b": I["b_conv_b"],
        "a_log": I["b_a_log"], "dt_bias": I["b_dt_bias"], "norm_w": I["b_norm_w"],
        "ca_re": I["c_a_re"].reshape(DEPTH, 1024), "ca_im": I["c_a_im"].reshape(DEPTH, 1024), "clogdt": I["c_log_dt"],
        "cb_re": I["c_b_re"].reshape(DEPTH, 1024, 16), "cb_im": I["c_b_im"].reshape(DEPTH, 1024, 16),
        "cc_re": I["c_c_re"].reshape(DEPTH, 256, 64), "cc_im": I["c_c_im"].reshape(DEPTH, 256, 64),
        "c_d": I["c_d"].reshape(DEPTH, 256), "glu_w": I["c_glu_w"], "glu_b": I["c_glu_b"],
        "w_out": I["w_out"], "ln1g": I["ln1_g"], "ln1b": I["ln1_b"], "w_up": I["w_up"], "b_up": I["b_up"],
        "w_dn": I["w_down"], "ln2g": I["ln2_g"], "ln2b": I["ln2_b"],
    }
    in_maps = []
    for c in range(NCORES):
        sl = slice(4 * c, 4 * c + 4)
        mp = dict(shared)
        mp["xp"] = I["x_prompt"][c]
        mp["xs"] = np.ascontiguousarray(I["x_sample"][sl].reshape(64, D))
        mp["ck"] = np.ascontiguousarray(I["cache_a_k"][:, sl].reshape(DEPTH, 4, 512, 512))
        mp["cv"] = np.ascontiguousarray(I["cache_a_v"][:, sl].reshape(DEPTH, 4, 512, 512))
        mp["sconv"] = np.ascontiguousarray(I["state_b_conv"][:, sl])
        mp["sssm"] = np.ascontiguousarray(I["state_b_ssm"][:, sl])
        mp["scre"] = np.ascontiguousarray(I["state_c_re"][:, sl].reshape(DEPTH, 4, 1024))
        mp["scim"] = np.ascontiguousarray(I["state_c_im"][:, sl].reshape(DEPTH, 4, 1024))
        in_maps.append(mp)
    res = run_bass_kernel_spmd(nc, in_maps, core_ids=list(range(NCORES)))
    R = res.results
    B = NCORES
    st = lambda n: np.stack([R[c][n] for c in range(B)])
    y_p = st("yp")
    y_s = st("ys").reshape(B * 4, 16, D)
    kp = np.stack([R[c]["o_kp"] for c in range(B)], axis=1).reshape(DEPTH, B, 512, 8, 64)
    vp = np.stack([R[c]["o_vp"] for c in range(B)], axis=1).reshape(DEPTH, B, 512, 8, 64)
    convp = np.stack([R[c]["o_convp"] for c in range(B)], axis=1)
    ssmp = np.stack([R[c]["o_ssmp"] for c in range(B)], axis=1)
    crep = np.stack([R[c]["o_crep"] for c in range(B)], axis=1).reshape(DEPTH, B, 16, 64)
    cimp = np.stack([R[c]["o_cimp"] for c in range(B)], axis=1).reshape(DEPTH, B, 16, 64)
    ks = np.concatenate([R[c]["o_ks"].reshape(DEPTH, 4, 16, 8, 64) for c in range(B)], axis=1)
    vs = np.concatenate([R[c]["o_vs"].reshape(DEPTH, 4, 16, 8, 64) for c in range(B)], axis=1)
    convs = np.concatenate([R[c]["o_convs"] for c in range(B)], axis=1)
    ssms = np.concatenate([R[c]["o_ssms"] for c in range(B)], axis=1)
    cres = np.concatenate([R[c]["o_cres"].reshape(DEPTH, 4, 16, 64) for c in range(B)], axis=1)
    cims = np.concatenate([R[c]["o_cims"].reshape(DEPTH, 4, 16, 64) for c in range(B)], axis=1)
    outs = (y_p, y_s, kp, vp, convp, ssmp, crep, cimp, ks, vs, convs, ssms, cres, cims)
    return tuple(np.ascontiguousarray(a.astype(np.float32)) for a in outs)


def kernel(**inputs):
    return run(inputs, 4096, 2)
```
